# Optimizing a Trainium2 kernel written in Bass

```python
import jax, jax.numpy as jnp
from jax import lax
import numpy as np

D_MODEL = 1024
BATCH = 8
SEQ = 2048
DEPTH = 4

GRID_W = 64
CTX_LEN = 256
HEAD_DIM = 64
ROPE_THETA = 10000.0
EPS = 1e-6
Q_BLOCK = 128
NEG_INF = -1e30
ATTN_SCALE = HEAD_DIM ** -0.5

A_HEADS = 8
A_KV_HEADS = 2
B_HEADS = 8
B_KV_HEADS = 2
B_WINDOW = 128
C_WIDTH = 512
C_CONV = 31
D_HEADS = 8
NA_KH = 8
NA_KW = 16

A_Q = A_HEADS * HEAD_DIM
A_KV = A_KV_HEADS * HEAD_DIM
B_Q = B_HEADS * HEAD_DIM
B_KV = B_KV_HEADS * HEAD_DIM
D_W = D_HEADS * HEAD_DIM
EVEN_WIDTHS = (A_Q, A_KV, A_KV, A_Q, B_Q, B_KV, B_KV, B_Q)
ODD_WIDTHS = (C_WIDTH, C_WIDTH, C_WIDTH, D_W, D_W, D_W, D_W)
EVEN_IN = sum(EVEN_WIDTHS)
ODD_IN = sum(ODD_WIDTHS)
EVEN_MIX = A_Q + B_Q
ODD_MIX = C_WIDTH + D_W

kernel_name = "hybrid_dit_ctx_prefix_attn_conv_natten"


def rms_norm(x, g):
    xf = x.astype(jnp.float32)
    y = xf * lax.rsqrt(jnp.mean(xf * xf, axis=-1, keepdims=True) + EPS)
    return (y * g.astype(jnp.float32)).astype(x.dtype)


def layer_norm(x, g, b):
    xf = x.astype(jnp.float32)
    mu = jnp.mean(xf, axis=-1, keepdims=True)
    var = jnp.mean(jnp.square(xf - mu), axis=-1, keepdims=True)
    y = (xf - mu) * lax.rsqrt(var + EPS)
    return (y * g.astype(jnp.float32) + b.astype(jnp.float32)).astype(x.dtype)


def split_cols(h, widths):
    offs = [int(o) for o in np.cumsum(widths)[:-1]]
    return jnp.split(h, offs, axis=-1)


def heads(t, n_heads):
    return t.reshape(t.shape[:-1] + (n_heads, HEAD_DIM))


def group(q, kv_heads):
    return q.reshape(q.shape[:-2] + (kv_heads, q.shape[-2] // kv_heads, HEAD_DIM))


def axial_rope_tables(n):
    t = jnp.arange(n)
    row = (t // GRID_W).astype(jnp.float32)
    col = (t % GRID_W).astype(jnp.float32)
    half = HEAD_DIM // 2
    freqs = ROPE_THETA ** (-jnp.arange(0, half, 2, dtype=jnp.float32) / half)
    ang = jnp.concatenate([row[:, None] * freqs, col[:, None] * freqs], axis=-1)
    return jnp.cos(ang), jnp.sin(ang)


def apply_rope(x, cos, sin):
    xf = x.astype(jnp.float32)
    x1, x2 = xf[..., 0::2], xf[..., 1::2]
    c, s = cos[None, :, None, :], sin[None, :, None, :]
    out = jnp.stack([x1 * c - x2 * s, x1 * s + x2 * c], axis=-1).reshape(x.shape)
    return out.astype(x.dtype)


def gqa_attend(q, k, v, bias=None, sink=None):
    s = jnp.einsum("bqhgd,bkhd->bhgqk", q, k, preferred_element_type=jnp.float32) * ATTN_SCALE
    if bias is not None:
        s = s + bias
    if sink is not None:
        sink_col = jnp.broadcast_to(sink.astype(jnp.float32)[None, :, :, None, None], s.shape[:-1] + (1,))
        p = jax.nn.softmax(jnp.concatenate([s, sink_col], axis=-1), axis=-1)[..., :-1]
    else:
        p = jax.nn.softmax(s, axis=-1)
    return jnp.einsum("bhgqk,bkhd->bqhgd", p.astype(v.dtype), v)


def even_mixer(h_lat, h_ctx, w_in, w_out, q_gain, k_gain, sink, cos, sin, update_ctx):
    bsz, n, _ = h_lat.shape
    ctx_len = h_ctx.shape[1]
    n_blk = n // Q_BLOCK
    ga, gb = A_HEADS // A_KV_HEADS, B_HEADS // B_KV_HEADS
    sink = sink.reshape(B_KV_HEADS, gb)

    def project(h):
        aq, ak, av, ag, bq, bk, bv, bg = split_cols(h @ w_in, EVEN_WIDTHS)
        aq = rms_norm(heads(aq, A_HEADS), q_gain)
        ak = rms_norm(heads(ak, A_KV_HEADS), k_gain)
        return (aq, ak, heads(av, A_KV_HEADS), ag,
                heads(bq, B_HEADS), heads(bk, B_KV_HEADS), heads(bv, B_KV_HEADS), bg)

    aq, ak, av, ag, bq, bk, bv, bg = project(h_lat)
    c_aq, c_ak, c_av, c_ag, c_bq, c_bk, c_bv, c_bg = project(h_ctx)
    aq, ak, bq, bk = (apply_rope(t, cos, sin) for t in (aq, ak, bq, bk))

    ka = jnp.concatenate([c_ak, ak], axis=1)
    va = jnp.concatenate([c_av, av], axis=1)
    qa_blk = jnp.moveaxis(group(aq, A_KV_HEADS).reshape(bsz, n_blk, Q_BLOCK, A_KV_HEADS, ga, HEAD_DIM), 1, 0)
    oa = lax.map(lambda qb: gqa_attend(qb, ka, va), qa_blk)
    oa = jnp.moveaxis(oa, 0, 1).reshape(bsz, n, A_Q)

    band = Q_BLOCK + 2 * B_WINDOW
    kb_pad = jnp.pad(bk, ((0, 0), (B_WINDOW, B_WINDOW), (0, 0), (0, 0)))
    vb_pad = jnp.pad(bv, ((0, 0), (B_WINDOW, B_WINDOW), (0, 0), (0, 0)))
    rel = jnp.arange(band)[None, :] - B_WINDOW - jnp.arange(Q_BLOCK)[:, None]
    in_band = jnp.abs(rel) <= B_WINDOW
    ctx_open = jnp.zeros((Q_BLOCK, ctx_len), jnp.float32)
    qb_blk = jnp.moveaxis(group(bq, B_KV_HEADS).reshape(bsz, n_blk, Q_BLOCK, B_KV_HEADS, gb, HEAD_DIM), 1, 0)

    def b_block(args):
        blk, qb = args
        start = blk * Q_BLOCK
        kpos = start - B_WINDOW + jnp.arange(band)
        valid = in_band & ((kpos >= 0) & (kpos < n))[None, :]
        bias = jnp.concatenate([ctx_open, jnp.where(valid, 0.0, NEG_INF)], axis=-1)
        kk = jnp.concatenate([c_bk, lax.dynamic_slice_in_dim(kb_pad, start, band, axis=1)], axis=1)
        vv = jnp.concatenate([c_bv, lax.dynamic_slice_in_dim(vb_pad, start, band, axis=1)], axis=1)
        return gqa_attend(qb, kk, vv, bias, sink)

    ob = lax.map(b_block, (jnp.arange(n_blk), qb_blk))
    ob = jnp.moveaxis(ob, 0, 1).reshape(bsz, n, B_Q)

    y_lat = jnp.concatenate([oa * jax.nn.silu(ag), ob * jax.nn.silu(bg)], axis=-1) @ w_out
    if not update_ctx:
        return y_lat, None
    oa_c = gqa_attend(group(c_aq, A_KV_HEADS), c_ak, c_av).reshape(bsz, ctx_len, A_Q)
    ob_c = gqa_attend(group(c_bq, B_KV_HEADS), c_bk, c_bv, sink=sink).reshape(bsz, ctx_len, B_Q)
    y_ctx = jnp.concatenate([oa_c * jax.nn.silu(c_ag), ob_c * jax.nn.silu(c_bg)], axis=-1) @ w_out
    return y_lat, y_ctx


def conformer_conv(val, glu_gate, silu_gate, dw_w, dw_b, ln_g, ln_b):
    u = val * jax.nn.sigmoid(glu_gate)
    u = lax.conv_general_dilated(u, dw_w.reshape(C_CONV, 1, C_WIDTH), window_strides=(1,),
                                 padding=[(C_CONV // 2, C_CONV // 2)],
                                 dimension_numbers=("NWC", "WIO", "NWC"),
                                 feature_group_count=C_WIDTH) + dw_b
    u = jax.nn.silu(layer_norm(u, ln_g, ln_b))
    return u * jax.nn.silu(silu_gate)


def odd_mixer(h_lat, h_ctx, w_in, w_out, dw_w, dw_b, ln_g, ln_b, rpb, update_ctx):
    bsz, n, _ = h_lat.shape
    ctx_len = h_ctx.shape[1]
    rows = n // GRID_W
    kh = min(NA_KH, rows)

    cv, cgl, cg, dq, dk, dv, dg = split_cols(h_lat @ w_in, ODD_WIDTHS)
    x_cv, x_cgl, x_cg, x_dq, x_dk, x_dv, x_dg = split_cols(h_ctx @ w_in, ODD_WIDTHS)
    c_dk, c_dv = heads(x_dk, D_HEADS), heads(x_dv, D_HEADS)

    oc = conformer_conv(cv, cgl, cg, dw_w, dw_b, ln_g, ln_b)

    q_rows = jnp.moveaxis(heads(dq, D_HEADS).reshape(bsz, rows, GRID_W, D_HEADS, HEAD_DIM), 1, 0)
    k_grid = heads(dk, D_HEADS).reshape(bsz, rows, GRID_W, D_HEADS, HEAD_DIM)
    v_grid = heads(dv, D_HEADS).reshape(bsz, rows, GRID_W, D_HEADS, HEAD_DIM)
    cols = jnp.arange(GRID_W)
    col_start = jnp.clip(cols - NA_KW // 2, 0, GRID_W - NA_KW)
    col_in = (cols[None, :] >= col_start[:, None]) & (cols[None, :] < col_start[:, None] + NA_KW)
    col_mask = jnp.where(col_in, 0.0, NEG_INF)
    dc_idx = jnp.clip(cols[None, :] - cols[:, None] + NA_KW - 1, 0, 2 * NA_KW - 2)
    rpb = rpb.astype(jnp.float32)
    ctx_open = jnp.zeros((D_HEADS, GRID_W, ctx_len), jnp.float32)

    def d_row(args):
        r, qr = args
        rs = jnp.clip(r - kh // 2, 0, rows - kh)
        kr = lax.dynamic_slice_in_dim(k_grid, rs, kh, axis=1).reshape(bsz, kh * GRID_W, D_HEADS, HEAD_DIM)
        vr = lax.dynamic_slice_in_dim(v_grid, rs, kh, axis=1).reshape(bsz, kh * GRID_W, D_HEADS, HEAD_DIM)
        dr_idx = rs + jnp.arange(kh) - r + NA_KH - 1
        bias = rpb[:, dr_idx][:, :, dc_idx] + col_mask[None, None]
        bias = jnp.transpose(bias, (0, 2, 1, 3)).reshape(D_HEADS, GRID_W, kh * GRID_W)
        bias = jnp.concatenate([ctx_open, bias], axis=-1)[None, :, None]
        kk = jnp.concatenate([c_dk, kr], axis=1)
        vv = jnp.concatenate([c_dv, vr], axis=1)
        return gqa_attend(qr[:, :, :, None, :], kk, vv, bias).reshape(bsz, GRID_W, D_W)

    od = lax.map(d_row, (jnp.arange(rows), q_rows))
    od = jnp.moveaxis(od, 0, 1).reshape(bsz, n, D_W)

    y_lat = jnp.concatenate([oc, od * jax.nn.silu(dg)], axis=-1) @ w_out
    if not update_ctx:
        return y_lat, None
    oc_c = conformer_conv(x_cv, x_cgl, x_cg, dw_w, dw_b, ln_g, ln_b)
    od_c = gqa_attend(heads(x_dq, D_HEADS)[:, :, :, None, :], c_dk, c_dv).reshape(bsz, ctx_len, D_W)
    y_ctx = jnp.concatenate([oc_c, od_c * jax.nn.silu(x_dg)], axis=-1) @ w_out
    return y_lat, y_ctx


def modulation(cond, w, b):
    m = jax.nn.silu(cond) @ w + b
    return jnp.split(m, 3, axis=-1)


def setup_inputs(seed: int = 0) -> dict:
    key = jax.random.key(seed)
    ks = jax.random.split(key, 24)
    n_even = (DEPTH + 1) // 2
    n_odd = DEPTH // 2
    f32 = jnp.float32

    def nrm(k, shape, scale):
        return jax.random.normal(k, shape, f32) * scale

    return {
        "x": nrm(ks[0], (BATCH, SEQ, D_MODEL), 1.0),
        "c": nrm(ks[1], (BATCH, D_MODEL), 1.0),
        "ctx": nrm(ks[2], (BATCH, CTX_LEN, D_MODEL), 1.0),
        "c_ctx": nrm(ks[3], (D_MODEL,), 1.0),
        "mod_w": nrm(ks[4], (DEPTH, D_MODEL, 3 * D_MODEL), 0.5 * D_MODEL ** -0.5),
        "mod_b": nrm(ks[5], (DEPTH, 3 * D_MODEL), 0.02),
        "norm_g": 1.0 + nrm(ks[6], (DEPTH, D_MODEL), 0.05),
        "ev_w_in": nrm(ks[7], (n_even, D_MODEL, EVEN_IN), D_MODEL ** -0.5),
        "ev_w_out": nrm(ks[8], (n_even, EVEN_MIX, D_MODEL), EVEN_MIX ** -0.5),
        "a_q_gain": 1.0 + nrm(ks[9], (n_even, HEAD_DIM), 0.05),
        "a_k_gain": 1.0 + nrm(ks[10], (n_even, HEAD_DIM), 0.05),
        "b_sink": nrm(ks[11], (n_even, B_HEADS), 0.5),
        "od_w_in": nrm(ks[12], (n_odd, D_MODEL, ODD_IN), D_MODEL ** -0.5),
        "od_w_out": nrm(ks[13], (n_odd, ODD_MIX, D_MODEL), ODD_MIX ** -0.5),
        "c_dw_w": nrm(ks[14], (n_odd, C_CONV, C_WIDTH), C_CONV ** -0.5),
        "c_dw_b": nrm(ks[15], (n_odd, C_WIDTH), 0.02),
        "c_ln_g": 1.0 + nrm(ks[16], (n_odd, C_WIDTH), 0.05),
        "c_ln_b": nrm(ks[17], (n_odd, C_WIDTH), 0.02),
        "d_rpb": nrm(ks[18], (n_odd, D_HEADS, 2 * NA_KH - 1, 2 * NA_KW - 1), 0.1),
        "final_g": 1.0 + nrm(ks[19], (D_MODEL,), 0.05),
    }


def reference(x, c, ctx, c_ctx, mod_w, mod_b, norm_g, ev_w_in, ev_w_out, a_q_gain, a_k_gain, b_sink,
              od_w_in, od_w_out, c_dw_w, c_dw_b, c_ln_g, c_ln_b, d_rpb, final_g):
    n = x.shape[1]
    cos, sin = axial_rope_tables(n)
    x_lat, x_ctx = x, ctx
    for i in range(DEPTH):
        update_ctx = i < DEPTH - 1
        sh_l, sc_l, g_l = modulation(c, mod_w[i], mod_b[i])
        sh_c, sc_c, g_c = modulation(c_ctx, mod_w[i], mod_b[i])
        h_lat = rms_norm(x_lat, norm_g[i]) * (1.0 + sc_l[:, None, :]) + sh_l[:, None, :]
        h_ctx = rms_norm(x_ctx, norm_g[i]) * (1.0 + sc_c) + sh_c
        j = i // 2
        if i % 2 == 0:
            y_lat, y_ctx = even_mixer(h_lat, h_ctx, ev_w_in[j], ev_w_out[j], a_q_gain[j], a_k_gain[j],
                                      b_sink[j], cos, sin, update_ctx)
        else:
            y_lat, y_ctx = odd_mixer(h_lat, h_ctx, od_w_in[j], od_w_out[j], c_dw_w[j], c_dw_b[j],
                                     c_ln_g[j], c_ln_b[j], d_rpb[j], update_ctx)
        x_lat = x_lat + g_l[:, None, :] * y_lat
        if update_ctx:
            x_ctx = x_ctx + g_c * y_ctx
    return rms_norm(x_lat, final_g)
```

```python
import numpy as np
from contextlib import ExitStack
import concourse.bass as bass
import concourse.mybir as mybir
from concourse.bass_utils import run_bass_kernel_spmd

F32 = mybir.dt.float32
BF16 = mybir.dt.bfloat16
AF = mybir.ActivationFunctionType
ALU = mybir.AluOpType

NT = 2304
NLAT = 2048
CHUNKS = [(0, 512), (512, 512), (1024, 512), (1536, 512), (2048, 256)]
EPS = 1e-6
NDMA_SEMS = 24


class Prog:
    ENGS = ("tensor", "vector", "scalar", "gpsimd", "sync")

    def __init__(self, nc):
        self.nc = nc
        self.ops = []

    def add(self, eng, fn, reads=(), writes=(), dma=False):
        self.ops.append((eng, fn, tuple(reads), tuple(writes), dma))

    def pe(self, fn, reads=(), writes=()): self.add("tensor", fn, reads, writes)
    def dve(self, fn, reads=(), writes=()): self.add("vector", fn, reads, writes)
    def act(self, fn, reads=(), writes=()): self.add("scalar", fn, reads, writes)
    def pool(self, fn, reads=(), writes=()): self.add("gpsimd", fn, reads, writes)
    def dma(self, fn, reads=(), writes=()): self.add("sync", fn, reads, writes, dma=True)

    def emit(self, stack):
        nc = self.nc
        sems = {e: stack.enter_context(nc.semaphore("s_" + e)) for e in self.ENGS if e != "sync"}
        dsems = [stack.enter_context(nc.semaphore("d%d" % i)) for i in range(NDMA_SEMS)]
        dcount = [0] * NDMA_SEMS
        seq = {e: 0 for e in self.ENGS}
        waited = {e: {} for e in self.ENGS}
        last_w = {}
        readers = {}
        per_eng = {e: [] for e in self.ENGS}
        ndma = 0
        for (eng, fn, reads, writes, dma) in self.ops:
            deps = {}

            def need(tok):
                s, v = tok
                if deps.get(s, 0) < v:
                    deps[s] = v
            for k in reads:
                if k in last_w:
                    if not (last_w[k][0] == eng == "tensor"):
                        need(last_w[k][1])
            for k in writes:
                if k in last_w and not (last_w[k][0] == eng and not dma):
                    need(last_w[k][1])
                for re, tok in readers.get(k, {}).items():
                    if re == eng and not dma:
                        continue
                    need(tok)
            if dma:
                si = ndma % NDMA_SEMS
                ndma += 1
                if dcount[si] > 0:
                    need((("d", si), 16 * dcount[si]))
                dcount[si] += 1
                tok = (("d", si), 16 * dcount[si])
                inc = 16
            else:
                seq[eng] += 1
                tok = (("e", eng), seq[eng])
                inc = 1
            waits = []
            for s, v in deps.items():
                if waited[eng].get(s, 0) < v:
                    waited[eng][s] = v
                    waits.append((s, v))
            per_eng[eng].append((waits, fn, tok[0], inc))
            for k in reads:
                readers.setdefault(k, {})[eng if not dma else ("dma", ndma)] = tok
            for k in writes:
                last_w[k] = (eng if not dma else "dma", tok)
                readers[k] = {}
        final = [(("d", i), 16 * dcount[i]) for i in range(NDMA_SEMS) if dcount[i] > 0]

        def semof(s):
            return dsems[s[1]] if s[0] == "d" else sems[s[1]]

        block = stack.enter_context(nc.Block())

        def make(engname):
            def body(e):
                for waits, fn, s, inc in per_eng[engname]:
                    for ws, wv in waits:
                        e.wait_ge(semof(ws), wv)
                    ins = fn(e)
                    ins.then_inc(semof(s), inc)
                if engname == "sync":
                    for ws, wv in final:
                        e.wait_ge(semof(ws), wv)
            return body

        for engname in self.ENGS:
            if per_eng[engname] or engname == "sync":
                getattr(block, engname)(make(engname))
        self.stats = {e: len(per_eng[e]) for e in self.ENGS}


def _pair_cols(t):
    return np.concatenate([np.arange(t * 64, t * 64 + 64), np.arange((4 + t) * 64, (4 + t) * 64 + 64)])


def _swap(idx):
    return idx ^ 1


def _piece_list(nl):
    pieces = []

    def mod(l):
        for m in range(24):
            pieces.append(("mod", l, np.arange(m * 128, m * 128 + 128)))
    mod(0)
    for l in range(nl):
        jl = l // 2
        if l % 2 == 0:
            for base in (0, 1280):
                kb, vb, gb = base + 512, base + 640, base + 768
                kc = np.arange(128)
                pieces.append(("ein", jl, kb + kc))
                pieces.append(("ein", jl, kb + _swap(kc)))
                pieces.append(("ein", jl, vb + kc))
                for t in range(4):
                    pc = _pair_cols(t)
                    pieces.append(("ein", jl, base + pc))
                    pieces.append(("ein", jl, base + _swap(pc)))
                    pieces.append(("ein", jl, gb + pc))
                for t in range(4):
                    pieces.append(("eout", jl, (0 if base == 0 else 512) + _pair_cols(t)))
                if base == 0 and l + 1 < nl:
                    mod(l + 1)
        else:
            for c in range(4):
                pieces.append(("oin", jl, 0 + c * 128 + np.arange(128)))
                pieces.append(("oin", jl, 512 + c * 128 + np.arange(128)))
            for c in range(4):
                pieces.append(("oin", jl, 1024 + c * 128 + np.arange(128)))
            for c in range(4):
                pieces.append(("oout", jl, c * 128 + np.arange(128)))
            if l + 1 < nl:
                mod(l + 1)
            for t in range(4):
                pieces.append(("oin", jl, 2048 + t * 128 + np.arange(128)))
                pieces.append(("oin", jl, 2560 + t * 128 + np.arange(128)))
                pieces.append(("oin", jl, 1536 + t * 128 + np.arange(128)))
                pieces.append(("oin", jl, 3072 + t * 128 + np.arange(128)))
            for t in range(4):
                pieces.append(("oout", jl, 512 + t * 128 + np.arange(128)))
    return pieces


def _host_prep(inp, nl):
    f32 = np.float32
    pieces = _piece_list(nl)
    W = np.empty((len(pieces), 128, 1024), f32)
    for i, (kind, l, idx) in enumerate(pieces):
        if kind == "mod":
            w = inp["mod_w"][l][:, idx]
            W[i] = w.reshape(8, 128, 128).transpose(1, 0, 2).reshape(128, 1024)
        elif kind == "ein":
            w = inp["ev_w_in"][l][:, idx]
            W[i] = w.reshape(8, 128, 128).transpose(1, 0, 2).reshape(128, 1024)
        elif kind == "oin":
            w = inp["od_w_in"][l][:, idx]
            W[i] = w.reshape(8, 128, 128).transpose(1, 0, 2).reshape(128, 1024)
        elif kind == "eout":
            W[i] = inp["ev_w_out"][l][idx, :]
        else:
            W[i] = inp["od_w_out"][l][idx, :]
    cst = np.zeros((128, 128 * 4 + 256), f32)
    cst[:, 0:128] = np.eye(128, dtype=f32)
    sh = np.zeros((128, 128), f32)
    sh[np.arange(64), 64 + np.arange(64)] = 1.0
    cst[:, 128:256] = sh
    bo = np.zeros((128, 128), f32)
    bo[:64, :64] = 1.0 / 64
    bo[64:, 64:] = 1.0 / 64
    cst[:, 256:384] = bo
    cst[:, 384:512] = 1.0
    jj = np.arange(128)[:, None]
    ii = np.arange(128)[None, :]
    cst[:, 512:640] = (jj >= ii).astype(f32)
    cst[:, 640:768] = (jj <= ii).astype(f32)
    t = np.arange(NLAT)
    row = (t // 64).astype(f32)
    col = (t % 64).astype(f32)
    half = 32
    freqs = (f32(10000.0) ** (-np.arange(0, half, 2, dtype=f32) / f32(half))).astype(f32)
    ang = np.concatenate([row[:, None] * freqs, col[:, None] * freqs], axis=-1).astype(f32)
    cos, sin = np.cos(ang).astype(f32), np.sin(ang).astype(f32)
    d = np.arange(128) % 64
    C = cos[:, d // 2].T
    S = sin[:, d // 2].T * np.where(d % 2 == 0, -1.0, 1.0)[:, None].astype(f32)
    rope = np.empty((4, 128, 2, 512), f32)
    for c in range(4):
        rope[c, :, 0, :] = C[:, c * 512:(c + 1) * 512]
        rope[c, :, 1, :] = S[:, c * 512:(c + 1) * 512]
    n_odd = max(1, nl // 2)
    tg = np.full((n_odd, 4, 3, 128, 1024), -1e30, f32)
    variants = [(dl, False) for dl in range(-3, 4)] + [(-2, True), (-1, False), (0, False), (1, False), (2, True)]
    kk = np.arange(128)
    rkl, ck = kk // 64, kk % 64
    rl, cq = kk // 64, kk % 64
    cs = np.clip(cq - 8, 0, 48)
    colok = (ck[:, None] >= cs[None, :]) & (ck[:, None] < cs[None, :] + 16)
    dc = np.clip(ck[:, None] - cq[None, :] + 15, 0, 30)
    for jl in range(nl // 2):
        rpb = inp["d_rpb"][jl]
        for pr in range(4):
            for s in range(2):
                h = 2 * pr + s
                for vi, (dl, msk) in enumerate(variants):
                    dr = 2 * dl + rkl[:, None] - rl[None, :]
                    ok = colok & (np.abs(dr) <= 7)
                    if msk:
                        ok = ok & (dr >= -4) & (dr <= 3)
                    g = rpb[h][np.clip(dr + 7, 0, 14), dc]
                    blk = np.where(ok, g, f32(-1e30)).astype(f32)
                    q = s * 12 + vi
                    tg[jl, pr, q // 8, :, (q % 8) * 128:(q % 8) * 128 + 128] = blk
    def fm(v):
        return np.asarray(v, f32).reshape(8, 128).T
    pars = []
    for b in range(8):
        cols = [fm(inp["c"][b]), fm(inp["c_ctx"])]
        for l in range(4):
            cols.append(fm(inp["norm_g"][l]))
            cols.append(np.asarray(inp["mod_b"][l], f32).reshape(24, 128).T)
        for jl in range(2):
            for nm in ("a_q_gain", "a_k_gain"):
                gq = np.asarray(inp[nm][jl], f32)
                cols.append(gq[d][:, None])
                cols.append(gq[d ^ 1][:, None])
            cols.append(np.broadcast_to(np.asarray(inp["b_sink"][jl], f32)[None, :], (128, 8)))
        for jl in range(2):
            dw = np.asarray(inp["c_dw_w"][jl], f32)
            cols.append(dw.reshape(31, 4, 128).transpose(2, 1, 0).reshape(128, 124))
            for nm in ("c_dw_b", "c_ln_g", "c_ln_b"):
                cols.append(np.asarray(inp[nm][jl], f32).reshape(4, 128).T)
        cols.append(fm(inp["final_g"]))
        pars.append(np.ascontiguousarray(np.concatenate(cols, axis=1)))
    xs = [np.ascontiguousarray(np.concatenate([inp["x"][b], inp["ctx"][b]], axis=0)) for b in range(8)]
    return dict(W=W, cst=cst, rope=rope, tg=tg, pars=pars, xs=xs, npar=pars[0].shape[1])


def _par_off():
    o = {}
    p = 0
    o["c"] = p; p += 8
    o["cctx"] = p; p += 8
    for l in range(4):
        o["ng", l] = p; p += 8
        o["mb", l] = p; p += 24
    for jl in range(2):
        o["qg", jl] = p; p += 1
        o["qgs", jl] = p; p += 1
        o["kg", jl] = p; p += 1
        o["kgs", jl] = p; p += 1
        o["sink", jl] = p; p += 8
    for jl in range(2):
        o["dww", jl] = p; p += 124
        o["dwb", jl] = p; p += 4
        o["lng", jl] = p; p += 4
        o["lnb", jl] = p; p += 4
    o["fg"] = p; p += 8
    o["n"] = p
    return o


def build(nl=4):
    nc = bass.Bass("TRN2", target_bir_lowering=False)
    PO = _par_off()
    npieces = len(_piece_list(nl))
    n_odd = max(1, nl // 2)
    xd = nc.dram_tensor("x", [NT, 1024], F32, kind="ExternalInput").ap()
    pard = nc.dram_tensor("par", [128, PO["n"]], F32, kind="ExternalInput").ap()
    wd = nc.dram_tensor("w", [npieces, 128, 1024], F32, kind="ExternalInput").ap()
    cstd = nc.dram_tensor("cst", [128, 768], F32, kind="ExternalInput").ap()
    roped = nc.dram_tensor("rope", [4, 128, 1024], F32, kind="ExternalInput").ap()
    tgd = nc.dram_tensor("tg", [n_odd, 4, 3, 128, 1024], F32, kind="ExternalInput").ap()
    yd = nc.dram_tensor("y", [NLAT, 1024], F32, kind="ExternalOutput").ap()

    st = ExitStack()
    with st:
        def SB(name, shape, dt):
            return st.enter_context(nc.sbuf_tensor(name, shape, dt))
        XT = SB("XT", [128, 8, NT], F32)
        HT = SB("HT", [128, 8, NT], BF16)
        QM = SB("QM", [128, 4, NT], BF16)
        KG = SB("KG", [128, 2, NT], BF16)
        VA = SB("VA", [128, 18, 2, 65], BF16)
        PT = SB("PT", [128, 2, 512], BF16)
        STG = SB("STG", [128, 2, 1024], F32)
        WB = SB("WB", [128, 4, 1024], BF16)
        WO = SB("WO", [128, 4, 1024], BF16)
        RCS = SB("RCS", [128, 2, 1024], F32)
        ED = SB("ED", [128, 3968], BF16)
        CB = SB("CB", [128, 768], BF16)
        ONS = SB("ONS", [128, 2, 128], BF16)
        PAR = SB("PAR", [128, PO["n"]], F32)
        MODW = SB("MODW", [128, 24, 2], F32)
        MCO = SB("MCO", [128, 3, 8, 2], F32)
        CS = SB("CS", [128, 8, 2], BF16)
        ESK = SB("ESK", [128, 8], F32)
        TF = [SB("TF%d" % i, [128, 512], F32) for i in range(4)]
        TB = [SB("TB%d" % i, [128, 512], BF16) for i in range(4)]
        RW = SB("RW", [128, 512], F32)
        RH = SB("RH", [128, 512], BF16)
        RL = SB("RL", [128, 512], BF16)
        OS = SB("OS", [64, 512], F32)
        ON = [SB("ON%d" % i, [64, 512], BF16) for i in range(2)]
        PS = [st.enter_context(nc.psum_tensor("PS%d" % i, [128, 512], F32)) for i in range(8)]

        KTt = KG[:, 0, :]
        GTt = KG[:, 1, :]
        UP = KG[:].rearrange("p a b -> p (a b)")
        VA36 = VA[:].rearrange("p t g d -> p (t g) d")
        IDENT = CB[:, 0:128]
        SHIFT = CB[:, 128:256]
        BONES = CB[:, 256:384]
        ONES = CB[:, 384:512]
        TRIL = CB[:, 512:640]
        TRIU = CB[:, 640:768]
        DG = ED[:, 0:3968].rearrange("p (k m) -> p k m", k=31)
        ET = ED[:, 0:3072].rearrange("p (s i q) -> p s i q", s=2, i=12)

        P = Prog(nc)
        rr = {"stg": 0, "wb": 0, "pt": 0, "sb": 0, "rcs": 0, "pj": 0, "piece": 0}

        def ACT(out, in_, func, reads, writes, **kw):
            P.act(lambda e: e.activation(out, in_, func, **kw), reads, writes)

        def TT(out, a, b, op, reads, writes, eng="vector"):
            P.add(eng, lambda e: e.tensor_tensor(out, a, b, op), reads, writes)

        def STT(out, in0, scalar, in1, op0, op1, reads, writes):
            P.dve(lambda e: e.scalar_tensor_tensor(out, in0, scalar, in1, op0, op1), reads, writes)

        def TS(out, in0, s1, s2, op0, op1, reads, writes):
            if op1 is None:
                P.dve(lambda e: e.tensor_scalar(out, in0, s1, None, op0), reads, writes)
            else:
                P.dve(lambda e: e.tensor_scalar(out, in0, s1, s2, op0, op1), reads, writes)

        def CPY(out, in_, reads, writes, eng="vector"):
            P.add(eng, lambda e: e.tensor_copy(out, in_), reads, writes)

        def RCP(out, in_, reads, writes):
            P.dve(lambda e: e.reciprocal(out, in_), reads, writes)

        def MM(lst, reads, writes):
            def f(e):
                r = None
                for (o, l, rh, s0, s1) in lst:
                    r = e.matmul(o, l, rh, start=s0, stop=s1)
                return r
            P.pe(f, reads, writes)

        def tiles(c0, n):
            return list(range(c0 // 128, (c0 + n) // 128))

        def next_piece():
            i = rr["piece"]; rr["piece"] += 1
            s = rr["stg"] % 2; rr["stg"] += 1
            w = rr["wb"] % 4; rr["wb"] += 1
            P.dma(lambda e: e.dma_start(out=STG[:, s, :], in_=wd[i]), writes=[("STG", s)])
            CPY(WB[:, w, :], STG[:, s, :], [("STG", s)], [("WB", w)], eng="gpsimd")
            return w

        def next_wo(j):
            i = rr["piece"]; rr["piece"] += 1
            s = rr["stg"] % 2; rr["stg"] += 1
            P.dma(lambda e: e.dma_start(out=STG[:, s, :], in_=wd[i]), writes=[("STG", s)])
            CPY(WO[:, j, :], STG[:, s, :], [("STG", s)], [("WO", j)], eng="gpsimd")

        def proj(w, c0, n, bank):
            wv = WB[:, w, :].rearrange("p (k m) -> p k m", k=8)
            MM([(PS[bank][:, 0:n], wv[:, kc, :], HT[:, kc, c0:c0 + n], kc == 0, kc == 7) for kc in range(8)],
               [("WB", w), ("H", c0)], [("ps", bank)])

        def sigm_recip(dst, src_ps, n, rkeys):
            ACT(dst[:, 0:n], src_ps, AF.Exp, rkeys, [("t", id(dst))], scale=-1.0)
            TS(dst[:, 0:n], dst[:, 0:n], 1.0, None, ALU.add, None, [("t", id(dst))], [("t", id(dst))])
            RCP(dst[:, 0:n], dst[:, 0:n], [("t", id(dst))], [("t", id(dst))])

        P.dma(lambda e: e.dma_start(out=PAR[:], in_=pard), writes=["PAR"])
        P.dma(lambda e: e.dma_start(out=STG[:, 0, 0:768], in_=cstd), writes=[("STG", 0)])
        CPY(CB[:], STG[:, 0, 0:768], [("STG", 0)], ["CB"])
        rr["stg"] = 1
        P.pool(lambda e: e.memset(ONS[:, 0, :], 1.0 / 1024), writes=["ONS"])
        P.pool(lambda e: e.memset(ONS[:, 1, :], 1.0 / 512), writes=["ONS"])
        P.pool(lambda e: e.memset(VA[:], 1.0), writes=[("V", t) for t in range(18)])
        P.pool(lambda e: e.memset(RW[:], 1.0), writes=["RW"])
        for q, key in enumerate(("c", "cctx")):
            src = PAR[:, PO[key]:PO[key] + 8]
            ACT(TF[0][:, 0:8], src, AF.Exp, ["PAR"], [("t", id(TF[0]))], scale=-1.0)
            TS(TF[0][:, 0:8], TF[0][:, 0:8], 1.0, None, ALU.add, None, [("t", id(TF[0]))], [("t", id(TF[0]))])
            RCP(TF[0][:, 0:8], TF[0][:, 0:8], [("t", id(TF[0]))], [("t", id(TF[0]))])
            TT(CS[:, :, q], src, TF[0][:, 0:8], ALU.mult, [("t", id(TF[0])), "PAR"], ["CS"])
        for t in range(18):
            s = rr["stg"] % 2; rr["stg"] += 1
            P.dma(lambda e, t=t, s=s: e.dma_start(out=STG[:, s, :], in_=xd[t * 128:(t + 1) * 128, :]), writes=[("STG", s)])
            hi, lo = WB[:, 2 * (t % 2), :], WB[:, 2 * (t % 2) + 1, :]
            kh, kl = ("WB", 2 * (t % 2)), ("WB", 2 * (t % 2) + 1)
            CPY(hi, STG[:, s, :], [("STG", s)], [kh])
            TT(lo, STG[:, s, :], hi, ALU.subtract, [("STG", s), kh], [kl])
            for fg in range(2):
                bank = rr["pj"] % 2; rr["pj"] += 1
                lst = []
                for i in range(4):
                    f = fg * 4 + i
                    lst.append((PS[bank][:, i * 128:(i + 1) * 128], hi[:, f * 128:(f + 1) * 128], IDENT, True, False))
                    lst.append((PS[bank][:, i * 128:(i + 1) * 128], lo[:, f * 128:(f + 1) * 128], IDENT, False, True))
                MM(lst, [kh, kl, "CB"], [("ps", bank)])
                cc = min(t // 4, 4)
                P.act(lambda e, bank=bank, fg=fg, t=t: e.copy(XT[:, fg * 4:fg * 4 + 4, t * 128:(t + 1) * 128],
                                                                PS[bank][:, :].rearrange("p (a b) -> p a b", a=4)),
                      [("ps", bank)], [("X", kc, CHUNKS[cc][0]) for kc in range(fg * 4, fg * 4 + 4)])

        def modulation(l):
            for m in range(24):
                w = next_piece()
                wv = WB[:, w, :].rearrange("p (k m) -> p k m", k=8)
                MM([(PS[6][:, 0:2], wv[:, kc, :], CS[:, kc, :], kc == 0, kc == 7) for kc in range(8)],
                   [("WB", w), "CS"], [("ps", 6)])
                TS(MODW[:, m, :], PS[6][:, 0:2], PAR[:, PO["mb", l] + m:PO["mb", l] + m + 1], None, ALU.add, None,
                   [("ps", 6), "PAR"], ["MODW"])

        def mod_finish(l):
            ng = PAR[:, PO["ng", l]:PO["ng", l] + 8]
            for q in range(2):
                TS(MCO[:, 0, :, q], MODW[:, 8:16, q], 1.0, None, ALU.add, None, ["MODW"], ["MCO"])
                TT(MCO[:, 0, :, q], MCO[:, 0, :, q], ng, ALU.mult, ["MCO", "PAR"], ["MCO"])
                CPY(MCO[:, 1, :, q], MODW[:, 0:8, q], ["MODW"], ["MCO"])
                CPY(MCO[:, 2, :, q], MODW[:, 16:24, q], ["MODW"], ["MCO"])

        def norm_to_h():
            for (c0, n) in CHUNKS:
                q = 0 if c0 < NLAT else 1
                for kc in range(8):
                    sq = TB[kc % 2]
                    ACT(sq[:, 0:n], XT[:, kc, c0:c0 + n], AF.Square, [("X", kc, c0)], [("t", id(sq))])
                    MM([(PS[6][:, 0:n], ONS[:, 0, :], sq[:, 0:n], kc == 0, kc == 7)], [("t", id(sq)), "ONS"], [("ps", 6)])
                ACT(TF[0][:, 0:n], PS[6][:, 0:n], AF.Ln, [("ps", 6)], [("t", id(TF[0]))], bias=EPS)
                ACT(TF[0][:, 0:n], TF[0][:, 0:n], AF.Exp, [("t", id(TF[0]))], [("t", id(TF[0]))], scale=-0.5)
                for kc in range(8):
                    tmp = TF[1 + kc % 2]
                    STT(tmp[:, 0:n], XT[:, kc, c0:c0 + n], MCO[:, 0, kc, q:q + 1], TF[0][:, 0:n], ALU.mult, ALU.mult,
                        [("X", kc, c0), "MCO", ("t", id(TF[0]))], [("t", id(tmp))])
                    ACT(HT[:, kc, c0:c0 + n], tmp[:, 0:n], AF.Identity, [("t", id(tmp)), "MCO"], [("H", c0)],
                        bias=MCO[:, 1, kc, q:q + 1], scale=1.0)

        def qk_tile(dst_fn, dkeys_fn, use_norm, gain_ap, gain_sw_ap, chunks):
            w = next_piece()
            ws = next_piece()
            for (c0, n) in chunks:
                lat = c0 < NLAT
                proj(w, c0, n, 0)
                if lat:
                    proj(ws, c0, n, 1)
                if use_norm:
                    ACT(TB[0][:, 0:n], PS[0][:, 0:n], AF.Square, [("ps", 0)], [("t", id(TB[0]))])
                    MM([(PS[6][:, 0:n], BONES, TB[0][:, 0:n], True, True)], [("t", id(TB[0])), "CB"], [("ps", 6)])
                    ACT(TF[0][:, 0:n], PS[6][:, 0:n], AF.Ln, [("ps", 6)], [("t", id(TF[0]))], bias=EPS)
                    ACT(TF[0][:, 0:n], TF[0][:, 0:n], AF.Exp, [("t", id(TF[0]))], [("t", id(TF[0]))], scale=-0.5)
                    STT(TF[1][:, 0:n], PS[0][:, 0:n], gain_ap, TF[0][:, 0:n], ALU.mult, ALU.mult,
                        [("ps", 0), "PAR", ("t", id(TF[0]))], [("t", id(TF[1]))])
                    if lat:
                        STT(TF[2][:, 0:n], PS[1][:, 0:n], gain_sw_ap, TF[0][:, 0:n], ALU.mult, ALU.mult,
                            [("ps", 1), "PAR", ("t", id(TF[0]))], [("t", id(TF[2]))])
                    a_src, b_src = TF[1][:, 0:n], TF[2][:, 0:n]
                    akeys, bkeys = [("t", id(TF[1]))], [("t", id(TF[2]))]
                else:
                    a_src, b_src = PS[0][:, 0:n], PS[1][:, 0:n]
                    akeys, bkeys = [("ps", 0)], [("ps", 1)]
                if lat:
                    rs = rr["rcs"] % 2; rr["rcs"] += 1
                    ci = c0 // 512
                    P.dma(lambda e, rs=rs, ci=ci: e.dma_start(out=RCS[:, rs, :], in_=roped[ci]), writes=[("RCS", rs)])
                    TT(TF[1][:, 0:n], a_src, RCS[:, rs, 0:n], ALU.mult, akeys + [("RCS", rs)], [("t", id(TF[1]))])
                    TT(TF[2][:, 0:n], b_src, RCS[:, rs, 512:512 + n], ALU.mult, bkeys + [("RCS", rs)], [("t", id(TF[2]))])
                    TT(dst_fn(c0, n), TF[1][:, 0:n], TF[2][:, 0:n], ALU.add,
                       [("t", id(TF[1])), ("t", id(TF[2]))], dkeys_fn(c0, n))
                else:
                    if use_norm:
                        CPY(dst_fn(c0, n), a_src, akeys, dkeys_fn(c0, n))
                    else:
                        P.act(lambda e, c0=c0, n=n: e.copy(dst_fn(c0, n), PS[0][:, 0:n]), [("ps", 0)], dkeys_fn(c0, n))

        def plain_tile(dst_fn, dkeys_fn, chunks):
            w = next_piece()
            for (c0, n) in chunks:
                bank = rr["pj"] % 2; rr["pj"] += 1
                proj(w, c0, n, bank)
                P.act(lambda e, c0=c0, n=n, bank=bank: e.copy(dst_fn(c0, n), PS[bank][:, 0:n]), [("ps", bank)], dkeys_fn(c0, n))

        def v_tile():
            w = next_piece()
            wv = WB[:, w, :].rearrange("p (k m) -> p k m", k=8)
            for g4 in range(5):
                tl = list(range(g4 * 4, min(g4 * 4 + 4, 18)))
                bank = rr["pj"] % 2; rr["pj"] += 1
                lst = []
                for i, t in enumerate(tl):
                    for kc in range(8):
                        lst.append((PS[bank][:, i * 128:(i + 1) * 128], HT[:, kc, t * 128:(t + 1) * 128], wv[:, kc, :], kc == 0, kc == 7))
                MM(lst, [("WB", w)] + [("H", CHUNKS[min(t // 4, 4)][0]) for t in tl], [("ps", bank)])
                nt_ = len(tl)
                P.act(lambda e, bank=bank, tl=tl, nt_=nt_: e.copy(
                    VA36[:, 2 * tl[0]:2 * tl[0] + 2 * nt_, 0:64],
                    PS[bank][:, 0:nt_ * 128].rearrange("p (a b) -> p a b", b=64)),
                    [("ps", bank)], [("V", t) for t in tl])

        def gate_tile(chunks):
            w = next_piece()
            for (c0, n) in chunks:
                bank = rr["pj"] % 2; rr["pj"] += 1
                proj(w, c0, n, bank)
                sigm_recip(TF[3], PS[bank][:, 0:n], n, [("ps", bank)])
                TT(GTt[:, c0:c0 + n], PS[bank][:, 0:n], TF[3][:, 0:n], ALU.mult,
                   [("ps", bank), ("t", id(TF[3]))], [("G", t) for t in tiles(c0, n)])

        def attend(j, s, c0, nq, ktiles):
            G = 512 // nq
            qk = [("Q", j, t) for t in tiles(c0, nq)]
            nk = len(ktiles)
            done = 0
            Ob = 4 + s
            for g0 in range(0, nk, G):
                grp = ktiles[g0:g0 + G]
                sb = 2 + rr["sb"] % 2; rr["sb"] += 1
                pt = rr["pt"] % 2; rr["pt"] += 1
                wdt = len(grp) * nq
                MM([(PS[sb][:, i * nq:(i + 1) * nq], KTt[64 * s:64 * s + 64, kt * 128:(kt + 1) * 128],
                     QM[64 * s:64 * s + 64, j, c0:c0 + nq], True, True) for i, (kt, _m) in enumerate(grp)],
                   [("K", kt) for kt, _m in grp] + qk, [("ps", sb)])
                ACT(PT[:, pt, 0:wdt], PS[sb][:, 0:wdt], AF.Exp, [("ps", sb)], [("PT", pt)], scale=0.125)
                for i, (kt, m) in enumerate(grp):
                    if m is not None:
                        TT(PT[:, pt, i * nq:(i + 1) * nq], PT[:, pt, i * nq:(i + 1) * nq], m[0], ALU.mult,
                           [("PT", pt), m[1]], [("PT", pt)])
                MM([(PS[Ob][0:65, 0:nq], VA[:, kt, s, 0:65], PT[:, pt, i * nq:(i + 1) * nq],
                     done + i == 0, done + i == nk - 1) for i, (kt, _m) in enumerate(grp)],
                   [("PT", pt)] + [("V", kt) for kt, _m in grp], [("ps", Ob)])
                done += len(grp)

        def attn_epilogue(j, c0, nq, sink_heads):
            for s in range(2):
                Ob = 4 + s
                if sink_heads is not None:
                    TS(RW[64:65, 0:nq], PS[Ob][64:65, 0:nq], ESK[64:65, sink_heads[s]:sink_heads[s] + 1], None, ALU.add, None,
                       [("ps", Ob), "ESK"], ["RW"])
                    RCP(RW[64:65, 0:nq], RW[64:65, 0:nq], ["RW"], ["RW"])
                else:
                    RCP(RW[64:65, 0:nq], PS[Ob][64:65, 0:nq], [("ps", Ob)], ["RW"])
                CPY(RH[64:65, 0:nq], RW[64:65, 0:nq], ["RW"], ["RH"])
                TT(RL[64:65, 0:nq], RW[64:65, 0:nq], RH[64:65, 0:nq], ALU.subtract, ["RW", "RH"], ["RL"])
                MM([(PS[6][0:64, 0:nq], ONES[64:65, 0:64], RH[64:65, 0:nq], True, False),
                    (PS[6][0:64, 0:nq], ONES[64:65, 0:64], RL[64:65, 0:nq], False, True)], ["RH", "RL", "CB"], [("ps", 6)])
                P.act(lambda e, Ob=Ob: e.copy(OS[0:64, 0:nq], PS[Ob][0:64, 0:nq]), [("ps", Ob)], ["OS"])
                TT(ON[s][:, 0:nq], OS[0:64, 0:nq], PS[6][0:64, 0:nq], ALU.mult, ["OS", ("ps", 6)], [("ON", s)])
            MM([(PS[7][:, 0:nq], IDENT[0:64, :], ON[0][:, 0:nq], True, False),
                (PS[7][:, 0:nq], SHIFT[0:64, :], ON[1][:, 0:nq], False, True)], [("ON", 0), ("ON", 1), "CB"], [("ps", 7)])
            tl = tiles(c0, nq)
            TT(QM[:, j, c0:c0 + nq], PS[7][:, 0:nq], GTt[:, c0:c0 + nq], ALU.mult,
               [("ps", 7)] + [("G", t) for t in tl], [("Q", j, t) for t in tl])

        def out_proj(upd_ctx):
            for (c0, n) in CHUNKS:
                if c0 >= NLAT and not upd_ctx:
                    continue
                q = 0 if c0 < NLAT else 1
                for m in range(8):
                    bank = rr["pj"] % 2; rr["pj"] += 1
                    MM([(PS[bank][:, 0:n], WO[:, j, m * 128:(m + 1) * 128], QM[:, j, c0:c0 + n], j == 0, j == 3) for j in range(4)],
                       [("WO", j) for j in range(4)] + [("Q", j, t) for j in range(4) for t in tiles(c0, n)], [("ps", bank)])
                    STT(XT[:, m, c0:c0 + n], PS[bank][:, 0:n], MCO[:, 2, m, q:q + 1], XT[:, m, c0:c0 + n], ALU.mult, ALU.add,
                        [("ps", bank), "MCO", ("X", m, c0)], [("X", m, c0)])

        QCH = lambda upd: [ch for ch in CHUNKS if ch[0] < NLAT or upd]
        qdst = lambda j: (lambda c0, n: QM[:, j, c0:c0 + n])
        qkeys = lambda j: (lambda c0, n: [("Q", j, t) for t in tiles(c0, n)])
        kdst = lambda c0, n: KTt[:, c0:c0 + n]
        kkeys = lambda c0, n: [("K", t) for t in tiles(c0, n)]

        def even_layer(l, upd):
            jl = l // 2
            ACT(ESK[:], PAR[:, PO["sink", jl]:PO["sink", jl] + 8], AF.Exp, ["PAR"], ["ESK"])
            for mixer in range(2):
                isA = mixer == 0
                kg = PAR[:, PO["kg", jl]:PO["kg", jl] + 1]
                kgs = PAR[:, PO["kgs", jl]:PO["kgs", jl] + 1]
                qg = PAR[:, PO["qg", jl]:PO["qg", jl] + 1]
                qgs = PAR[:, PO["qgs", jl]:PO["qgs", jl] + 1]
                qk_tile(kdst, kkeys, isA, kg, kgs, CHUNKS)
                v_tile()
                for j in range(4):
                    qk_tile(qdst(j), qkeys(j), isA, qg, qgs, QCH(upd))
                    gate_tile(QCH(upd))
                    if isA:
                        for (c0, n) in QCH(upd):
                            kts = [(kt, None) for kt in ([16, 17] + list(range(16)) if c0 < NLAT else [16, 17])]
                            for s in range(2):
                                attend(j, s, c0, n, kts)
                            attn_epilogue(j, c0, n, None)
                    else:
                        blocks = list(range(16)) + ([16, 17] if upd else [])
                        for qb in blocks:
                            kts = [(16, None), (17, None)]
                            if qb < 16:
                                if qb > 0:
                                    kts.append((qb - 1, (TRIL, "CB")))
                                kts.append((qb, None))
                                if qb < 15:
                                    kts.append((qb + 1, (TRIU, "CB")))
                            for s in range(2):
                                attend(j, s, qb * 128, 128, kts)
                            attn_epilogue(j, qb * 128, 128, (j, 4 + j))
                for j in range(4):
                    next_wo(j)
                out_proj(upd)
                if mixer == 0 and l + 1 < nl:
                    modulation(l + 1)

        def odd_layer(l, upd):
            jl = l // 2
            chunks = QCH(upd)
            seqs = [(0, 2048, 0)] + ([(2048, 256, 2078)] if upd else [])
            P.pool(lambda e: e.memset(UP[:, 0:2364], 0.0), writes=[("K", t) for t in range(18)] + [("G", t) for t in range(18)] + ["UP"])
            for c in range(4):
                wv_ = next_piece()
                wg_ = next_piece()
                for (c0, n) in chunks:
                    proj(wv_, c0, n, 0)
                    proj(wg_, c0, n, 1)
                    sigm_recip(TF[3], PS[1][:, 0:n], n, [("ps", 1)])
                    base = 15 + c0 if c0 < NLAT else 2078 + 15 + (c0 - NLAT)
                    TT(UP[:, base:base + n], PS[0][:, 0:n], TF[3][:, 0:n], ALU.mult, [("ps", 0), ("t", id(TF[3]))], ["UP"])
                for k in range(31):
                    col = PO["dww", jl] + c * 31 + k
                    TS(DG[:, k, :], IDENT, PAR[:, col:col + 1], None, ALU.mult, None, ["CB", "PAR"], ["DG"])
                for (c0, n) in chunks:
                    bank = rr["pj"] % 2; rr["pj"] += 1
                    base = c0 if c0 < NLAT else 2078 + (c0 - NLAT)
                    MM([(PS[bank][:, 0:n], DG[:, k, :], UP[:, base + k:base + k + n], k == 0, k == 30) for k in range(31)],
                       ["DG", "UP"], [("ps", bank)])
                    ACT(QM[:, c, c0:c0 + n], PS[bank][:, 0:n], AF.Identity, [("ps", bank), "PAR"],
                        [("Q", c, t) for t in tiles(c0, n)], bias=PAR[:, PO["dwb", jl] + c:PO["dwb", jl] + c + 1], scale=1.0)
            wgs = [next_piece() for _ in range(4)]
            for (c0, n) in chunks:
                tl = tiles(c0, n)
                for c in range(4):
                    MM([(PS[6][:, 0:n], ONS[:, 1, :], QM[:, c, c0:c0 + n], c == 0, c == 3)], [("Q", c, t) for t in tl] + ["ONS"], [("ps", 6)])
                for c in range(4):
                    sq = TB[c % 2]
                    ACT(sq[:, 0:n], QM[:, c, c0:c0 + n], AF.Square, [("Q", c, t) for t in tl], [("t", id(sq))])
                    MM([(PS[7][:, 0:n], ONS[:, 1, :], sq[:, 0:n], c == 0, c == 3)], [("t", id(sq)), "ONS"], [("ps", 7)])
                P.act(lambda e, n=n: e.copy(TF[0][:, 0:n], PS[6][:, 0:n]), [("ps", 6)], [("t", id(TF[0]))])
                TT(TF[1][:, 0:n], TF[0][:, 0:n], TF[0][:, 0:n], ALU.mult, [("t", id(TF[0]))], [("t", id(TF[1]))])
                TT(TF[1][:, 0:n], PS[7][:, 0:n], TF[1][:, 0:n], ALU.subtract, [("ps", 7), ("t", id(TF[1]))], [("t", id(TF[1]))])
                ACT(TF[1][:, 0:n], TF[1][:, 0:n], AF.Ln, [("t", id(TF[1]))], [("t", id(TF[1]))], bias=EPS)
                ACT(TF[1][:, 0:n], TF[1][:, 0:n], AF.Exp, [("t", id(TF[1]))], [("t", id(TF[1]))], scale=-0.5)
                for c in range(4):
                    lg = PAR[:, PO["lng", jl] + c:PO["lng", jl] + c + 1]
                    lb = PAR[:, PO["lnb", jl] + c:PO["lnb", jl] + c + 1]
                    qmk = [("Q", c, t) for t in tl]
                    TT(TF[2][:, 0:n], QM[:, c, c0:c0 + n], TF[0][:, 0:n], ALU.subtract, qmk + [("t", id(TF[0]))], [("t", id(TF[2]))])
                    TT(TF[2][:, 0:n], TF[2][:, 0:n], TF[1][:, 0:n], ALU.mult, [("t", id(TF[2])), ("t", id(TF[1]))], [("t", id(TF[2]))])
                    ACT(TF[2][:, 0:n], TF[2][:, 0:n], AF.Identity, [("t", id(TF[2])), "PAR"], [("t", id(TF[2]))], bias=lb, scale=lg)
                    sigm_recip(TF[3], TF[2][:, 0:n], n, [("t", id(TF[2]))])
                    TT(TF[2][:, 0:n], TF[2][:, 0:n], TF[3][:, 0:n], ALU.mult, [("t", id(TF[2])), ("t", id(TF[3]))], [("t", id(TF[2]))])
                    proj(wgs[c], c0, n, 0)
                    sigm_recip(TF[3], PS[0][:, 0:n], n, [("ps", 0)])
                    TT(TF[3][:, 0:n], PS[0][:, 0:n], TF[3][:, 0:n], ALU.mult, [("ps", 0), ("t", id(TF[3]))], [("t", id(TF[3]))])
                    TT(QM[:, c, c0:c0 + n], TF[2][:, 0:n], TF[3][:, 0:n], ALU.mult, [("t", id(TF[2])), ("t", id(TF[3]))], qmk)
            for j in range(4):
                next_wo(j)
            out_proj(upd)
            if l + 1 < nl:
                modulation(l + 1)
            for j in range(4):
                plain_tile(kdst, lambda c0, n: kkeys(c0, n) + ["UP"], CHUNKS)
                v_tile()
                plain_tile(qdst(j), qkeys(j), chunks)
                gate_tile(chunks)
                for pc in range(3):
                    s_ = rr["stg"] % 2; rr["stg"] += 1
                    P.dma(lambda e, s_=s_, pc=pc, j=j: e.dma_start(out=STG[:, s_, :], in_=tgd[jl, j, pc]), writes=[("STG", s_)])
                    ACT(ED[:, pc * 1024:(pc + 1) * 1024], STG[:, s_, :], AF.Exp, [("STG", s_)], ["DG"])
                blocks = list(range(16)) + ([16, 17] if upd else [])
                for p_ in blocks:
                    for s in range(2):
                        kts = [(16, None), (17, None)]
                        if p_ < 16:
                            if p_ == 0:
                                lt, i0 = [0, 1, 2, 3], 3
                            elif p_ == 1:
                                lt, i0 = [0, 1, 2, 3], 2
                            elif p_ == 14:
                                lt, i0 = [12, 13, 14, 15], 1
                            elif p_ == 15:
                                lt, i0 = [12, 13, 14, 15], 0
                            else:
                                lt, i0 = list(range(p_ - 2, p_ + 3)), 7
                            for i, kt in enumerate(lt):
                                kts.append((kt, (ET[:, s, i0 + i, :], "DG")))
                        attend(j, s, p_ * 128, 128, kts)
                    attn_epilogue(j, p_ * 128, 128, None)
            for j in range(4):
                next_wo(j)
            out_proj(upd)

        modulation(0)
        for l in range(nl):
            upd = l < nl - 1
            mod_finish(l)
            norm_to_h()
            if l % 2 == 0:
                even_layer(l, upd)
            else:
                odd_layer(l, upd)
        fg = PAR[:, PO["fg"]:PO["fg"] + 8]
        for (c0, n) in CHUNKS[:4]:
            for kc in range(8):
                sq = TB[kc % 2]
                ACT(sq[:, 0:n], XT[:, kc, c0:c0 + n], AF.Square, [("X", kc, c0)], [("t", id(sq))])
                MM([(PS[6][:, 0:n], ONS[:, 0, :], sq[:, 0:n], kc == 0, kc == 7)], [("t", id(sq)), "ONS"], [("ps", 6)])
            ACT(TF[0][:, 0:n], PS[6][:, 0:n], AF.Ln, [("ps", 6)], [("t", id(TF[0]))], bias=EPS)
            ACT(TF[0][:, 0:n], TF[0][:, 0:n], AF.Exp, [("t", id(TF[0]))], [("t", id(TF[0]))], scale=-0.5)
            for kc in range(8):
                STT(TF[1][:, 0:n], XT[:, kc, c0:c0 + n], fg[:, kc:kc + 1], TF[0][:, 0:n], ALU.mult, ALU.mult,
                    [("X", kc, c0), "PAR", ("t", id(TF[0]))], [("t", id(TF[1]))])
                CPY(HT[:, kc, c0:c0 + n], TF[1][:, 0:n], [("t", id(TF[1]))], [("H", c0)])
                TT(QM[:, kc % 4, (kc // 4) * 512:(kc // 4) * 512 + n], TF[1][:, 0:n], HT[:, kc, c0:c0 + n], ALU.subtract,
                   [("t", id(TF[1])), ("H", c0)], [("LO", kc)])
            for ti in range(4):
                t = c0 // 128 + ti
                s = rr["stg"] % 2; rr["stg"] += 1
                for fgp in range(2):
                    bank = rr["pj"] % 2; rr["pj"] += 1
                    lst = []
                    for i in range(4):
                        kc = fgp * 4 + i
                        lo = QM[:, kc % 4, (kc // 4) * 512 + ti * 128:(kc // 4) * 512 + ti * 128 + 128]
                        lst.append((PS[bank][:, i * 128:(i + 1) * 128], HT[:, kc, t * 128:(t + 1) * 128], IDENT, True, False))
                        lst.append((PS[bank][:, i * 128:(i + 1) * 128], lo, IDENT, False, True))
                    MM(lst, [("H", c0), "CB"] + [("LO", kc) for kc in range(8)], [("ps", bank)])
                    P.act(lambda e, bank=bank, s=s, fgp=fgp: e.copy(STG[:, s, fgp * 512:(fgp + 1) * 512], PS[bank][:, :]),
                          [("ps", bank)], [("STG", s)])
                P.dma(lambda e, s=s, t=t: e.dma_start(out=yd[t * 128:(t + 1) * 128, :], in_=STG[:, s, :]), reads=[("STG", s)])
        assert rr["piece"] == npieces, (rr["piece"], npieces)
        P.emit(st)
    return nc, P


_CACHE = {}


def run(inputs, nl=4, cores=8):
    prep = _host_prep(inputs, nl)
    if nl not in _CACHE:
        _CACHE[nl] = build(nl)
    nc, _ = _CACHE[nl]
    in_maps = []
    for b in range(cores):
        in_maps.append({"x": prep["xs"][b], "par": prep["pars"][b], "w": prep["W"], "cst": prep["cst"][:, 0:768],
                        "rope": prep["rope"].reshape(4, 128, 1024), "tg": prep["tg"]})
    res = run_bass_kernel_spmd(nc, in_maps, core_ids=list(range(cores)))
    return np.stack([np.asarray(r["y"], dtype=np.float32) for r in res.results], axis=0)


def kernel(**inputs):
    inputs = {k: np.asarray(v) for k, v in inputs.items()}
    return run(inputs, 4, 8)
```

```python
import numpy as np
from contextlib import ExitStack
import concourse.bass as bass
import concourse.mybir as mybir
from concourse.bass_utils import run_bass_kernel_spmd

F32 = mybir.dt.float32
BF16 = mybir.dt.bfloat16
AF = mybir.ActivationFunctionType
ALU = mybir.AluOpType

NT = 2304
NLAT = 2048
CHUNKS = [(0, 512), (512, 512), (1024, 512), (1536, 512), (2048, 256)]
EPS = 1e-6
NDMA_SEMS = 24


class Prog:
    ENGS = ("tensor", "vector", "scalar", "gpsimd", "sync")

    def __init__(self, nc):
        self.nc = nc
        self.ops = []

    def add(self, eng, fn, reads=(), writes=(), dma=False):
        self.ops.append((eng, fn, tuple(reads), tuple(writes), dma))

    def pe(self, fn, reads=(), writes=()): self.add("tensor", fn, reads, writes)
    def dve(self, fn, reads=(), writes=()): self.add("vector", fn, reads, writes)
    def act(self, fn, reads=(), writes=()): self.add("scalar", fn, reads, writes)
    def pool(self, fn, reads=(), writes=()): self.add("gpsimd", fn, reads, writes)
    def dma(self, fn, reads=(), writes=()): self.add("sync", fn, reads, writes, dma=True)

    def emit(self, stack):
        nc = self.nc
        sems = {e: stack.enter_context(nc.semaphore("s_" + e)) for e in self.ENGS if e != "sync"}
        dsems = [stack.enter_context(nc.semaphore("d%d" % i)) for i in range(NDMA_SEMS)]
        dcount = [0] * NDMA_SEMS
        seq = {e: 0 for e in self.ENGS}
        waited = {e: {} for e in self.ENGS}
        last_w = {}
        readers = {}
        per_eng = {e: [] for e in self.ENGS}
        ndma = 0
        for (eng, fn, reads, writes, dma) in self.ops:
            psr = tuple(k for k in reads if isinstance(k, tuple) and k[0] == "ps")
            if psr:
                reads = tuple(k for k in reads if k not in psr)
                writes = tuple(writes) + tuple(k for k in psr if k not in writes)
            deps = {}

            def need(tok):
                s, v = tok
                if deps.get(s, 0) < v:
                    deps[s] = v
            for k in reads:
                if k in last_w:
                    if not (last_w[k][0] == eng == "tensor"):
                        need(last_w[k][1])
            for k in writes:
                if k in last_w and not (last_w[k][0] == eng == "tensor"):
                    need(last_w[k][1])
                for re, tok in readers.get(k, {}).items():
                    if re == eng == "tensor":
                        continue
                    need(tok)
            if dma:
                si = ndma % NDMA_SEMS
                ndma += 1
                if dcount[si] > 0:
                    need((("d", si), 16 * dcount[si]))
                dcount[si] += 1
                tok = (("d", si), 16 * dcount[si])
                inc = 16
            else:
                seq[eng] += 1
                tok = (("e", eng), seq[eng])
                inc = 1
            waits = []
            for s, v in deps.items():
                if waited[eng].get(s, 0) < v:
                    waited[eng][s] = v
                    waits.append((s, v))
            per_eng[eng].append((waits, fn, tok[0], inc))
            for k in reads:
                readers.setdefault(k, {})[eng if not dma else ("dma", ndma)] = tok
            for k in writes:
                last_w[k] = (eng if not dma else "dma", tok)
                readers[k] = {}
        final = [(("d", i), 16 * dcount[i]) for i in range(NDMA_SEMS) if dcount[i] > 0]

        def semof(s):
            return dsems[s[1]] if s[0] == "d" else sems[s[1]]

        block = stack.enter_context(nc.Block())

        def make(engname):
            def body(e):
                for waits, fn, s, inc in per_eng[engname]:
                    for ws, wv in waits:
                        e.wait_ge(semof(ws), wv)
                    ins = fn(e)
                    ins.then_inc(semof(s), inc)
                if engname == "sync":
                    for ws, wv in final:
                        e.wait_ge(semof(ws), wv)
            return body

        for engname in self.ENGS:
            if per_eng[engname] or engname == "sync":
                getattr(block, engname)(make(engname))
        self.stats = {e: len(per_eng[e]) for e in self.ENGS}


def _pair_cols(t):
    return np.concatenate([np.arange(t * 64, t * 64 + 64), np.arange((4 + t) * 64, (4 + t) * 64 + 64)])


def _swap(idx):
    return idx ^ 1


def _piece_list(nl):
    pieces = []

    def mod(l):
        for m in range(24):
            pieces.append(("mod", l, np.arange(m * 128, m * 128 + 128)))
    mod(0)
    for l in range(nl):
        jl = l // 2
        if l % 2 == 0:
            for base in (0, 1280):
                kb, vb, gb = base + 512, base + 640, base + 768
                kc = np.arange(128)
                pieces.append(("ein", jl, kb + kc))
                pieces.append(("ein", jl, kb + _swap(kc)))
                pieces.append(("ein", jl, vb + kc))
                for t in range(4):
                    pc = _pair_cols(t)
                    pieces.append(("ein", jl, base + pc))
                    pieces.append(("ein", jl, base + _swap(pc)))
                    pieces.append(("ein", jl, gb + pc))
                for t in range(4):
                    pieces.append(("eout", jl, (0 if base == 0 else 512) + _pair_cols(t)))
                if base == 0 and l + 1 < nl:
                    mod(l + 1)
        else:
            for c in range(4):
                pieces.append(("oin", jl, 0 + c * 128 + np.arange(128)))
                pieces.append(("oin", jl, 512 + c * 128 + np.arange(128)))
            for c in range(4):
                pieces.append(("oin", jl, 1024 + c * 128 + np.arange(128)))
            for c in range(4):
                pieces.append(("oout", jl, c * 128 + np.arange(128)))
            if l + 1 < nl:
                mod(l + 1)
            for t in range(4):
                pieces.append(("oin", jl, 2048 + t * 128 + np.arange(128)))
                pieces.append(("oin", jl, 2560 + t * 128 + np.arange(128)))
                pieces.append(("oin", jl, 1536 + t * 128 + np.arange(128)))
                pieces.append(("oin", jl, 3072 + t * 128 + np.arange(128)))
            for t in range(4):
                pieces.append(("oout", jl, 512 + t * 128 + np.arange(128)))
    return pieces


def _host_prep(inp, nl):
    f32 = np.float32
    pieces = _piece_list(nl)
    W = np.empty((len(pieces), 128, 1024), f32)
    for i, (kind, l, idx) in enumerate(pieces):
        if kind == "mod":
            w = inp["mod_w"][l][:, idx]
            W[i] = w.reshape(8, 128, 128).transpose(1, 0, 2).reshape(128, 1024)
        elif kind == "ein":
            w = inp["ev_w_in"][l][:, idx]
            W[i] = w.reshape(8, 128, 128).transpose(1, 0, 2).reshape(128, 1024)
        elif kind == "oin":
            w = inp["od_w_in"][l][:, idx]
            W[i] = w.reshape(8, 128, 128).transpose(1, 0, 2).reshape(128, 1024)
        elif kind == "eout":
            W[i] = inp["ev_w_out"][l][idx, :]
        else:
            W[i] = inp["od_w_out"][l][idx, :]
    cst = np.zeros((128, 128 * 4 + 256), f32)
    cst[:, 0:128] = np.eye(128, dtype=f32)
    sh = np.zeros((128, 128), f32)
    sh[np.arange(64), 64 + np.arange(64)] = 1.0
    cst[:, 128:256] = sh
    bo = np.zeros((128, 128), f32)
    bo[:64, :64] = 1.0 / 64
    bo[64:, 64:] = 1.0 / 64
    cst[:, 256:384] = bo
    cst[:, 384:512] = 1.0
    jj = np.arange(128)[:, None]
    ii = np.arange(128)[None, :]
    cst[:, 512:640] = (jj >= ii).astype(f32)
    cst[:, 640:768] = (jj <= ii).astype(f32)
    t = np.arange(NLAT)
    row = (t // 64).astype(f32)
    col = (t % 64).astype(f32)
    half = 32
    freqs = (f32(10000.0) ** (-np.arange(0, half, 2, dtype=f32) / f32(half))).astype(f32)
    ang = np.concatenate([row[:, None] * freqs, col[:, None] * freqs], axis=-1).astype(f32)
    cos, sin = np.cos(ang).astype(f32), np.sin(ang).astype(f32)
    d = np.arange(128) % 64
    C = cos[:, d // 2].T
    S = sin[:, d // 2].T * np.where(d % 2 == 0, -1.0, 1.0)[:, None].astype(f32)
    rope = np.empty((4, 128, 2, 512), f32)
    for c in range(4):
        rope[c, :, 0, :] = C[:, c * 512:(c + 1) * 512]
        rope[c, :, 1, :] = S[:, c * 512:(c + 1) * 512]
    n_odd = max(1, nl // 2)
    tg = np.full((n_odd, 4, 3, 128, 1024), -1e30, f32)
    variants = [(dl, False) for dl in range(-3, 4)] + [(-2, True), (-1, False), (0, False), (1, False), (2, True)]
    kk = np.arange(128)
    rkl, ck = kk // 64, kk % 64
    rl, cq = kk // 64, kk % 64
    cs = np.clip(cq - 8, 0, 48)
    colok = (ck[:, None] >= cs[None, :]) & (ck[:, None] < cs[None, :] + 16)
    dc = np.clip(ck[:, None] - cq[None, :] + 15, 0, 30)
    for jl in range(nl // 2):
        rpb = inp["d_rpb"][jl]
        for pr in range(4):
            for s in range(2):
                h = 2 * pr + s
                for vi, (dl, msk) in enumerate(variants):
                    dr = 2 * dl + rkl[:, None] - rl[None, :]
                    ok = colok & (np.abs(dr) <= 7)
                    if msk:
                        ok = ok & (dr >= -4) & (dr <= 3)
                    g = rpb[h][np.clip(dr + 7, 0, 14), dc]
                    blk = np.where(ok, g, f32(-1e30)).astype(f32)
                    q = s * 12 + vi
                    tg[jl, pr, q // 8, :, (q % 8) * 128:(q % 8) * 128 + 128] = blk
    def fm(v):
        return np.asarray(v, f32).reshape(8, 128).T
    pars = []
    for b in range(8):
        cols = [fm(inp["c"][b]), fm(inp["c_ctx"])]
        for l in range(4):
            cols.append(fm(inp["norm_g"][l]))
            cols.append(np.asarray(inp["mod_b"][l], f32).reshape(24, 128).T)
        for jl in range(2):
            for nm in ("a_q_gain", "a_k_gain"):
                gq = np.asarray(inp[nm][jl], f32)
                cols.append(gq[d][:, None])
                cols.append(gq[d ^ 1][:, None])
            cols.append(np.broadcast_to(np.asarray(inp["b_sink"][jl], f32)[None, :], (128, 8)))
        for jl in range(2):
            dw = np.asarray(inp["c_dw_w"][jl], f32)
            cols.append(dw.reshape(31, 4, 128).transpose(2, 1, 0).reshape(128, 124))
            for nm in ("c_dw_b", "c_ln_g", "c_ln_b"):
                cols.append(np.asarray(inp[nm][jl], f32).reshape(4, 128).T)
        cols.append(fm(inp["final_g"]))
        pars.append(np.ascontiguousarray(np.concatenate(cols, axis=1)))
    xs = [np.ascontiguousarray(np.concatenate([inp["x"][b], inp["ctx"][b]], axis=0)) for b in range(8)]
    return dict(W=W, cst=cst, rope=rope, tg=tg, pars=pars, xs=xs, npar=pars[0].shape[1])


def _par_off():
    o = {}
    p = 0
    o["c"] = p; p += 8
    o["cctx"] = p; p += 8
    for l in range(4):
        o["ng", l] = p; p += 8
        o["mb", l] = p; p += 24
    for jl in range(2):
        o["qg", jl] = p; p += 1
        o["qgs", jl] = p; p += 1
        o["kg", jl] = p; p += 1
        o["kgs", jl] = p; p += 1
        o["sink", jl] = p; p += 8
    for jl in range(2):
        o["dww", jl] = p; p += 124
        o["dwb", jl] = p; p += 4
        o["lng", jl] = p; p += 4
        o["lnb", jl] = p; p += 4
    o["fg"] = p; p += 8
    o["n"] = p
    return o


def build(nl=4):
    nc = bass.Bass("TRN2", target_bir_lowering=False)
    PO = _par_off()
    npieces = len(_piece_list(nl))
    n_odd = max(1, nl // 2)
    xd = nc.dram_tensor("x", [NT, 1024], F32, kind="ExternalInput").ap()
    pard = nc.dram_tensor("par", [128, PO["n"]], F32, kind="ExternalInput").ap()
    wd = nc.dram_tensor("w", [npieces, 128, 1024], F32, kind="ExternalInput").ap()
    cstd = nc.dram_tensor("cst", [128, 768], F32, kind="ExternalInput").ap()
    roped = nc.dram_tensor("rope", [4, 128, 1024], F32, kind="ExternalInput").ap()
    tgd = nc.dram_tensor("tg", [n_odd, 4, 3, 128, 1024], F32, kind="ExternalInput").ap()
    yd = nc.dram_tensor("y", [NLAT, 1024], F32, kind="ExternalOutput").ap()

    st = ExitStack()
    with st:
        def SB(name, shape, dt):
            return st.enter_context(nc.sbuf_tensor(name, shape, dt))
        XT = SB("XT", [128, 8, NT], F32)
        HT = SB("HT", [128, 8, NT], BF16)
        QM = SB("QM", [128, 4, NT], BF16)
        KG = SB("KG", [128, 2, NT], BF16)
        VA = SB("VA", [128, 18, 2, 65], BF16)
        PT = SB("PT", [128, 4, 512], BF16)
        STG = SB("STG", [128, 2, 1024], F32)
        WB = SB("WB", [128, 4, 1024], BF16)
        WO = SB("WO", [128, 4, 1024], BF16)
        RCS = SB("RCS", [128, 2, 1024], F32)
        ED = SB("ED", [128, 3968], BF16)
        CB = SB("CB", [128, 768], BF16)
        ONS = SB("ONS", [128, 2, 128], BF16)
        PAR = SB("PAR", [128, PO["n"]], F32)
        MODW = SB("MODW", [128, 24, 2], F32)
        MCO = SB("MCO", [128, 3, 8, 2], F32)
        CS = SB("CS", [128, 8, 2], BF16)
        ESK = SB("ESK", [128, 8], F32)
        TF = [SB("TF%d" % i, [128, 512], F32) for i in range(4)]
        TB = [SB("TB%d" % i, [128, 512], BF16) for i in range(2)]
        RW = SB("RW", [128, 512], F32)
        RH = SB("RH", [128, 512], BF16)
        RL = SB("RL", [128, 512], BF16)
        OS = SB("OS", [64, 512], F32)
        ON = [SB("ON%d" % i, [64, 512], BF16) for i in range(2)]
        PS = [st.enter_context(nc.psum_tensor("PS%d" % i, [128, 512], F32)) for i in range(8)]

        KTt = KG[:, 0, :]
        GTt = KG[:, 1, :]
        UP = KG[:].rearrange("p a b -> p (a b)")
        VA36 = VA[:].rearrange("p t g d -> p (t g) d")
        IDENT = CB[:, 0:128]
        SHIFT = CB[:, 128:256]
        BONES = CB[:, 256:384]
        ONES = CB[:, 384:512]
        TRIL = CB[:, 512:640]
        TRIU = CB[:, 640:768]
        DG = ED[:, 0:3968].rearrange("p (k m) -> p k m", k=31)
        ET = ED[:, 0:3072].rearrange("p (s i q) -> p s i q", s=2, i=12)

        P = Prog(nc)
        rr = {"stg": 0, "wb": 0, "pt": 0, "sb": 0, "rcs": 0, "pj": 0, "piece": 0}

        def ACT(out, in_, func, reads, writes, **kw):
            P.act(lambda e: e.activation(out, in_, func, **kw), reads, writes)

        def TT(out, a, b, op, reads, writes, eng="vector"):
            P.add(eng, lambda e: e.tensor_tensor(out, a, b, op), reads, writes)

        def STT(out, in0, scalar, in1, op0, op1, reads, writes):
            P.dve(lambda e: e.scalar_tensor_tensor(out, in0, scalar, in1, op0, op1), reads, writes)

        def TS(out, in0, s1, s2, op0, op1, reads, writes):
            if op1 is None:
                P.dve(lambda e: e.tensor_scalar(out, in0, s1, None, op0), reads, writes)
            else:
                P.dve(lambda e: e.tensor_scalar(out, in0, s1, s2, op0, op1), reads, writes)

        def CPY(out, in_, reads, writes, eng="vector"):
            P.add(eng, lambda e: e.tensor_copy(out, in_), reads, writes)

        def RCP(out, in_, reads, writes, bias=0.0):
            ACT(out, in_, AF.Ln, reads, writes, bias=bias)
            ACT(out, out, AF.Exp, writes, writes, scale=-1.0)

        def MM(lst, reads, writes):
            def f(e):
                r = None
                for (o, l, rh, s0, s1) in lst:
                    r = e.matmul(o, l, rh, start=s0, stop=s1)
                return r
            P.pe(f, reads, writes)

        def tiles(c0, n):
            return list(range(c0 // 128, (c0 + n) // 128))

        def next_piece():
            i = rr["piece"]; rr["piece"] += 1
            s = rr["stg"] % 2; rr["stg"] += 1
            w = rr["wb"] % 4; rr["wb"] += 1
            P.dma(lambda e: e.dma_start(out=STG[:, s, :], in_=wd[i]), writes=[("STG", s)])
            CPY(WB[:, w, :], STG[:, s, :], [("STG", s)], [("WB", w)], eng="gpsimd")
            return w

        def next_wo(j):
            i = rr["piece"]; rr["piece"] += 1
            s = rr["stg"] % 2; rr["stg"] += 1
            P.dma(lambda e: e.dma_start(out=STG[:, s, :], in_=wd[i]), writes=[("STG", s)])
            CPY(WO[:, j, :], STG[:, s, :], [("STG", s)], [("WO", j)], eng="gpsimd")

        def proj(w, c0, n, bank):
            wv = WB[:, w, :].rearrange("p (k m) -> p k m", k=8)
            MM([(PS[bank][:, 0:n], wv[:, kc, :], HT[:, kc, c0:c0 + n], kc == 0, kc == 7) for kc in range(8)],
               [("WB", w), ("H", c0)], [("ps", bank)])

        def sigm_recip(dst, src_ps, n, rkeys):
            ACT(dst[:, 0:n], src_ps, AF.Exp, rkeys, [("t", id(dst))], scale=-1.0)
            RCP(dst[:, 0:n], dst[:, 0:n], [("t", id(dst))], [("t", id(dst))], bias=1.0)

        P.dma(lambda e: e.dma_start(out=PAR[:], in_=pard), writes=["PAR"])
        P.dma(lambda e: e.dma_start(out=STG[:, 0, 0:768], in_=cstd), writes=[("STG", 0)])
        CPY(CB[:], STG[:, 0, 0:768], [("STG", 0)], ["CB"])
        rr["stg"] = 1
        P.pool(lambda e: e.memset(ONS[:, 0, :], 1.0 / 1024), writes=["ONS"])
        P.pool(lambda e: e.memset(ONS[:, 1, :], 1.0 / 512), writes=["ONS"])
        P.pool(lambda e: e.memset(VA[:], 1.0), writes=[("V", t) for t in range(18)])
        for q, key in enumerate(("c", "cctx")):
            src = PAR[:, PO[key]:PO[key] + 8]
            ACT(TF[0][:, 0:8], src, AF.Exp, ["PAR"], [("t", id(TF[0]))], scale=-1.0)
            RCP(TF[0][:, 0:8], TF[0][:, 0:8], [("t", id(TF[0]))], [("t", id(TF[0]))], bias=1.0)
            TT(CS[:, :, q], src, TF[0][:, 0:8], ALU.mult, [("t", id(TF[0])), "PAR"], ["CS"])
        for t in range(18):
            s = rr["stg"] % 2; rr["stg"] += 1
            P.dma(lambda e, t=t, s=s: e.dma_start(out=STG[:, s, :], in_=xd[t * 128:(t + 1) * 128, :]), writes=[("STG", s)])
            hi, lo = WB[:, 2 * (t % 2), :], WB[:, 2 * (t % 2) + 1, :]
            kh, kl = ("WB", 2 * (t % 2)), ("WB", 2 * (t % 2) + 1)
            CPY(hi, STG[:, s, :], [("STG", s)], [kh])
            TT(lo, STG[:, s, :], hi, ALU.subtract, [("STG", s), kh], [kl])
            for fg in range(2):
                bank = rr["pj"] % 2; rr["pj"] += 1
                lst = []
                for i in range(4):
                    f = fg * 4 + i
                    lst.append((PS[bank][:, i * 128:(i + 1) * 128], hi[:, f * 128:(f + 1) * 128], IDENT, True, False))
                    lst.append((PS[bank][:, i * 128:(i + 1) * 128], lo[:, f * 128:(f + 1) * 128], IDENT, False, True))
                MM(lst, [kh, kl, "CB"], [("ps", bank)])
                cc = min(t // 4, 4)
                P.act(lambda e, bank=bank, fg=fg, t=t: e.copy(XT[:, fg * 4:fg * 4 + 4, t * 128:(t + 1) * 128],
                                                                PS[bank][:, :].rearrange("p (a b) -> p a b", a=4)),
                      [("ps", bank)], [("X", kc, CHUNKS[cc][0]) for kc in range(fg * 4, fg * 4 + 4)])

        def modulation(l):
            for m in range(24):
                w = next_piece()
                wv = WB[:, w, :].rearrange("p (k m) -> p k m", k=8)
                MM([(PS[6][:, 0:2], wv[:, kc, :], CS[:, kc, :], kc == 0, kc == 7) for kc in range(8)],
                   [("WB", w), "CS"], [("ps", 6)])
                TS(MODW[:, m, :], PS[6][:, 0:2], PAR[:, PO["mb", l] + m:PO["mb", l] + m + 1], None, ALU.add, None,
                   [("ps", 6), "PAR"], ["MODW"])

        def mod_finish(l):
            ng = PAR[:, PO["ng", l]:PO["ng", l] + 8]
            for q in range(2):
                TS(MCO[:, 0, :, q], MODW[:, 8:16, q], 1.0, None, ALU.add, None, ["MODW"], ["MCO"])
                TT(MCO[:, 0, :, q], MCO[:, 0, :, q], ng, ALU.mult, ["MCO", "PAR"], ["MCO"])
                CPY(MCO[:, 1, :, q], MODW[:, 0:8, q], ["MODW"], ["MCO"])
                CPY(MCO[:, 2, :, q], MODW[:, 16:24, q], ["MODW"], ["MCO"])

        def norm_to_h():
            for (c0, n) in CHUNKS:
                q = 0 if c0 < NLAT else 1
                for kc in range(8):
                    sq = TB[kc % 2]
                    ACT(sq[:, 0:n], XT[:, kc, c0:c0 + n], AF.Square, [("X", kc, c0)], [("t", id(sq))])
                    MM([(PS[6][:, 0:n], ONS[:, 0, :], sq[:, 0:n], kc == 0, kc == 7)], [("t", id(sq)), "ONS"], [("ps", 6)])
                ACT(TF[0][:, 0:n], PS[6][:, 0:n], AF.Ln, [("ps", 6)], [("t", id(TF[0]))], bias=EPS)
                ACT(TF[0][:, 0:n], TF[0][:, 0:n], AF.Exp, [("t", id(TF[0]))], [("t", id(TF[0]))], scale=-0.5)
                for kc in range(8):
                    tmp = TF[1 + kc % 2]
                    STT(tmp[:, 0:n], XT[:, kc, c0:c0 + n], MCO[:, 0, kc, q:q + 1], TF[0][:, 0:n], ALU.mult, ALU.mult,
                        [("X", kc, c0), "MCO", ("t", id(TF[0]))], [("t", id(tmp))])
                    ACT(HT[:, kc, c0:c0 + n], tmp[:, 0:n], AF.Identity, [("t", id(tmp)), "MCO"], [("H", c0)],
                        bias=MCO[:, 1, kc, q:q + 1], scale=1.0)

        def qk_tile(dst_fn, dkeys_fn, use_norm, gain_ap, gain_sw_ap, chunks):
            w = next_piece()
            ws = next_piece()
            for (c0, n) in chunks:
                lat = c0 < NLAT
                proj(w, c0, n, 0)
                if lat:
                    proj(ws, c0, n, 1)
                if use_norm:
                    ACT(TB[0][:, 0:n], PS[0][:, 0:n], AF.Square, [("ps", 0)], [("t", id(TB[0]))])
                    MM([(PS[6][:, 0:n], BONES, TB[0][:, 0:n], True, True)], [("t", id(TB[0])), "CB"], [("ps", 6)])
                    ACT(TF[0][:, 0:n], PS[6][:, 0:n], AF.Ln, [("ps", 6)], [("t", id(TF[0]))], bias=EPS)
                    ACT(TF[0][:, 0:n], TF[0][:, 0:n], AF.Exp, [("t", id(TF[0]))], [("t", id(TF[0]))], scale=-0.5)
                    STT(TF[1][:, 0:n], PS[0][:, 0:n], gain_ap, TF[0][:, 0:n], ALU.mult, ALU.mult,
                        [("ps", 0), "PAR", ("t", id(TF[0]))], [("t", id(TF[1]))])
                    if lat:
                        STT(TF[2][:, 0:n], PS[1][:, 0:n], gain_sw_ap, TF[0][:, 0:n], ALU.mult, ALU.mult,
                            [("ps", 1), "PAR", ("t", id(TF[0]))], [("t", id(TF[2]))])
                    a_src, b_src = TF[1][:, 0:n], TF[2][:, 0:n]
                    akeys, bkeys = [("t", id(TF[1]))], [("t", id(TF[2]))]
                else:
                    a_src, b_src = PS[0][:, 0:n], PS[1][:, 0:n]
                    akeys, bkeys = [("ps", 0)], [("ps", 1)]
                if lat:
                    rs = rr["rcs"] % 2; rr["rcs"] += 1
                    ci = c0 // 512
                    P.dma(lambda e, rs=rs, ci=ci: e.dma_start(out=RCS[:, rs, :], in_=roped[ci]), writes=[("RCS", rs)])
                    TT(TF[1][:, 0:n], a_src, RCS[:, rs, 0:n], ALU.mult, akeys + [("RCS", rs)], [("t", id(TF[1]))])
                    TT(TF[2][:, 0:n], b_src, RCS[:, rs, 512:512 + n], ALU.mult, bkeys + [("RCS", rs)], [("t", id(TF[2]))])
                    TT(dst_fn(c0, n), TF[1][:, 0:n], TF[2][:, 0:n], ALU.add,
                       [("t", id(TF[1])), ("t", id(TF[2]))], dkeys_fn(c0, n))
                else:
                    if use_norm:
                        CPY(dst_fn(c0, n), a_src, akeys, dkeys_fn(c0, n))
                    else:
                        P.act(lambda e, c0=c0, n=n: e.copy(dst_fn(c0, n), PS[0][:, 0:n]), [("ps", 0)], dkeys_fn(c0, n))

        def plain_tile(dst_fn, dkeys_fn, chunks):
            w = next_piece()
            for (c0, n) in chunks:
                bank = rr["pj"] % 2; rr["pj"] += 1
                proj(w, c0, n, bank)
                P.act(lambda e, c0=c0, n=n, bank=bank: e.copy(dst_fn(c0, n), PS[bank][:, 0:n]), [("ps", bank)], dkeys_fn(c0, n))

        def v_tile():
            w = next_piece()
            wv = WB[:, w, :].rearrange("p (k m) -> p k m", k=8)
            for g4 in range(5):
                tl = list(range(g4 * 4, min(g4 * 4 + 4, 18)))
                bank = rr["pj"] % 2; rr["pj"] += 1
                lst = []
                for i, t in enumerate(tl):
                    for kc in range(8):
                        lst.append((PS[bank][:, i * 128:(i + 1) * 128], HT[:, kc, t * 128:(t + 1) * 128], wv[:, kc, :], kc == 0, kc == 7))
                MM(lst, [("WB", w)] + [("H", CHUNKS[min(t // 4, 4)][0]) for t in tl], [("ps", bank)])
                nt_ = len(tl)
                P.act(lambda e, bank=bank, tl=tl, nt_=nt_: e.copy(
                    VA36[:, 2 * tl[0]:2 * tl[0] + 2 * nt_, 0:64],
                    PS[bank][:, 0:nt_ * 128].rearrange("p (a b) -> p a b", b=64)),
                    [("ps", bank)], [("V", t) for t in tl])

        def gate_tile(chunks):
            w = next_piece()
            for (c0, n) in chunks:
                bank = rr["pj"] % 2; rr["pj"] += 1
                proj(w, c0, n, bank)
                sigm_recip(TF[3], PS[bank][:, 0:n], n, [("ps", bank)])
                TT(GTt[:, c0:c0 + n], PS[bank][:, 0:n], TF[3][:, 0:n], ALU.mult,
                   [("ps", bank), ("t", id(TF[3]))], [("G", t) for t in tiles(c0, n)])

        def slot(nq, idx):
            if nq == 128:
                return idx * 128, [idx]
            return 0, list(range((nq + 127) // 128))

        def attend_pair(j, c0, nq, kts, g):
            G = 512 // nq
            qk = [("Q", j, t) for t in tiles(c0, nq)]
            nk = len(kts[0])
            groups = [(g0, min(g0 + G, nk)) for g0 in range(0, nk, G)]
            obank = [4 + 2 * g, 5 + 2 * g]
            ocol = [0, 0]
            okeys = [[("ps", obank[0])], [("ps", obank[1])]]
            sbank = {}

            def emitS(gi):
                a, b = groups[gi]
                for s in range(2):
                    sb = rr["sb"] % 4; rr["sb"] += 1
                    sbank[gi, s] = sb
                    grp = kts[s][a:b]
                    MM([(PS[sb][:, i * nq:(i + 1) * nq], KTt[64 * s:64 * s + 64, kt * 128:(kt + 1) * 128],
                         QM[64 * s:64 * s + 64, j, c0:c0 + nq], True, True) for i, (kt, _m) in enumerate(grp)],
                       [("K", kt) for kt, _m in grp] + qk, [("ps", sb)])
            emitS(0)
            for gi, (a, b) in enumerate(groups):
                if gi + 1 < len(groups):
                    emitS(gi + 1)
                pts = []
                for s in range(2):
                    sb = sbank[gi, s]
                    pt = rr["pt"] % 4; rr["pt"] += 1
                    pts.append(pt)
                    wdt = (b - a) * nq
                    ACT(PT[:, pt, 0:wdt], PS[sb][:, 0:wdt], AF.Exp, [("ps", sb)], [("PT", pt)], scale=0.125)
                for s in range(2):
                    pt = pts[s]
                    for i, (kt, m) in enumerate(kts[s][a:b]):
                        if m is not None:
                            TT(PT[:, pt, i * nq:(i + 1) * nq], PT[:, pt, i * nq:(i + 1) * nq], m[0], ALU.mult,
                               [("PT", pt), m[1]], [("PT", pt)])
                for s in range(2):
                    pt = pts[s]
                    grp = kts[s][a:b]
                    MM([(PS[obank[s]][0:65, ocol[s]:ocol[s] + nq], VA[:, kt, s, 0:65], PT[:, pt, i * nq:(i + 1) * nq],
                         a + i == 0, a + i == nk - 1) for i, (kt, _m) in enumerate(grp)],
                       [("PT", pt)] + [("V", kt) for kt, _m in grp], okeys[s])

        def attn_epilogue(j, c0, nq, sink_heads, g):
            obank = [4 + 2 * g, 5 + 2 * g]
            m0, mq = slot(nq, g)
            for s in range(2):
                o0 = 0
                okeys = [("ps", obank[s])]
                bb = rr["sb"] % 4; rr["sb"] += 1
                r0, rq = slot(nq, 2 * g + s)
                rwk = [("RW", q) for q in rq]
                rhk = [("RH", q) for q in rq]
                rlk = [("RL", q) for q in rq]
                b6k = [("ps", bb)]
                osk = [("OS", q) for q in rq]
                onk = [("ON", s, q) for q in mq]
                Osum = PS[obank[s]][64:65, o0:o0 + nq]
                bias = ESK[64:65, sink_heads[s]:sink_heads[s] + 1] if sink_heads is not None else 0.0
                RCP(RW[64:65, r0:r0 + nq], Osum, okeys + (["ESK"] if sink_heads is not None else []), rwk, bias=bias)
                CPY(RH[64:65, r0:r0 + nq], RW[64:65, r0:r0 + nq], rwk, rhk)
                TT(RL[64:65, r0:r0 + nq], RW[64:65, r0:r0 + nq], RH[64:65, r0:r0 + nq], ALU.subtract, rwk + rhk, rlk)
                MM([(PS[bb][0:64, 0:nq], ONES[64:65, 0:64], RH[64:65, r0:r0 + nq], True, False),
                    (PS[bb][0:64, 0:nq], ONES[64:65, 0:64], RL[64:65, r0:r0 + nq], False, True)], rhk + rlk + ["CB"], b6k)
                P.act(lambda e, s=s, o0=o0, r0=r0: e.copy(OS[0:64, r0:r0 + nq], PS[obank[s]][0:64, o0:o0 + nq]), okeys, osk)
                TT(ON[s][:, m0:m0 + nq], OS[0:64, r0:r0 + nq], PS[bb][0:64, 0:nq], ALU.mult, osk + b6k, onk)
            mb = rr["sb"] % 4; rr["sb"] += 1
            m7k = [("ps", mb)]
            MM([(PS[mb][:, 0:nq], IDENT[0:64, :], ON[0][:, m0:m0 + nq], True, False),
                (PS[mb][:, 0:nq], SHIFT[0:64, :], ON[1][:, m0:m0 + nq], False, True)],
               [("ON", 0, q) for q in mq] + [("ON", 1, q) for q in mq] + ["CB"], m7k)
            tl = tiles(c0, nq)
            TT(QM[:, j, c0:c0 + nq], PS[mb][:, 0:nq], GTt[:, c0:c0 + nq], ALU.mult,
               m7k + [("G", t) for t in tl], [("Q", j, t) for t in tl])

        def run_blocks(j, blocks, sink_heads):
            pending = None
            for bi, (c0, nq, kts) in enumerate(blocks):
                g = bi % 2 if nq == 128 else 0
                attend_pair(j, c0, nq, kts, g)
                if pending is not None:
                    pending()
                    pending = None
                if nq == 128:
                    pending = (lambda c0=c0, nq=nq, g=g: attn_epilogue(j, c0, nq, sink_heads, g))
                else:
                    attn_epilogue(j, c0, nq, sink_heads, g)
            if pending is not None:
                pending()

        def out_proj(upd_ctx):
            for (c0, n) in CHUNKS:
                if c0 >= NLAT and not upd_ctx:
                    continue
                q = 0 if c0 < NLAT else 1
                for m in range(8):
                    bank = rr["pj"] % 2; rr["pj"] += 1
                    MM([(PS[bank][:, 0:n], WO[:, j, m * 128:(m + 1) * 128], QM[:, j, c0:c0 + n], j == 0, j == 3) for j in range(4)],
                       [("WO", j) for j in range(4)] + [("Q", j, t) for j in range(4) for t in tiles(c0, n)], [("ps", bank)])
                    STT(XT[:, m, c0:c0 + n], PS[bank][:, 0:n], MCO[:, 2, m, q:q + 1], XT[:, m, c0:c0 + n], ALU.mult, ALU.add,
                        [("ps", bank), "MCO", ("X", m, c0)], [("X", m, c0)])

        QCH = lambda upd: [ch for ch in CHUNKS if ch[0] < NLAT or upd]
        qdst = lambda j: (lambda c0, n: QM[:, j, c0:c0 + n])
        qkeys = lambda j: (lambda c0, n: [("Q", j, t) for t in tiles(c0, n)])
        kdst = lambda c0, n: KTt[:, c0:c0 + n]
        kkeys = lambda c0, n: [("K", t) for t in tiles(c0, n)]

        def even_layer(l, upd):
            jl = l // 2
            ACT(ESK[:], PAR[:, PO["sink", jl]:PO["sink", jl] + 8], AF.Exp, ["PAR"], ["ESK"])
            for mixer in range(2):
                isA = mixer == 0
                kg = PAR[:, PO["kg", jl]:PO["kg", jl] + 1]
                kgs = PAR[:, PO["kgs", jl]:PO["kgs", jl] + 1]
                qg = PAR[:, PO["qg", jl]:PO["qg", jl] + 1]
                qgs = PAR[:, PO["qgs", jl]:PO["qgs", jl] + 1]
                qk_tile(kdst, kkeys, isA, kg, kgs, CHUNKS)
                v_tile()
                for j in range(4):
                    qk_tile(qdst(j), qkeys(j), isA, qg, qgs, QCH(upd))
                    gate_tile(QCH(upd))
                    if isA:
                        blocks = []
                        for (c0, n) in QCH(upd):
                            kl = [(kt, None) for kt in ([16, 17] + list(range(16)) if c0 < NLAT else [16, 17])]
                            blocks.append((c0, n, [kl, kl]))
                        run_blocks(j, blocks, None)
                    else:
                        blocks = []
                        for qb in list(range(16)) + ([16, 17] if upd else []):
                            kl = [(16, None), (17, None)]
                            if qb < 16:
                                if qb > 0:
                                    kl.append((qb - 1, (TRIL, "CB")))
                                kl.append((qb, None))
                                if qb < 15:
                                    kl.append((qb + 1, (TRIU, "CB")))
                            blocks.append((qb * 128, 128, [kl, kl]))
                        run_blocks(j, blocks, (j, 4 + j))
                for j in range(4):
                    next_wo(j)
                out_proj(upd)
                if mixer == 0 and l + 1 < nl:
                    modulation(l + 1)

        def odd_layer(l, upd):
            jl = l // 2
            chunks = QCH(upd)
            seqs = [(0, 2048, 0)] + ([(2048, 256, 2078)] if upd else [])
            P.pool(lambda e: e.memset(UP[:, 0:2364], 0.0), writes=[("K", t) for t in range(18)] + [("G", t) for t in range(18)] + ["UP"])
            for c in range(4):
                wv_ = next_piece()
                wg_ = next_piece()
                for (c0, n) in chunks:
                    proj(wv_, c0, n, 0)
                    proj(wg_, c0, n, 1)
                    sigm_recip(TF[3], PS[1][:, 0:n], n, [("ps", 1)])
                    base = 15 + c0 if c0 < NLAT else 2078 + 15 + (c0 - NLAT)
                    TT(UP[:, base:base + n], PS[0][:, 0:n], TF[3][:, 0:n], ALU.mult, [("ps", 0), ("t", id(TF[3]))], ["UP"])
                for k in range(31):
                    col = PO["dww", jl] + c * 31 + k
                    TS(DG[:, k, :], IDENT, PAR[:, col:col + 1], None, ALU.mult, None, ["CB", "PAR"], ["DG"])
                for (c0, n) in chunks:
                    bank = rr["pj"] % 2; rr["pj"] += 1
                    base = c0 if c0 < NLAT else 2078 + (c0 - NLAT)
                    MM([(PS[bank][:, 0:n], DG[:, k, :], UP[:, base + k:base + k + n], k == 0, k == 30) for k in range(31)],
                       ["DG", "UP"], [("ps", bank)])
                    ACT(QM[:, c, c0:c0 + n], PS[bank][:, 0:n], AF.Identity, [("ps", bank), "PAR"],
                        [("Q", c, t) for t in tiles(c0, n)], bias=PAR[:, PO["dwb", jl] + c:PO["dwb", jl] + c + 1], scale=1.0)
            wgs = [next_piece() for _ in range(4)]
            for (c0, n) in chunks:
                tl = tiles(c0, n)
                for c in range(4):
                    MM([(PS[6][:, 0:n], ONS[:, 1, :], QM[:, c, c0:c0 + n], c == 0, c == 3)], [("Q", c, t) for t in tl] + ["ONS"], [("ps", 6)])
                for c in range(4):
                    sq = TB[c % 2]
                    ACT(sq[:, 0:n], QM[:, c, c0:c0 + n], AF.Square, [("Q", c, t) for t in tl], [("t", id(sq))])
                    MM([(PS[7][:, 0:n], ONS[:, 1, :], sq[:, 0:n], c == 0, c == 3)], [("t", id(sq)), "ONS"], [("ps", 7)])
                P.act(lambda e, n=n: e.copy(TF[0][:, 0:n], PS[6][:, 0:n]), [("ps", 6)], [("t", id(TF[0]))])
                TT(TF[1][:, 0:n], TF[0][:, 0:n], TF[0][:, 0:n], ALU.mult, [("t", id(TF[0]))], [("t", id(TF[1]))])
                TT(TF[1][:, 0:n], PS[7][:, 0:n], TF[1][:, 0:n], ALU.subtract, [("ps", 7), ("t", id(TF[1]))], [("t", id(TF[1]))])
                ACT(TF[1][:, 0:n], TF[1][:, 0:n], AF.Ln, [("t", id(TF[1]))], [("t", id(TF[1]))], bias=EPS)
                ACT(TF[1][:, 0:n], TF[1][:, 0:n], AF.Exp, [("t", id(TF[1]))], [("t", id(TF[1]))], scale=-0.5)
                for c in range(4):
                    lg = PAR[:, PO["lng", jl] + c:PO["lng", jl] + c + 1]
                    lb = PAR[:, PO["lnb", jl] + c:PO["lnb", jl] + c + 1]
                    qmk = [("Q", c, t) for t in tl]
                    TT(TF[2][:, 0:n], QM[:, c, c0:c0 + n], TF[0][:, 0:n], ALU.subtract, qmk + [("t", id(TF[0]))], [("t", id(TF[2]))])
                    TT(TF[2][:, 0:n], TF[2][:, 0:n], TF[1][:, 0:n], ALU.mult, [("t", id(TF[2])), ("t", id(TF[1]))], [("t", id(TF[2]))])
                    ACT(TF[2][:, 0:n], TF[2][:, 0:n], AF.Identity, [("t", id(TF[2])), "PAR"], [("t", id(TF[2]))], bias=lb, scale=lg)
                    sigm_recip(TF[3], TF[2][:, 0:n], n, [("t", id(TF[2]))])
                    TT(TF[2][:, 0:n], TF[2][:, 0:n], TF[3][:, 0:n], ALU.mult, [("t", id(TF[2])), ("t", id(TF[3]))], [("t", id(TF[2]))])
                    proj(wgs[c], c0, n, 0)
                    sigm_recip(TF[3], PS[0][:, 0:n], n, [("ps", 0)])
                    TT(TF[3][:, 0:n], PS[0][:, 0:n], TF[3][:, 0:n], ALU.mult, [("ps", 0), ("t", id(TF[3]))], [("t", id(TF[3]))])
                    TT(QM[:, c, c0:c0 + n], TF[2][:, 0:n], TF[3][:, 0:n], ALU.mult, [("t", id(TF[2])), ("t", id(TF[3]))], qmk)
            for j in range(4):
                next_wo(j)
            out_proj(upd)
            if l + 1 < nl:
                modulation(l + 1)
            for j in range(4):
                plain_tile(kdst, lambda c0, n: kkeys(c0, n) + ["UP"], CHUNKS)
                v_tile()
                plain_tile(qdst(j), qkeys(j), chunks)
                gate_tile(chunks)
                for pc in range(3):
                    s_ = rr["stg"] % 2; rr["stg"] += 1
                    P.dma(lambda e, s_=s_, pc=pc, j=j: e.dma_start(out=STG[:, s_, :], in_=tgd[jl, j, pc]), writes=[("STG", s_)])
                    ACT(ED[:, pc * 1024:(pc + 1) * 1024], STG[:, s_, :], AF.Exp, [("STG", s_)], ["DG"])
                blocks = []
                for p_ in list(range(16)) + ([16, 17] if upd else []):
                    kls = []
                    for s in range(2):
                        kl = [(16, None), (17, None)]
                        if p_ < 16:
                            if p_ == 0:
                                lt, i0 = [0, 1, 2, 3], 3
                            elif p_ == 1:
                                lt, i0 = [0, 1, 2, 3], 2
                            elif p_ == 14:
                                lt, i0 = [12, 13, 14, 15], 1
                            elif p_ == 15:
                                lt, i0 = [12, 13, 14, 15], 0
                            else:
                                lt, i0 = list(range(p_ - 2, p_ + 3)), 7
                            for i, kt in enumerate(lt):
                                kl.append((kt, (ET[:, s, i0 + i, :], "DG")))
                        kls.append(kl)
                    blocks.append((p_ * 128, 128, kls))
                run_blocks(j, blocks, None)
            for j in range(4):
                next_wo(j)
            out_proj(upd)

        modulation(0)
        for l in range(nl):
            upd = l < nl - 1
            mod_finish(l)
            norm_to_h()
            if l % 2 == 0:
                even_layer(l, upd)
            else:
                odd_layer(l, upd)
        fg = PAR[:, PO["fg"]:PO["fg"] + 8]
        for (c0, n) in CHUNKS[:4]:
            for kc in range(8):
                sq = TB[kc % 2]
                ACT(sq[:, 0:n], XT[:, kc, c0:c0 + n], AF.Square, [("X", kc, c0)], [("t", id(sq))])
                MM([(PS[6][:, 0:n], ONS[:, 0, :], sq[:, 0:n], kc == 0, kc == 7)], [("t", id(sq)), "ONS"], [("ps", 6)])
            ACT(TF[0][:, 0:n], PS[6][:, 0:n], AF.Ln, [("ps", 6)], [("t", id(TF[0]))], bias=EPS)
            ACT(TF[0][:, 0:n], TF[0][:, 0:n], AF.Exp, [("t", id(TF[0]))], [("t", id(TF[0]))], scale=-0.5)
            for kc in range(8):
                STT(TF[1][:, 0:n], XT[:, kc, c0:c0 + n], fg[:, kc:kc + 1], TF[0][:, 0:n], ALU.mult, ALU.mult,
                    [("X", kc, c0), "PAR", ("t", id(TF[0]))], [("t", id(TF[1]))])
                CPY(HT[:, kc, c0:c0 + n], TF[1][:, 0:n], [("t", id(TF[1]))], [("H", c0)])
                TT(QM[:, kc % 4, (kc // 4) * 512:(kc // 4) * 512 + n], TF[1][:, 0:n], HT[:, kc, c0:c0 + n], ALU.subtract,
                   [("t", id(TF[1])), ("H", c0)], [("LO", kc)])
            for ti in range(4):
                t = c0 // 128 + ti
                s = rr["stg"] % 2; rr["stg"] += 1
                for fgp in range(2):
                    bank = rr["pj"] % 2; rr["pj"] += 1
                    lst = []
                    for i in range(4):
                        kc = fgp * 4 + i
                        lo = QM[:, kc % 4, (kc // 4) * 512 + ti * 128:(kc // 4) * 512 + ti * 128 + 128]
                        lst.append((PS[bank][:, i * 128:(i + 1) * 128], HT[:, kc, t * 128:(t + 1) * 128], IDENT, True, False))
                        lst.append((PS[bank][:, i * 128:(i + 1) * 128], lo, IDENT, False, True))
                    MM(lst, [("H", c0), "CB"] + [("LO", kc) for kc in range(8)], [("ps", bank)])
                    P.act(lambda e, bank=bank, s=s, fgp=fgp: e.copy(STG[:, s, fgp * 512:(fgp + 1) * 512], PS[bank][:, :]),
                          [("ps", bank)], [("STG", s)])
                P.dma(lambda e, s=s, t=t: e.dma_start(out=yd[t * 128:(t + 1) * 128, :], in_=STG[:, s, :]), reads=[("STG", s)])
        assert rr["piece"] == npieces, (rr["piece"], npieces)
        P.emit(st)
    return nc, P


_CACHE = {}


def run(inputs, nl=4, cores=8):
    prep = _host_prep(inputs, nl)
    if nl not in _CACHE:
        _CACHE[nl] = build(nl)
    nc, _ = _CACHE[nl]
    in_maps = []
    for b in range(cores):
        in_maps.append({"x": prep["xs"][b], "par": prep["pars"][b], "w": prep["W"], "cst": prep["cst"][:, 0:768],
                        "rope": prep["rope"].reshape(4, 128, 1024), "tg": prep["tg"]})
    res = run_bass_kernel_spmd(nc, in_maps, core_ids=list(range(cores)))
    return np.stack([np.asarray(r["y"], dtype=np.float32) for r in res.results], axis=0)


def kernel(**inputs):
    inputs = {k: np.asarray(v) for k, v in inputs.items()}
    return run(inputs, 4, 8)
```

```python
import numpy as np
from contextlib import ExitStack
import concourse.bass as bass
import concourse.mybir as mybir
from concourse.bass_utils import run_bass_kernel_spmd

F32 = mybir.dt.float32
BF16 = mybir.dt.bfloat16
AF = mybir.ActivationFunctionType
ALU = mybir.AluOpType

NT = 2304
NLAT = 2048
CHUNKS = [(0, 512), (512, 512), (1024, 512), (1536, 512), (2048, 256)]
EPS = 1e-6
NDMA_SEMS = 24


class Prog:
    ENGS = ("tensor", "vector", "scalar", "gpsimd", "sync")

    def __init__(self, nc):
        self.nc = nc
        self.ops = []

    def add(self, eng, fn, reads=(), writes=(), dma=False):
        self.ops.append((eng, fn, tuple(reads), tuple(writes), dma))

    def pe(self, fn, reads=(), writes=()): self.add("tensor", fn, reads, writes)
    def dve(self, fn, reads=(), writes=()): self.add("vector", fn, reads, writes)
    def act(self, fn, reads=(), writes=()): self.add("scalar", fn, reads, writes)
    def pool(self, fn, reads=(), writes=()): self.add("gpsimd", fn, reads, writes)
    def dma(self, fn, reads=(), writes=()): self.add("sync", fn, reads, writes, dma=True)

    def emit(self, stack):
        nc = self.nc
        sems = {e: stack.enter_context(nc.semaphore("s_" + e)) for e in self.ENGS if e != "sync"}
        dsems = [stack.enter_context(nc.semaphore("d%d" % i)) for i in range(NDMA_SEMS)]
        dcount = [0] * NDMA_SEMS
        seq = {e: 0 for e in self.ENGS}
        waited = {e: {} for e in self.ENGS}
        last_w = {}
        readers = {}
        per_eng = {e: [] for e in self.ENGS}
        ndma = 0
        for (eng, fn, reads, writes, dma) in self.ops:
            psr = tuple(k for k in reads if isinstance(k, tuple) and k[0] == "ps")
            if psr:
                reads = tuple(k for k in reads if k not in psr)
                writes = tuple(writes) + tuple(k for k in psr if k not in writes)
            deps = {}

            def need(tok):
                s, v = tok
                if deps.get(s, 0) < v:
                    deps[s] = v
            for k in reads:
                if k in last_w:
                    if not (last_w[k][0] == eng == "tensor"):
                        need(last_w[k][1])
            for k in writes:
                if k in last_w and not (last_w[k][0] == eng == "tensor"):
                    need(last_w[k][1])
                for re, tok in readers.get(k, {}).items():
                    if re == eng == "tensor":
                        continue
                    need(tok)
            if dma:
                si = ndma % NDMA_SEMS
                ndma += 1
                if dcount[si] > 0:
                    need((("d", si), 16 * dcount[si]))
                dcount[si] += 1
                tok = (("d", si), 16 * dcount[si])
                inc = 16
            else:
                seq[eng] += 1
                tok = (("e", eng), seq[eng])
                inc = 1
            waits = []
            for s, v in deps.items():
                if waited[eng].get(s, 0) < v:
                    waited[eng][s] = v
                    waits.append((s, v))
            per_eng[eng].append((waits, fn, tok[0], inc))
            for k in reads:
                readers.setdefault(k, {})[eng if not dma else ("dma", ndma)] = tok
            for k in writes:
                last_w[k] = (eng if not dma else "dma", tok)
                readers[k] = {}
        final = [(("d", i), 16 * dcount[i]) for i in range(NDMA_SEMS) if dcount[i] > 0]

        def semof(s):
            return dsems[s[1]] if s[0] == "d" else sems[s[1]]

        block = stack.enter_context(nc.Block())

        def make(engname):
            def body(e):
                for waits, fn, s, inc in per_eng[engname]:
                    for ws, wv in waits:
                        e.wait_ge(semof(ws), wv)
                    ins = fn(e)
                    ins.then_inc(semof(s), inc)
                if engname == "sync":
                    for ws, wv in final:
                        e.wait_ge(semof(ws), wv)
            return body

        for engname in self.ENGS:
            if per_eng[engname] or engname == "sync":
                getattr(block, engname)(make(engname))
        self.stats = {e: len(per_eng[e]) for e in self.ENGS}


def _pair_cols(t):
    return np.concatenate([np.arange(t * 64, t * 64 + 64), np.arange((4 + t) * 64, (4 + t) * 64 + 64)])


def _swap(idx):
    return idx ^ 1


def _piece_list(nl):
    pieces = []

    def mod(l):
        for m in range(24):
            pieces.append(("mod", l, np.arange(m * 128, m * 128 + 128)))
    mod(0)
    for l in range(nl):
        jl = l // 2
        if l % 2 == 0:
            for base in (0, 1280):
                kb, vb, gb = base + 512, base + 640, base + 768
                kc = np.arange(128)
                pieces.append(("ein", jl, kb + kc))
                pieces.append(("ein", jl, kb + _swap(kc)))
                pieces.append(("ein", jl, vb + kc))
                if base == 0:
                    for t in range(4):
                        pc = _pair_cols(t)
                        pieces.append(("ein", jl, base + pc))
                        pieces.append(("ein", jl, base + _swap(pc)))
                        pieces.append(("ein", jl, gb + pc))
                else:
                    for t in range(4):
                        pc = _pair_cols(t)
                        pieces.append(("ein", jl, base + pc))
                        pieces.append(("ein", jl, base + _swap(pc)))
                    for t in range(4):
                        pieces.append(("ein", jl, gb + _pair_cols(t)))
                for t in range(4):
                    pieces.append(("eout", jl, (0 if base == 0 else 512) + _pair_cols(t)))
                if base == 0 and l + 1 < nl:
                    mod(l + 1)
        else:
            for c in range(4):
                pieces.append(("oin", jl, 0 + c * 128 + np.arange(128)))
                pieces.append(("oin", jl, 512 + c * 128 + np.arange(128)))
            for c in range(4):
                pieces.append(("oin", jl, 1024 + c * 128 + np.arange(128)))
            for c in range(4):
                pieces.append(("oout", jl, c * 128 + np.arange(128)))
            if l + 1 < nl:
                mod(l + 1)
            for t in range(4):
                pieces.append(("oin", jl, 2048 + t * 128 + np.arange(128)))
                pieces.append(("oin", jl, 2560 + t * 128 + np.arange(128)))
                pieces.append(("oin", jl, 1536 + t * 128 + np.arange(128)))
                pieces.append(("oin", jl, 3072 + t * 128 + np.arange(128)))
            for t in range(4):
                pieces.append(("oout", jl, 512 + t * 128 + np.arange(128)))
    return pieces


def _host_prep(inp, nl):
    f32 = np.float32
    pieces = _piece_list(nl)
    W = np.empty((len(pieces), 128, 1024), f32)
    for i, (kind, l, idx) in enumerate(pieces):
        if kind == "mod":
            w = inp["mod_w"][l][:, idx]
            W[i] = w.reshape(8, 128, 128).transpose(1, 0, 2).reshape(128, 1024)
        elif kind == "ein":
            w = inp["ev_w_in"][l][:, idx]
            W[i] = w.reshape(8, 128, 128).transpose(1, 0, 2).reshape(128, 1024)
        elif kind == "oin":
            w = inp["od_w_in"][l][:, idx]
            W[i] = w.reshape(8, 128, 128).transpose(1, 0, 2).reshape(128, 1024)
        elif kind == "eout":
            W[i] = inp["ev_w_out"][l][idx, :]
        else:
            W[i] = inp["od_w_out"][l][idx, :]
    cst = np.zeros((128, 128 * 4 + 256), f32)
    cst[:, 0:128] = np.eye(128, dtype=f32)
    sh = np.zeros((128, 128), f32)
    sh[np.arange(64), 64 + np.arange(64)] = 1.0
    cst[:, 128:256] = sh
    bo = np.zeros((128, 128), f32)
    bo[:64, :64] = 1.0 / 64
    bo[64:, 64:] = 1.0 / 64
    cst[:, 256:384] = bo
    cst[:, 384:512] = 1.0
    jj = np.arange(128)[:, None]
    ii = np.arange(128)[None, :]
    cst[:, 512:640] = (jj >= ii).astype(f32)
    cst[:, 640:768] = (jj <= ii).astype(f32)
    t = np.arange(NLAT)
    row = (t // 64).astype(f32)
    col = (t % 64).astype(f32)
    half = 32
    freqs = (f32(10000.0) ** (-np.arange(0, half, 2, dtype=f32) / f32(half))).astype(f32)
    ang = np.concatenate([row[:, None] * freqs, col[:, None] * freqs], axis=-1).astype(f32)
    cos, sin = np.cos(ang).astype(f32), np.sin(ang).astype(f32)
    d = np.arange(128) % 64
    C = cos[:, d // 2].T
    S = sin[:, d // 2].T * np.where(d % 2 == 0, -1.0, 1.0)[:, None].astype(f32)
    rope = np.empty((4, 128, 2, 512), f32)
    for c in range(4):
        rope[c, :, 0, :] = C[:, c * 512:(c + 1) * 512]
        rope[c, :, 1, :] = S[:, c * 512:(c + 1) * 512]
    n_odd = max(1, nl // 2)
    tg = np.full((n_odd, 4, 3, 128, 1024), -1e30, f32)
    variants = [(dl, False) for dl in range(-3, 4)] + [(-2, True), (-1, False), (0, False), (1, False), (2, True)]
    kk = np.arange(128)
    rkl, ck = kk // 64, kk % 64
    rl, cq = kk // 64, kk % 64
    cs = np.clip(cq - 8, 0, 48)
    colok = (ck[:, None] >= cs[None, :]) & (ck[:, None] < cs[None, :] + 16)
    dc = np.clip(ck[:, None] - cq[None, :] + 15, 0, 30)
    for jl in range(nl // 2):
        rpb = inp["d_rpb"][jl]
        for pr in range(4):
            for s in range(2):
                h = 2 * pr + s
                for vi, (dl, msk) in enumerate(variants):
                    dr = 2 * dl + rkl[:, None] - rl[None, :]
                    ok = colok & (np.abs(dr) <= 7)
                    if msk:
                        ok = ok & (dr >= -4) & (dr <= 3)
                    g = rpb[h][np.clip(dr + 7, 0, 14), dc]
                    blk = np.where(ok, g, f32(-1e30)).astype(f32)
                    q = s * 12 + vi
                    tg[jl, pr, q // 8, :, (q % 8) * 128:(q % 8) * 128 + 128] = blk
    def fm(v):
        return np.asarray(v, f32).reshape(8, 128).T
    pars = []
    for b in range(8):
        cols = [fm(inp["c"][b]), fm(inp["c_ctx"])]
        for l in range(4):
            cols.append(fm(inp["norm_g"][l]))
            cols.append(np.asarray(inp["mod_b"][l], f32).reshape(24, 128).T)
        for jl in range(2):
            for nm in ("a_q_gain", "a_k_gain"):
                gq = np.asarray(inp[nm][jl], f32)
                cols.append(gq[d][:, None])
                cols.append(gq[d ^ 1][:, None])
            cols.append(np.broadcast_to(np.asarray(inp["b_sink"][jl], f32)[None, :], (128, 8)))
        for jl in range(2):
            dw = np.asarray(inp["c_dw_w"][jl], f32)
            cols.append(dw.reshape(31, 4, 128).transpose(2, 1, 0).reshape(128, 124))
            for nm in ("c_dw_b", "c_ln_g", "c_ln_b"):
                cols.append(np.asarray(inp[nm][jl], f32).reshape(4, 128).T)
        cols.append(fm(inp["final_g"]))
        pars.append(np.ascontiguousarray(np.concatenate(cols, axis=1)))
    xs = [np.ascontiguousarray(np.concatenate([inp["x"][b], inp["ctx"][b]], axis=0)) for b in range(8)]
    return dict(W=W, cst=cst, rope=rope, tg=tg, pars=pars, xs=xs, npar=pars[0].shape[1])


def _par_off():
    o = {}
    p = 0
    o["c"] = p; p += 8
    o["cctx"] = p; p += 8
    for l in range(4):
        o["ng", l] = p; p += 8
        o["mb", l] = p; p += 24
    for jl in range(2):
        o["qg", jl] = p; p += 1
        o["qgs", jl] = p; p += 1
        o["kg", jl] = p; p += 1
        o["kgs", jl] = p; p += 1
        o["sink", jl] = p; p += 8
    for jl in range(2):
        o["dww", jl] = p; p += 124
        o["dwb", jl] = p; p += 4
        o["lng", jl] = p; p += 4
        o["lnb", jl] = p; p += 4
    o["fg"] = p; p += 8
    o["n"] = p
    return o


def build(nl=4):
    nc = bass.Bass("TRN2", target_bir_lowering=False)
    PO = _par_off()
    npieces = len(_piece_list(nl))
    n_odd = max(1, nl // 2)
    xd = nc.dram_tensor("x", [NT, 1024], F32, kind="ExternalInput").ap()
    pard = nc.dram_tensor("par", [128, PO["n"]], F32, kind="ExternalInput").ap()
    wd = nc.dram_tensor("w", [npieces, 128, 1024], F32, kind="ExternalInput").ap()
    cstd = nc.dram_tensor("cst", [128, 768], F32, kind="ExternalInput").ap()
    roped = nc.dram_tensor("rope", [4, 128, 1024], F32, kind="ExternalInput").ap()
    tgd = nc.dram_tensor("tg", [n_odd, 4, 3, 128, 1024], F32, kind="ExternalInput").ap()
    yd = nc.dram_tensor("y", [NLAT, 1024], F32, kind="ExternalOutput").ap()

    st = ExitStack()
    with st:
        def SB(name, shape, dt):
            return st.enter_context(nc.sbuf_tensor(name, shape, dt))
        XT = SB("XT", [128, 8, NT], F32)
        HT = SB("HT", [128, 8, NT], BF16)
        QM = SB("QM", [128, 4, NT], BF16)
        KG = SB("KG", [128, 2, NT], BF16)
        VA = SB("VA", [128, 18, 2, 65], BF16)
        PT = SB("PT", [128, 4, 512], BF16)
        STG = SB("STG", [128, 2, 1024], F32)
        WB = SB("WB", [128, 4, 1024], BF16)
        WO = SB("WO", [128, 4, 1024], BF16)
        RCS = SB("RCS", [128, 2, 1024], F32)
        ED = SB("ED", [128, 3968], BF16)
        CB = SB("CB", [128, 768], BF16)
        ONS = SB("ONS", [128, 2, 128], BF16)
        PAR = SB("PAR", [128, PO["n"]], F32)
        MODW = SB("MODW", [128, 24, 2], F32)
        MCO = SB("MCO", [128, 3, 8, 2], F32)
        CS = SB("CS", [128, 8, 2], BF16)
        ESK = SB("ESK", [128, 8], F32)
        TF = [SB("TF%d" % i, [128, 512], F32) for i in range(4)]
        TB = [SB("TB%d" % i, [128, 512], BF16) for i in range(2)]
        RW = SB("RW", [128, 512], F32)
        RH = SB("RH", [128, 512], BF16)
        RL = SB("RL", [128, 512], BF16)
        OS = SB("OS", [64, 512], F32)
        ON = [SB("ON%d" % i, [64, 512], BF16) for i in range(2)]
        PS = [st.enter_context(nc.psum_tensor("PS%d" % i, [128, 512], F32)) for i in range(8)]

        KTt = KG[:, 0, :]
        GTt = KG[:, 1, :]
        UP = KG[:].rearrange("p a b -> p (a b)")
        VA36 = VA[:].rearrange("p t g d -> p (t g) d")
        IDENT = CB[:, 0:128]
        SHIFT = CB[:, 128:256]
        BONES = CB[:, 256:384]
        ONES = CB[:, 384:512]
        TRIL = CB[:, 512:640]
        TRIU = CB[:, 640:768]
        DG = ED[:, 0:3968].rearrange("p (k m) -> p k m", k=31)
        ET = ED[:, 0:3072].rearrange("p (s i q) -> p s i q", s=2, i=12)

        P = Prog(nc)
        rr = {"stg": 0, "wb": 0, "pt": 0, "sb": 0, "rcs": 0, "pj": 0, "piece": 0}

        def ACT(out, in_, func, reads, writes, **kw):
            P.act(lambda e: e.activation(out, in_, func, **kw), reads, writes)

        def TT(out, a, b, op, reads, writes, eng="vector"):
            P.add(eng, lambda e: e.tensor_tensor(out, a, b, op), reads, writes)

        def STT(out, in0, scalar, in1, op0, op1, reads, writes):
            P.dve(lambda e: e.scalar_tensor_tensor(out, in0, scalar, in1, op0, op1), reads, writes)

        def TS(out, in0, s1, s2, op0, op1, reads, writes):
            if op1 is None:
                P.dve(lambda e: e.tensor_scalar(out, in0, s1, None, op0), reads, writes)
            else:
                P.dve(lambda e: e.tensor_scalar(out, in0, s1, s2, op0, op1), reads, writes)

        def CPY(out, in_, reads, writes, eng="vector"):
            P.add(eng, lambda e: e.tensor_copy(out, in_), reads, writes)

        def RCP(out, in_, reads, writes, bias=0.0):
            ACT(out, in_, AF.Ln, reads, writes, bias=bias)
            ACT(out, out, AF.Exp, writes, writes, scale=-1.0)

        def MM(lst, reads, writes):
            def f(e):
                r = None
                for (o, l, rh, s0, s1) in lst:
                    r = e.matmul(o, l, rh, start=s0, stop=s1)
                return r
            P.pe(f, reads, writes)

        def tiles(c0, n):
            return list(range(c0 // 128, (c0 + n) // 128))

        def next_piece():
            i = rr["piece"]; rr["piece"] += 1
            s = rr["stg"] % 2; rr["stg"] += 1
            w = rr["wb"] % 4; rr["wb"] += 1
            P.dma(lambda e: e.dma_start(out=STG[:, s, :], in_=wd[i]), writes=[("STG", s)])
            CPY(WB[:, w, :], STG[:, s, :], [("STG", s)], [("WB", w)], eng="gpsimd")
            return w

        def next_wo(j):
            i = rr["piece"]; rr["piece"] += 1
            s = rr["stg"] % 2; rr["stg"] += 1
            P.dma(lambda e: e.dma_start(out=STG[:, s, :], in_=wd[i]), writes=[("STG", s)])
            CPY(WO[:, j, :], STG[:, s, :], [("STG", s)], [("WO", j)], eng="gpsimd")

        def proj(w, c0, n, bank):
            wv = WB[:, w, :].rearrange("p (k m) -> p k m", k=8)
            MM([(PS[bank][:, 0:n], wv[:, kc, :], HT[:, kc, c0:c0 + n], kc == 0, kc == 7) for kc in range(8)],
               [("WB", w), ("H", c0)], [("ps", bank)])

        def sigm_recip(dst, src_ps, n, rkeys):
            ACT(dst[:, 0:n], src_ps, AF.Exp, rkeys, [("t", id(dst))], scale=-1.0)
            RCP(dst[:, 0:n], dst[:, 0:n], [("t", id(dst))], [("t", id(dst))], bias=1.0)

        P.dma(lambda e: e.dma_start(out=PAR[:], in_=pard), writes=["PAR"])
        P.dma(lambda e: e.dma_start(out=STG[:, 0, 0:768], in_=cstd), writes=[("STG", 0)])
        CPY(CB[:], STG[:, 0, 0:768], [("STG", 0)], ["CB"])
        rr["stg"] = 1
        P.pool(lambda e: e.memset(ONS[:, 0, :], 1.0 / 1024), writes=["ONS"])
        P.pool(lambda e: e.memset(ONS[:, 1, :], 1.0 / 512), writes=["ONS"])
        P.pool(lambda e: e.memset(VA[:], 1.0), writes=[("V", t) for t in range(18)])
        for q, key in enumerate(("c", "cctx")):
            src = PAR[:, PO[key]:PO[key] + 8]
            ACT(TF[0][:, 0:8], src, AF.Exp, ["PAR"], [("t", id(TF[0]))], scale=-1.0)
            RCP(TF[0][:, 0:8], TF[0][:, 0:8], [("t", id(TF[0]))], [("t", id(TF[0]))], bias=1.0)
            TT(CS[:, :, q], src, TF[0][:, 0:8], ALU.mult, [("t", id(TF[0])), "PAR"], ["CS"])
        for t in range(18):
            s = rr["stg"] % 2; rr["stg"] += 1
            P.dma(lambda e, t=t, s=s: e.dma_start(out=STG[:, s, :], in_=xd[t * 128:(t + 1) * 128, :]), writes=[("STG", s)])
            hi, lo = WB[:, 2 * (t % 2), :], WB[:, 2 * (t % 2) + 1, :]
            kh, kl = ("WB", 2 * (t % 2)), ("WB", 2 * (t % 2) + 1)
            CPY(hi, STG[:, s, :], [("STG", s)], [kh])
            TT(lo, STG[:, s, :], hi, ALU.subtract, [("STG", s), kh], [kl])
            for fg in range(2):
                bank = rr["pj"] % 2; rr["pj"] += 1
                lst = []
                for i in range(4):
                    f = fg * 4 + i
                    lst.append((PS[bank][:, i * 128:(i + 1) * 128], hi[:, f * 128:(f + 1) * 128], IDENT, True, False))
                    lst.append((PS[bank][:, i * 128:(i + 1) * 128], lo[:, f * 128:(f + 1) * 128], IDENT, False, True))
                MM(lst, [kh, kl, "CB"], [("ps", bank)])
                cc = min(t // 4, 4)
                P.act(lambda e, bank=bank, fg=fg, t=t: e.copy(XT[:, fg * 4:fg * 4 + 4, t * 128:(t + 1) * 128],
                                                                PS[bank][:, :].rearrange("p (a b) -> p a b", a=4)),
                      [("ps", bank)], [("X", kc, CHUNKS[cc][0]) for kc in range(fg * 4, fg * 4 + 4)])

        def modulation(l):
            for m in range(24):
                w = next_piece()
                wv = WB[:, w, :].rearrange("p (k m) -> p k m", k=8)
                MM([(PS[6][:, 0:2], wv[:, kc, :], CS[:, kc, :], kc == 0, kc == 7) for kc in range(8)],
                   [("WB", w), "CS"], [("ps", 6)])
                TS(MODW[:, m, :], PS[6][:, 0:2], PAR[:, PO["mb", l] + m:PO["mb", l] + m + 1], None, ALU.add, None,
                   [("ps", 6), "PAR"], ["MODW"])

        def mod_finish(l):
            ng = PAR[:, PO["ng", l]:PO["ng", l] + 8]
            for q in range(2):
                TS(MCO[:, 0, :, q], MODW[:, 8:16, q], 1.0, None, ALU.add, None, ["MODW"], ["MCO"])
                TT(MCO[:, 0, :, q], MCO[:, 0, :, q], ng, ALU.mult, ["MCO", "PAR"], ["MCO"])
                CPY(MCO[:, 1, :, q], MODW[:, 0:8, q], ["MODW"], ["MCO"])
                CPY(MCO[:, 2, :, q], MODW[:, 16:24, q], ["MODW"], ["MCO"])

        def norm_to_h():
            for (c0, n) in CHUNKS:
                q = 0 if c0 < NLAT else 1
                for kc in range(8):
                    sq = TB[kc % 2]
                    ACT(sq[:, 0:n], XT[:, kc, c0:c0 + n], AF.Square, [("X", kc, c0)], [("t", id(sq))])
                    MM([(PS[6][:, 0:n], ONS[:, 0, :], sq[:, 0:n], kc == 0, kc == 7)], [("t", id(sq)), "ONS"], [("ps", 6)])
                ACT(TF[0][:, 0:n], PS[6][:, 0:n], AF.Ln, [("ps", 6)], [("t", id(TF[0]))], bias=EPS)
                ACT(TF[0][:, 0:n], TF[0][:, 0:n], AF.Exp, [("t", id(TF[0]))], [("t", id(TF[0]))], scale=-0.5)
                for kc in range(8):
                    tmp = TF[1 + kc % 2]
                    STT(tmp[:, 0:n], XT[:, kc, c0:c0 + n], MCO[:, 0, kc, q:q + 1], TF[0][:, 0:n], ALU.mult, ALU.mult,
                        [("X", kc, c0), "MCO", ("t", id(TF[0]))], [("t", id(tmp))])
                    ACT(HT[:, kc, c0:c0 + n], tmp[:, 0:n], AF.Identity, [("t", id(tmp)), "MCO"], [("H", c0)],
                        bias=MCO[:, 1, kc, q:q + 1], scale=1.0)

        def qk_tile(dst_fn, dkeys_fn, use_norm, gain_ap, gain_sw_ap, chunks):
            w = next_piece()
            ws = next_piece()
            for (c0, n) in chunks:
                lat = c0 < NLAT
                proj(w, c0, n, 0)
                if lat:
                    proj(ws, c0, n, 1)
                if use_norm:
                    ACT(TB[0][:, 0:n], PS[0][:, 0:n], AF.Square, [("ps", 0)], [("t", id(TB[0]))])
                    MM([(PS[6][:, 0:n], BONES, TB[0][:, 0:n], True, True)], [("t", id(TB[0])), "CB"], [("ps", 6)])
                    ACT(TF[0][:, 0:n], PS[6][:, 0:n], AF.Ln, [("ps", 6)], [("t", id(TF[0]))], bias=EPS)
                    ACT(TF[0][:, 0:n], TF[0][:, 0:n], AF.Exp, [("t", id(TF[0]))], [("t", id(TF[0]))], scale=-0.5)
                    STT(TF[1][:, 0:n], PS[0][:, 0:n], gain_ap, TF[0][:, 0:n], ALU.mult, ALU.mult,
                        [("ps", 0), "PAR", ("t", id(TF[0]))], [("t", id(TF[1]))])
                    if lat:
                        STT(TF[2][:, 0:n], PS[1][:, 0:n], gain_sw_ap, TF[0][:, 0:n], ALU.mult, ALU.mult,
                            [("ps", 1), "PAR", ("t", id(TF[0]))], [("t", id(TF[2]))])
                    a_src, b_src = TF[1][:, 0:n], TF[2][:, 0:n]
                    akeys, bkeys = [("t", id(TF[1]))], [("t", id(TF[2]))]
                else:
                    a_src, b_src = PS[0][:, 0:n], PS[1][:, 0:n]
                    akeys, bkeys = [("ps", 0)], [("ps", 1)]
                if lat:
                    rs = rr["rcs"] % 2; rr["rcs"] += 1
                    ci = c0 // 512
                    P.dma(lambda e, rs=rs, ci=ci: e.dma_start(out=RCS[:, rs, :], in_=roped[ci]), writes=[("RCS", rs)])
                    TT(TF[1][:, 0:n], a_src, RCS[:, rs, 0:n], ALU.mult, akeys + [("RCS", rs)], [("t", id(TF[1]))])
                    TT(TF[2][:, 0:n], b_src, RCS[:, rs, 512:512 + n], ALU.mult, bkeys + [("RCS", rs)], [("t", id(TF[2]))])
                    TT(dst_fn(c0, n), TF[1][:, 0:n], TF[2][:, 0:n], ALU.add,
                       [("t", id(TF[1])), ("t", id(TF[2]))], dkeys_fn(c0, n))
                else:
                    if use_norm:
                        CPY(dst_fn(c0, n), a_src, akeys, dkeys_fn(c0, n))
                    else:
                        P.act(lambda e, c0=c0, n=n: e.copy(dst_fn(c0, n), PS[0][:, 0:n]), [("ps", 0)], dkeys_fn(c0, n))

        def plain_tile(dst_fn, dkeys_fn, chunks):
            w = next_piece()
            for (c0, n) in chunks:
                bank = rr["pj"] % 2; rr["pj"] += 1
                proj(w, c0, n, bank)
                P.act(lambda e, c0=c0, n=n, bank=bank: e.copy(dst_fn(c0, n), PS[bank][:, 0:n]), [("ps", bank)], dkeys_fn(c0, n))

        def v_tile():
            w = next_piece()
            wv = WB[:, w, :].rearrange("p (k m) -> p k m", k=8)
            for g4 in range(5):
                tl = list(range(g4 * 4, min(g4 * 4 + 4, 18)))
                bank = rr["pj"] % 2; rr["pj"] += 1
                lst = []
                for i, t in enumerate(tl):
                    for kc in range(8):
                        lst.append((PS[bank][:, i * 128:(i + 1) * 128], HT[:, kc, t * 128:(t + 1) * 128], wv[:, kc, :], kc == 0, kc == 7))
                MM(lst, [("WB", w)] + [("H", CHUNKS[min(t // 4, 4)][0]) for t in tl], [("ps", bank)])
                nt_ = len(tl)
                P.act(lambda e, bank=bank, tl=tl, nt_=nt_: e.copy(
                    VA36[:, 2 * tl[0]:2 * tl[0] + 2 * nt_, 0:64],
                    PS[bank][:, 0:nt_ * 128].rearrange("p (a b) -> p a b", b=64)),
                    [("ps", bank)], [("V", t) for t in tl])

        def gate_tile(chunks, mul_into=None):
            w = next_piece()
            for (c0, n) in chunks:
                bank = rr["pj"] % 2; rr["pj"] += 1
                proj(w, c0, n, bank)
                sigm_recip(TF[3], PS[bank][:, 0:n], n, [("ps", bank)])
                if mul_into is None:
                    TT(GTt[:, c0:c0 + n], PS[bank][:, 0:n], TF[3][:, 0:n], ALU.mult,
                       [("ps", bank), ("t", id(TF[3]))], [("G", t) for t in tiles(c0, n)])
                else:
                    j = mul_into
                    qk_ = [("Q", j, t) for t in tiles(c0, n)]
                    TT(TF[3][:, 0:n], PS[bank][:, 0:n], TF[3][:, 0:n], ALU.mult,
                       [("ps", bank), ("t", id(TF[3]))], [("t", id(TF[3]))])
                    TT(QM[:, j, c0:c0 + n], QM[:, j, c0:c0 + n], TF[3][:, 0:n], ALU.mult, qk_ + [("t", id(TF[3]))], qk_)

        def slot(nq, idx):
            if nq == 128:
                return idx * 128, [idx]
            return 0, list(range((nq + 127) // 128))

        def attend_pair(j, c0, nq, kts, g, quad=False):
            if quad:
                qk = [("Q", i, t) for i in range(4) for t in tiles(c0, 128)]
                qrhs = [QM[64 * s:64 * s + 64, 0:4, c0:c0 + 128] for s in range(2)]
                nq = 512
            else:
                qk = [("Q", j, t) for t in tiles(c0, nq)]
                qrhs = [QM[64 * s:64 * s + 64, j, c0:c0 + nq] for s in range(2)]
            G = 512 // nq
            nk = len(kts[0])
            groups = [(g0, min(g0 + G, nk)) for g0 in range(0, nk, G)]
            obank = [4 + 2 * g, 5 + 2 * g]
            ocol = [0, 0]
            okeys = [[("ps", obank[0])], [("ps", obank[1])]]
            sbank = {}

            def emitS(gi):
                a, b = groups[gi]
                for s in range(2):
                    sb = rr["sb"] % 4; rr["sb"] += 1
                    sbank[gi, s] = sb
                    grp = kts[s][a:b]
                    MM([(PS[sb][:, i * nq:(i + 1) * nq], KTt[64 * s:64 * s + 64, kt * 128:(kt + 1) * 128],
                         qrhs[s], True, True) for i, (kt, _m) in enumerate(grp)],
                       [("K", kt) for kt, _m in grp] + qk, [("ps", sb)])
            emitS(0)
            for gi, (a, b) in enumerate(groups):
                if gi + 1 < len(groups):
                    emitS(gi + 1)
                pts = []
                for s in range(2):
                    sb = sbank[gi, s]
                    pt = rr["pt"] % 4; rr["pt"] += 1
                    pts.append(pt)
                    wdt = (b - a) * nq
                    ACT(PT[:, pt, 0:wdt], PS[sb][:, 0:wdt], AF.Exp, [("ps", sb)], [("PT", pt)], scale=0.125)
                for s in range(2):
                    pt = pts[s]
                    for i, (kt, m) in enumerate(kts[s][a:b]):
                        if m is not None:
                            ptv = PT[:, pt, i * nq:(i + 1) * nq]
                            if quad:
                                ptv = ptv.rearrange("p (a b) -> p a b", a=4)
                            TT(ptv, ptv, m[0], ALU.mult, [("PT", pt), m[1]], [("PT", pt)])
                for s in range(2):
                    pt = pts[s]
                    grp = kts[s][a:b]
                    MM([(PS[obank[s]][0:65, ocol[s]:ocol[s] + nq], VA[:, kt, s, 0:65], PT[:, pt, i * nq:(i + 1) * nq],
                         a + i == 0, a + i == nk - 1) for i, (kt, _m) in enumerate(grp)],
                       [("PT", pt)] + [("V", kt) for kt, _m in grp], okeys[s])

        def attn_epilogue(j, c0, nq, sink_heads, g):
            obank = [4 + 2 * g, 5 + 2 * g]
            m0, mq = slot(nq, g)
            for s in range(2):
                o0 = 0
                okeys = [("ps", obank[s])]
                bb = rr["sb"] % 4; rr["sb"] += 1
                r0, rq = slot(nq, 2 * g + s)
                rwk = [("RW", q) for q in rq]
                rhk = [("RH", q) for q in rq]
                rlk = [("RL", q) for q in rq]
                b6k = [("ps", bb)]
                osk = [("OS", q) for q in rq]
                onk = [("ON", s, q) for q in mq]
                Osum = PS[obank[s]][64:65, o0:o0 + nq]
                bias = ESK[64:65, sink_heads[s]:sink_heads[s] + 1] if sink_heads is not None else 0.0
                RCP(RW[64:65, r0:r0 + nq], Osum, okeys + (["ESK"] if sink_heads is not None else []), rwk, bias=bias)
                CPY(RH[64:65, r0:r0 + nq], RW[64:65, r0:r0 + nq], rwk, rhk)
                TT(RL[64:65, r0:r0 + nq], RW[64:65, r0:r0 + nq], RH[64:65, r0:r0 + nq], ALU.subtract, rwk + rhk, rlk)
                MM([(PS[bb][0:64, 0:nq], ONES[64:65, 0:64], RH[64:65, r0:r0 + nq], True, False),
                    (PS[bb][0:64, 0:nq], ONES[64:65, 0:64], RL[64:65, r0:r0 + nq], False, True)], rhk + rlk + ["CB"], b6k)
                P.act(lambda e, s=s, o0=o0, r0=r0: e.copy(OS[0:64, r0:r0 + nq], PS[obank[s]][0:64, o0:o0 + nq]), okeys, osk)
                TT(ON[s][:, m0:m0 + nq], OS[0:64, r0:r0 + nq], PS[bb][0:64, 0:nq], ALU.mult, osk + b6k, onk)
            mb = rr["sb"] % 4; rr["sb"] += 1
            m7k = [("ps", mb)]
            MM([(PS[mb][:, 0:nq], IDENT[0:64, :], ON[0][:, m0:m0 + nq], True, False),
                (PS[mb][:, 0:nq], SHIFT[0:64, :], ON[1][:, m0:m0 + nq], False, True)],
               [("ON", 0, q) for q in mq] + [("ON", 1, q) for q in mq] + ["CB"], m7k)
            tl = tiles(c0, nq)
            TT(QM[:, j, c0:c0 + nq], PS[mb][:, 0:nq], GTt[:, c0:c0 + nq], ALU.mult,
               m7k + [("G", t) for t in tl], [("Q", j, t) for t in tl])

        def quad_epilogue(c0, g):
            qv = lambda ap: ap.rearrange("p (a b) -> p a b", a=4)
            tl = tiles(c0, 128)
            qkeys_ = [("Q", i, t) for i in range(4) for t in tl]
            for s in range(2):
                ob = 4 + 2 * g + s
                okeys = [("ps", ob)]
                bb = rr["sb"] % 4; rr["sb"] += 1
                for i in range(4):
                    h = 4 * s + i
                    ACT(RW[64:65, i * 128:(i + 1) * 128], PS[ob][64:65, i * 128:(i + 1) * 128], AF.Ln,
                        okeys + ["ESK"], ["RWq"], bias=ESK[64:65, h:h + 1])
                ACT(RW[64:65, 0:512], RW[64:65, 0:512], AF.Exp, ["RWq"], ["RWq"], scale=-1.0)
                CPY(RH[64:65, 0:512], RW[64:65, 0:512], ["RWq"], ["RHq"])
                TT(RL[64:65, 0:512], RW[64:65, 0:512], RH[64:65, 0:512], ALU.subtract, ["RWq", "RHq"], ["RLq"])
                MM([(PS[bb][0:64, 0:512], ONES[64:65, 0:64], RH[64:65, 0:512], True, False),
                    (PS[bb][0:64, 0:512], ONES[64:65, 0:64], RL[64:65, 0:512], False, True)], ["RHq", "RLq", "CB"], [("ps", bb)])
                P.act(lambda e, ob=ob: e.copy(OS[0:64, 0:512], PS[ob][0:64, 0:512]), okeys, ["OSq"])
                if s == 0:
                    TT(QM[0:64, 0:4, c0:c0 + 128], qv(OS[0:64, 0:512]), qv(PS[bb][0:64, 0:512]), ALU.mult,
                       ["OSq", ("ps", bb)], qkeys_)
                else:
                    TT(ON[1][:, 0:512], OS[0:64, 0:512], PS[bb][0:64, 0:512], ALU.mult, ["OSq", ("ps", bb)], ["ONq"])
                    mb = rr["sb"] % 4; rr["sb"] += 1
                    MM([(PS[mb][:, 0:512], SHIFT[0:64, :], ON[1][:, 0:512], True, True)], ["ONq", "CB"], [("ps", mb)])
                    P.act(lambda e, mb=mb: e.copy(QM[64:128, 0:4, c0:c0 + 128], qv(PS[mb][64:128, 0:512])), [("ps", mb)], qkeys_)

        def run_quad_blocks(blocks):
            pending = None
            for bi, (c0, kts) in enumerate(blocks):
                g = bi % 2
                attend_pair(0, c0, 128, kts, g, quad=True)
                if pending is not None:
                    pending()
                pending = (lambda c0=c0, g=g: quad_epilogue(c0, g))
            if pending is not None:
                pending()

        def run_blocks(j, blocks, sink_heads):
            pending = None
            for bi, (c0, nq, kts) in enumerate(blocks):
                g = bi % 2 if nq == 128 else 0
                attend_pair(j, c0, nq, kts, g)
                if pending is not None:
                    pending()
                    pending = None
                if nq == 128:
                    pending = (lambda c0=c0, nq=nq, g=g: attn_epilogue(j, c0, nq, sink_heads, g))
                else:
                    attn_epilogue(j, c0, nq, sink_heads, g)
            if pending is not None:
                pending()

        def out_proj(upd_ctx):
            for (c0, n) in CHUNKS:
                if c0 >= NLAT and not upd_ctx:
                    continue
                q = 0 if c0 < NLAT else 1
                for m in range(8):
                    bank = rr["pj"] % 2; rr["pj"] += 1
                    MM([(PS[bank][:, 0:n], WO[:, j, m * 128:(m + 1) * 128], QM[:, j, c0:c0 + n], j == 0, j == 3) for j in range(4)],
                       [("WO", j) for j in range(4)] + [("Q", j, t) for j in range(4) for t in tiles(c0, n)], [("ps", bank)])
                    STT(XT[:, m, c0:c0 + n], PS[bank][:, 0:n], MCO[:, 2, m, q:q + 1], XT[:, m, c0:c0 + n], ALU.mult, ALU.add,
                        [("ps", bank), "MCO", ("X", m, c0)], [("X", m, c0)])

        QCH = lambda upd: [ch for ch in CHUNKS if ch[0] < NLAT or upd]
        qdst = lambda j: (lambda c0, n: QM[:, j, c0:c0 + n])
        qkeys = lambda j: (lambda c0, n: [("Q", j, t) for t in tiles(c0, n)])
        kdst = lambda c0, n: KTt[:, c0:c0 + n]
        kkeys = lambda c0, n: [("K", t) for t in tiles(c0, n)]

        def even_layer(l, upd):
            jl = l // 2
            ACT(ESK[:], PAR[:, PO["sink", jl]:PO["sink", jl] + 8], AF.Exp, ["PAR"], ["ESK"])
            for mixer in range(2):
                isA = mixer == 0
                kg = PAR[:, PO["kg", jl]:PO["kg", jl] + 1]
                kgs = PAR[:, PO["kgs", jl]:PO["kgs", jl] + 1]
                qg = PAR[:, PO["qg", jl]:PO["qg", jl] + 1]
                qgs = PAR[:, PO["qgs", jl]:PO["qgs", jl] + 1]
                qk_tile(kdst, kkeys, isA, kg, kgs, CHUNKS)
                v_tile()
                if isA:
                    for j in range(4):
                        qk_tile(qdst(j), qkeys(j), True, qg, qgs, QCH(upd))
                        gate_tile(QCH(upd))
                        blocks = []
                        for (c0, n) in QCH(upd):
                            kl = [(kt, None) for kt in ([16, 17] + list(range(16)) if c0 < NLAT else [16, 17])]
                            blocks.append((c0, n, [kl, kl]))
                        run_blocks(j, blocks, None)
                else:
                    for j in range(4):
                        qk_tile(qdst(j), qkeys(j), False, qg, qgs, QCH(upd))
                    tril4 = (CB[:, 512:640].unsqueeze(1).broadcast_to((128, 4, 128)), "CB")
                    triu4 = (CB[:, 640:768].unsqueeze(1).broadcast_to((128, 4, 128)), "CB")
                    blocks = []
                    for qb in list(range(16)) + ([16, 17] if upd else []):
                        kl = [(16, None), (17, None)]
                        if qb < 16:
                            if qb > 0:
                                kl.append((qb - 1, tril4))
                            kl.append((qb, None))
                            if qb < 15:
                                kl.append((qb + 1, triu4))
                        blocks.append((qb * 128, [kl, kl]))
                    run_quad_blocks(blocks)
                    for j in range(4):
                        gate_tile(QCH(upd), mul_into=j)
                for j in range(4):
                    next_wo(j)
                out_proj(upd)
                if mixer == 0 and l + 1 < nl:
                    modulation(l + 1)

        def odd_layer(l, upd):
            jl = l // 2
            chunks = QCH(upd)
            seqs = [(0, 2048, 0)] + ([(2048, 256, 2078)] if upd else [])
            P.pool(lambda e: e.memset(UP[:, 0:2364], 0.0), writes=[("K", t) for t in range(18)] + [("G", t) for t in range(18)] + ["UP"])
            for c in range(4):
                wv_ = next_piece()
                wg_ = next_piece()
                for (c0, n) in chunks:
                    proj(wv_, c0, n, 0)
                    proj(wg_, c0, n, 1)
                    sigm_recip(TF[3], PS[1][:, 0:n], n, [("ps", 1)])
                    base = 15 + c0 if c0 < NLAT else 2078 + 15 + (c0 - NLAT)
                    TT(UP[:, base:base + n], PS[0][:, 0:n], TF[3][:, 0:n], ALU.mult, [("ps", 0), ("t", id(TF[3]))], ["UP"])
                for k in range(31):
                    col = PO["dww", jl] + c * 31 + k
                    TS(DG[:, k, :], IDENT, PAR[:, col:col + 1], None, ALU.mult, None, ["CB", "PAR"], ["DG"])
                for (c0, n) in chunks:
                    bank = rr["pj"] % 2; rr["pj"] += 1
                    base = c0 if c0 < NLAT else 2078 + (c0 - NLAT)
                    MM([(PS[bank][:, 0:n], DG[:, k, :], UP[:, base + k:base + k + n], k == 0, k == 30) for k in range(31)],
                       ["DG", "UP"], [("ps", bank)])
                    ACT(QM[:, c, c0:c0 + n], PS[bank][:, 0:n], AF.Identity, [("ps", bank), "PAR"],
                        [("Q", c, t) for t in tiles(c0, n)], bias=PAR[:, PO["dwb", jl] + c:PO["dwb", jl] + c + 1], scale=1.0)
            wgs = [next_piece() for _ in range(4)]
            for (c0, n) in chunks:
                tl = tiles(c0, n)
                for c in range(4):
                    MM([(PS[6][:, 0:n], ONS[:, 1, :], QM[:, c, c0:c0 + n], c == 0, c == 3)], [("Q", c, t) for t in tl] + ["ONS"], [("ps", 6)])
                for c in range(4):
                    sq = TB[c % 2]
                    ACT(sq[:, 0:n], QM[:, c, c0:c0 + n], AF.Square, [("Q", c, t) for t in tl], [("t", id(sq))])
                    MM([(PS[7][:, 0:n], ONS[:, 1, :], sq[:, 0:n], c == 0, c == 3)], [("t", id(sq)), "ONS"], [("ps", 7)])
                P.act(lambda e, n=n: e.copy(TF[0][:, 0:n], PS[6][:, 0:n]), [("ps", 6)], [("t", id(TF[0]))])
                TT(TF[1][:, 0:n], TF[0][:, 0:n], TF[0][:, 0:n], ALU.mult, [("t", id(TF[0]))], [("t", id(TF[1]))])
                TT(TF[1][:, 0:n], PS[7][:, 0:n], TF[1][:, 0:n], ALU.subtract, [("ps", 7), ("t", id(TF[1]))], [("t", id(TF[1]))])
                ACT(TF[1][:, 0:n], TF[1][:, 0:n], AF.Ln, [("t", id(TF[1]))], [("t", id(TF[1]))], bias=EPS)
                ACT(TF[1][:, 0:n], TF[1][:, 0:n], AF.Exp, [("t", id(TF[1]))], [("t", id(TF[1]))], scale=-0.5)
                for c in range(4):
                    lg = PAR[:, PO["lng", jl] + c:PO["lng", jl] + c + 1]
                    lb = PAR[:, PO["lnb", jl] + c:PO["lnb", jl] + c + 1]
                    qmk = [("Q", c, t) for t in tl]
                    TT(TF[2][:, 0:n], QM[:, c, c0:c0 + n], TF[0][:, 0:n], ALU.subtract, qmk + [("t", id(TF[0]))], [("t", id(TF[2]))])
                    TT(TF[2][:, 0:n], TF[2][:, 0:n], TF[1][:, 0:n], ALU.mult, [("t", id(TF[2])), ("t", id(TF[1]))], [("t", id(TF[2]))])
                    ACT(TF[2][:, 0:n], TF[2][:, 0:n], AF.Identity, [("t", id(TF[2])), "PAR"], [("t", id(TF[2]))], bias=lb, scale=lg)
                    sigm_recip(TF[3], TF[2][:, 0:n], n, [("t", id(TF[2]))])
                    TT(TF[2][:, 0:n], TF[2][:, 0:n], TF[3][:, 0:n], ALU.mult, [("t", id(TF[2])), ("t", id(TF[3]))], [("t", id(TF[2]))])
                    proj(wgs[c], c0, n, 0)
                    sigm_recip(TF[3], PS[0][:, 0:n], n, [("ps", 0)])
                    TT(TF[3][:, 0:n], PS[0][:, 0:n], TF[3][:, 0:n], ALU.mult, [("ps", 0), ("t", id(TF[3]))], [("t", id(TF[3]))])
                    TT(QM[:, c, c0:c0 + n], TF[2][:, 0:n], TF[3][:, 0:n], ALU.mult, [("t", id(TF[2])), ("t", id(TF[3]))], qmk)
            for j in range(4):
                next_wo(j)
            out_proj(upd)
            if l + 1 < nl:
                modulation(l + 1)
            for j in range(4):
                plain_tile(kdst, lambda c0, n: kkeys(c0, n) + ["UP"], CHUNKS)
                v_tile()
                plain_tile(qdst(j), qkeys(j), chunks)
                gate_tile(chunks)
                for pc in range(3):
                    s_ = rr["stg"] % 2; rr["stg"] += 1
                    P.dma(lambda e, s_=s_, pc=pc, j=j: e.dma_start(out=STG[:, s_, :], in_=tgd[jl, j, pc]), writes=[("STG", s_)])
                    ACT(ED[:, pc * 1024:(pc + 1) * 1024], STG[:, s_, :], AF.Exp, [("STG", s_)], ["DG"])
                blocks = []
                for p_ in list(range(16)) + ([16, 17] if upd else []):
                    kls = []
                    for s in range(2):
                        kl = [(16, None), (17, None)]
                        if p_ < 16:
                            if p_ == 0:
                                lt, i0 = [0, 1, 2, 3], 3
                            elif p_ == 1:
                                lt, i0 = [0, 1, 2, 3], 2
                            elif p_ == 14:
                                lt, i0 = [12, 13, 14, 15], 1
                            elif p_ == 15:
                                lt, i0 = [12, 13, 14, 15], 0
                            else:
                                lt, i0 = list(range(p_ - 2, p_ + 3)), 7
                            for i, kt in enumerate(lt):
                                kl.append((kt, (ET[:, s, i0 + i, :], "DG")))
                        kls.append(kl)
                    blocks.append((p_ * 128, 128, kls))
                run_blocks(j, blocks, None)
            for j in range(4):
                next_wo(j)
            out_proj(upd)

        modulation(0)
        for l in range(nl):
            upd = l < nl - 1
            mod_finish(l)
            norm_to_h()
            if l % 2 == 0:
                even_layer(l, upd)
            else:
                odd_layer(l, upd)
        fg = PAR[:, PO["fg"]:PO["fg"] + 8]
        for (c0, n) in CHUNKS[:4]:
            for kc in range(8):
                sq = TB[kc % 2]
                ACT(sq[:, 0:n], XT[:, kc, c0:c0 + n], AF.Square, [("X", kc, c0)], [("t", id(sq))])
                MM([(PS[6][:, 0:n], ONS[:, 0, :], sq[:, 0:n], kc == 0, kc == 7)], [("t", id(sq)), "ONS"], [("ps", 6)])
            ACT(TF[0][:, 0:n], PS[6][:, 0:n], AF.Ln, [("ps", 6)], [("t", id(TF[0]))], bias=EPS)
            ACT(TF[0][:, 0:n], TF[0][:, 0:n], AF.Exp, [("t", id(TF[0]))], [("t", id(TF[0]))], scale=-0.5)
            for kc in range(8):
                STT(TF[1][:, 0:n], XT[:, kc, c0:c0 + n], fg[:, kc:kc + 1], TF[0][:, 0:n], ALU.mult, ALU.mult,
                    [("X", kc, c0), "PAR", ("t", id(TF[0]))], [("t", id(TF[1]))])
                CPY(HT[:, kc, c0:c0 + n], TF[1][:, 0:n], [("t", id(TF[1]))], [("H", c0)])
                TT(QM[:, kc % 4, (kc // 4) * 512:(kc // 4) * 512 + n], TF[1][:, 0:n], HT[:, kc, c0:c0 + n], ALU.subtract,
                   [("t", id(TF[1])), ("H", c0)], [("LO", kc)])
            for ti in range(4):
                t = c0 // 128 + ti
                s = rr["stg"] % 2; rr["stg"] += 1
                for fgp in range(2):
                    bank = rr["pj"] % 2; rr["pj"] += 1
                    lst = []
                    for i in range(4):
                        kc = fgp * 4 + i
                        lo = QM[:, kc % 4, (kc // 4) * 512 + ti * 128:(kc // 4) * 512 + ti * 128 + 128]
                        lst.append((PS[bank][:, i * 128:(i + 1) * 128], HT[:, kc, t * 128:(t + 1) * 128], IDENT, True, False))
                        lst.append((PS[bank][:, i * 128:(i + 1) * 128], lo, IDENT, False, True))
                    MM(lst, [("H", c0), "CB"] + [("LO", kc) for kc in range(8)], [("ps", bank)])
                    P.act(lambda e, bank=bank, s=s, fgp=fgp: e.copy(STG[:, s, fgp * 512:(fgp + 1) * 512], PS[bank][:, :]),
                          [("ps", bank)], [("STG", s)])
                P.dma(lambda e, s=s, t=t: e.dma_start(out=yd[t * 128:(t + 1) * 128, :], in_=STG[:, s, :]), reads=[("STG", s)])
        assert rr["piece"] == npieces, (rr["piece"], npieces)
        P.emit(st)
    return nc, P


_CACHE = {}


def run(inputs, nl=4, cores=8):
    prep = _host_prep(inputs, nl)
    if nl not in _CACHE:
        _CACHE[nl] = build(nl)
    nc, _ = _CACHE[nl]
    in_maps = []
    for b in range(cores):
        in_maps.append({"x": prep["xs"][b], "par": prep["pars"][b], "w": prep["W"], "cst": prep["cst"][:, 0:768],
                        "rope": prep["rope"].reshape(4, 128, 1024), "tg": prep["tg"]})
    res = run_bass_kernel_spmd(nc, in_maps, core_ids=list(range(cores)))
    return np.stack([np.asarray(r["y"], dtype=np.float32) for r in res.results], axis=0)


def kernel(**inputs):
    inputs = {k: np.asarray(v) for k, v in inputs.items()}
    return run(inputs, 4, 8)
```

```python
import numpy as np
from contextlib import ExitStack
import concourse.bass as bass
import concourse.mybir as mybir
from concourse.bass_utils import run_bass_kernel_spmd

F32 = mybir.dt.float32
BF16 = mybir.dt.bfloat16
AF = mybir.ActivationFunctionType
ALU = mybir.AluOpType

NT = 2304
NLAT = 2048
CHUNKS = [(0, 512), (512, 512), (1024, 512), (1536, 512), (2048, 256)]
EPS = 1e-6
NDMA_SEMS = 24


class Prog:
    ENGS = ("tensor", "vector", "scalar", "gpsimd", "sync")

    def __init__(self, nc):
        self.nc = nc
        self.ops = []

    def add(self, eng, fn, reads=(), writes=(), dma=False):
        self.ops.append((eng, fn, tuple(reads), tuple(writes), dma))

    def pe(self, fn, reads=(), writes=()): self.add("tensor", fn, reads, writes)
    def dve(self, fn, reads=(), writes=()): self.add("vector", fn, reads, writes)
    def act(self, fn, reads=(), writes=()): self.add("scalar", fn, reads, writes)
    def pool(self, fn, reads=(), writes=()): self.add("gpsimd", fn, reads, writes)
    def dma(self, fn, reads=(), writes=()): self.add("sync", fn, reads, writes, dma=True)

    def emit(self, stack):
        nc = self.nc
        sems = {e: stack.enter_context(nc.semaphore("s_" + e)) for e in self.ENGS if e != "sync"}
        dsems = [stack.enter_context(nc.semaphore("d%d" % i)) for i in range(NDMA_SEMS)]
        dcount = [0] * NDMA_SEMS
        seq = {e: 0 for e in self.ENGS}
        waited = {e: {} for e in self.ENGS}
        last_w = {}
        readers = {}
        per_eng = {e: [] for e in self.ENGS}
        ndma = 0
        for (eng, fn, reads, writes, dma) in self.ops:
            psr = tuple(k for k in reads if isinstance(k, tuple) and k[0] == "ps")
            if psr:
                reads = tuple(k for k in reads if k not in psr)
                writes = tuple(writes) + tuple(k for k in psr if k not in writes)
            deps = {}

            def need(tok):
                s, v = tok
                if deps.get(s, 0) < v:
                    deps[s] = v
            for k in reads:
                if k in last_w:
                    if not (last_w[k][0] == eng == "tensor"):
                        need(last_w[k][1])
            for k in writes:
                if k in last_w and not (last_w[k][0] == eng == "tensor"):
                    need(last_w[k][1])
                for re, tok in readers.get(k, {}).items():
                    if re == eng == "tensor":
                        continue
                    need(tok)
            if dma:
                si = ndma % NDMA_SEMS
                ndma += 1
                if dcount[si] > 0:
                    need((("d", si), 16 * dcount[si]))
                dcount[si] += 1
                tok = (("d", si), 16 * dcount[si])
                inc = 16
            else:
                seq[eng] += 1
                tok = (("e", eng), seq[eng])
                inc = 1
            waits = []
            for s, v in deps.items():
                if waited[eng].get(s, 0) < v:
                    waited[eng][s] = v
                    waits.append((s, v))
            per_eng[eng].append((waits, fn, tok[0], inc))
            for k in reads:
                readers.setdefault(k, {})[eng if not dma else ("dma", ndma)] = tok
            for k in writes:
                last_w[k] = (eng if not dma else "dma", tok)
                readers[k] = {}
        final = [(("d", i), 16 * dcount[i]) for i in range(NDMA_SEMS) if dcount[i] > 0]

        def semof(s):
            return dsems[s[1]] if s[0] == "d" else sems[s[1]]

        block = stack.enter_context(nc.Block())

        def make(engname):
            def body(e):
                for waits, fn, s, inc in per_eng[engname]:
                    for ws, wv in waits:
                        e.wait_ge(semof(ws), wv)
                    ins = fn(e)
                    ins.then_inc(semof(s), inc)
                if engname == "sync":
                    for ws, wv in final:
                        e.wait_ge(semof(ws), wv)
            return body

        for engname in self.ENGS:
            if per_eng[engname] or engname == "sync":
                getattr(block, engname)(make(engname))
        self.stats = {e: len(per_eng[e]) for e in self.ENGS}


def _pair_cols(t):
    return np.concatenate([np.arange(t * 64, t * 64 + 64), np.arange((4 + t) * 64, (4 + t) * 64 + 64)])


def _swap(idx):
    return idx ^ 1


def _piece_list(nl):
    pieces = []

    def mod(l):
        for m in range(24):
            pieces.append(("mod", l, np.arange(m * 128, m * 128 + 128)))
    mod(0)
    for l in range(nl):
        jl = l // 2
        if l % 2 == 0:
            for base in (0, 1280):
                kb, vb, gb = base + 512, base + 640, base + 768
                kc = np.arange(128)
                pieces.append(("ein", jl, kb + kc))
                pieces.append(("ein", jl, kb + _swap(kc)))
                pieces.append(("ein", jl, vb + kc))
                if base == 0:
                    for t in range(4):
                        pc = _pair_cols(t)
                        pieces.append(("ein", jl, base + pc))
                        pieces.append(("ein", jl, base + _swap(pc)))
                        pieces.append(("ein", jl, gb + pc))
                else:
                    for t in range(4):
                        pc = _pair_cols(t)
                        pieces.append(("ein", jl, base + pc))
                        pieces.append(("ein", jl, base + _swap(pc)))
                    for t in range(4):
                        pieces.append(("ein", jl, gb + _pair_cols(t)))
                for t in range(4):
                    pieces.append(("eout", jl, (0 if base == 0 else 512) + _pair_cols(t)))
                if base == 0 and l + 1 < nl:
                    mod(l + 1)
        else:
            for c in range(4):
                pieces.append(("oin", jl, 0 + c * 128 + np.arange(128)))
                pieces.append(("oin", jl, 512 + c * 128 + np.arange(128)))
            for c in range(4):
                pieces.append(("oin", jl, 1024 + c * 128 + np.arange(128)))
            for c in range(4):
                pieces.append(("oout", jl, c * 128 + np.arange(128)))
            if l + 1 < nl:
                mod(l + 1)
            for t in range(4):
                pieces.append(("oin", jl, 2048 + t * 128 + np.arange(128)))
                pieces.append(("oin", jl, 2560 + t * 128 + np.arange(128)))
                pieces.append(("oin", jl, 1536 + t * 128 + np.arange(128)))
                pieces.append(("oin", jl, 3072 + t * 128 + np.arange(128)))
            for t in range(4):
                pieces.append(("oout", jl, 512 + t * 128 + np.arange(128)))
    return pieces


def _host_prep(inp, nl):
    f32 = np.float32
    pieces = _piece_list(nl)
    W = np.empty((len(pieces), 128, 1024), f32)
    for i, (kind, l, idx) in enumerate(pieces):
        if kind == "mod":
            w = inp["mod_w"][l][:, idx]
            W[i] = w.reshape(8, 128, 128).transpose(1, 0, 2).reshape(128, 1024)
        elif kind == "ein":
            w = inp["ev_w_in"][l][:, idx]
            W[i] = w.reshape(8, 128, 128).transpose(1, 0, 2).reshape(128, 1024)
        elif kind == "oin":
            w = inp["od_w_in"][l][:, idx]
            W[i] = w.reshape(8, 128, 128).transpose(1, 0, 2).reshape(128, 1024)
        elif kind == "eout":
            W[i] = inp["ev_w_out"][l][idx, :]
        else:
            W[i] = inp["od_w_out"][l][idx, :]
    cst = np.zeros((128, 128 * 4 + 256), f32)
    cst[:, 0:128] = np.eye(128, dtype=f32)
    sh = np.zeros((128, 128), f32)
    sh[np.arange(64), 64 + np.arange(64)] = 1.0
    cst[:, 128:256] = sh
    bo = np.zeros((128, 128), f32)
    bo[:64, :64] = 1.0 / 64
    bo[64:, 64:] = 1.0 / 64
    cst[:, 256:384] = bo
    cst[:, 384:512] = 1.0
    jj = np.arange(128)[:, None]
    ii = np.arange(128)[None, :]
    cst[:, 512:640] = (jj >= ii).astype(f32)
    cst[:, 640:768] = (jj <= ii).astype(f32)
    t = np.arange(NLAT)
    row = (t // 64).astype(f32)
    col = (t % 64).astype(f32)
    half = 32
    freqs = (f32(10000.0) ** (-np.arange(0, half, 2, dtype=f32) / f32(half))).astype(f32)
    ang = np.concatenate([row[:, None] * freqs, col[:, None] * freqs], axis=-1).astype(f32)
    cos, sin = np.cos(ang).astype(f32), np.sin(ang).astype(f32)
    d = np.arange(128) % 64
    C = cos[:, d // 2].T
    S = sin[:, d // 2].T * np.where(d % 2 == 0, -1.0, 1.0)[:, None].astype(f32)
    rope = np.empty((4, 128, 2, 512), f32)
    for c in range(4):
        rope[c, :, 0, :] = C[:, c * 512:(c + 1) * 512]
        rope[c, :, 1, :] = S[:, c * 512:(c + 1) * 512]
    n_odd = max(1, nl // 2)
    tg = np.full((n_odd, 4, 3, 128, 1024), -1e30, f32)
    variants = [(dl, False) for dl in range(-3, 4)] + [(-2, True), (-1, False), (0, False), (1, False), (2, True)]
    kk = np.arange(128)
    rkl, ck = kk // 64, kk % 64
    rl, cq = kk // 64, kk % 64
    cs = np.clip(cq - 8, 0, 48)
    colok = (ck[:, None] >= cs[None, :]) & (ck[:, None] < cs[None, :] + 16)
    dc = np.clip(ck[:, None] - cq[None, :] + 15, 0, 30)
    for jl in range(nl // 2):
        rpb = inp["d_rpb"][jl]
        for pr in range(4):
            for s in range(2):
                h = 2 * pr + s
                for vi, (dl, msk) in enumerate(variants):
                    dr = 2 * dl + rkl[:, None] - rl[None, :]
                    ok = colok & (np.abs(dr) <= 7)
                    if msk:
                        ok = ok & (dr >= -4) & (dr <= 3)
                    g = rpb[h][np.clip(dr + 7, 0, 14), dc]
                    blk = np.where(ok, g, f32(-1e30)).astype(f32)
                    q = s * 12 + vi
                    tg[jl, pr, q // 8, :, (q % 8) * 128:(q % 8) * 128 + 128] = blk
    def fm(v):
        return np.asarray(v, f32).reshape(8, 128).T
    pars = []
    for b in range(8):
        cols = [fm(inp["c"][b]), fm(inp["c_ctx"])]
        for l in range(4):
            cols.append(fm(inp["norm_g"][l]))
            cols.append(np.asarray(inp["mod_b"][l], f32).reshape(24, 128).T)
        for jl in range(2):
            for nm in ("a_q_gain", "a_k_gain"):
                gq = np.asarray(inp[nm][jl], f32)
                cols.append(gq[d][:, None])
                cols.append(gq[d ^ 1][:, None])
            cols.append(np.broadcast_to(np.asarray(inp["b_sink"][jl], f32)[None, :], (128, 8)))
        for jl in range(2):
            dw = np.asarray(inp["c_dw_w"][jl], f32)
            cols.append(dw.reshape(31, 4, 128).transpose(2, 1, 0).reshape(128, 124))
            for nm in ("c_dw_b", "c_ln_g", "c_ln_b"):
                cols.append(np.asarray(inp[nm][jl], f32).reshape(4, 128).T)
        cols.append(fm(inp["final_g"]))
        pars.append(np.ascontiguousarray(np.concatenate(cols, axis=1)))
    xs = [np.ascontiguousarray(np.concatenate([inp["x"][b], inp["ctx"][b]], axis=0)) for b in range(8)]
    return dict(W=W, cst=cst, rope=rope, tg=tg, pars=pars, xs=xs, npar=pars[0].shape[1])


def _par_off():
    o = {}
    p = 0
    o["c"] = p; p += 8
    o["cctx"] = p; p += 8
    for l in range(4):
        o["ng", l] = p; p += 8
        o["mb", l] = p; p += 24
    for jl in range(2):
        o["qg", jl] = p; p += 1
        o["qgs", jl] = p; p += 1
        o["kg", jl] = p; p += 1
        o["kgs", jl] = p; p += 1
        o["sink", jl] = p; p += 8
    for jl in range(2):
        o["dww", jl] = p; p += 124
        o["dwb", jl] = p; p += 4
        o["lng", jl] = p; p += 4
        o["lnb", jl] = p; p += 4
    o["fg"] = p; p += 8
    o["n"] = p
    return o


def build(nl=4):
    nc = bass.Bass("TRN2", target_bir_lowering=False)
    PO = _par_off()
    npieces = len(_piece_list(nl))
    n_odd = max(1, nl // 2)
    xd = nc.dram_tensor("x", [NT, 1024], F32, kind="ExternalInput").ap()
    pard = nc.dram_tensor("par", [128, PO["n"]], F32, kind="ExternalInput").ap()
    wd = nc.dram_tensor("w", [npieces, 128, 1024], F32, kind="ExternalInput").ap()
    cstd = nc.dram_tensor("cst", [128, 768], F32, kind="ExternalInput").ap()
    roped = nc.dram_tensor("rope", [4, 128, 1024], F32, kind="ExternalInput").ap()
    tgd = nc.dram_tensor("tg", [n_odd, 4, 3, 128, 1024], F32, kind="ExternalInput").ap()
    yd = nc.dram_tensor("y", [NLAT, 1024], F32, kind="ExternalOutput").ap()

    st = ExitStack()
    with st:
        def SB(name, shape, dt):
            return st.enter_context(nc.sbuf_tensor(name, shape, dt))
        XT = SB("XT", [128, 8, NT], F32)
        HT = SB("HT", [128, 8, NT], BF16)
        QM = SB("QM", [128, 4, NT], BF16)
        KG = SB("KG", [128, 2, NT], BF16)
        VA = SB("VA", [128, 18, 2, 65], BF16)
        PT = SB("PT", [128, 6, 512], BF16)
        STG = SB("STG", [128, 2, 1024], F32)
        WB = SB("WB", [128, 4, 1024], BF16)
        WO = SB("WO", [128, 4, 1024], BF16)
        RCS = SB("RCS", [128, 2, 1024], F32)
        ED = SB("ED", [128, 3968], BF16)
        CB = SB("CB", [128, 768], BF16)
        ONS = SB("ONS", [128, 2, 128], BF16)
        PAR = SB("PAR", [128, PO["n"]], F32)
        MODW = SB("MODW", [128, 24, 2], F32)
        MCO = SB("MCO", [128, 3, 8, 2], F32)
        CS = SB("CS", [128, 8, 2], BF16)
        ESK = SB("ESK", [128, 8], F32)
        TF = [SB("TF%d" % i, [128, 512], F32) for i in range(4)]
        TB = [SB("TB%d" % i, [128, 512], BF16) for i in range(2)]
        RW = SB("RW", [128, 512], F32)
        RH = SB("RH", [128, 512], BF16)
        RL = SB("RL", [128, 512], BF16)
        OS = SB("OS", [64, 512], F32)
        ON = [SB("ON%d" % i, [64, 512], BF16) for i in range(2)]
        PS = [st.enter_context(nc.psum_tensor("PS%d" % i, [128, 512], F32)) for i in range(8)]

        KTt = KG[:, 0, :]
        GTt = KG[:, 1, :]
        UP = KG[:].rearrange("p a b -> p (a b)")
        VA36 = VA[:].rearrange("p t g d -> p (t g) d")
        IDENT = CB[:, 0:128]
        SHIFT = CB[:, 128:256]
        BONES = CB[:, 256:384]
        ONES = CB[:, 384:512]
        TRIL = CB[:, 512:640]
        TRIU = CB[:, 640:768]
        DG = ED[:, 0:3968].rearrange("p (k m) -> p k m", k=31)
        ET = ED[:, 0:3072].rearrange("p (s i q) -> p s i q", s=2, i=12)

        P = Prog(nc)
        rr = {"stg": 0, "wb": 0, "pt": 0, "sb": 0, "rcs": 0, "pj": 0, "piece": 0}

        def ACT(out, in_, func, reads, writes, **kw):
            P.act(lambda e: e.activation(out, in_, func, **kw), reads, writes)

        def TT(out, a, b, op, reads, writes, eng="vector"):
            P.add(eng, lambda e: e.tensor_tensor(out, a, b, op), reads, writes)

        def STT(out, in0, scalar, in1, op0, op1, reads, writes):
            P.dve(lambda e: e.scalar_tensor_tensor(out, in0, scalar, in1, op0, op1), reads, writes)

        def TS(out, in0, s1, s2, op0, op1, reads, writes):
            if op1 is None:
                P.dve(lambda e: e.tensor_scalar(out, in0, s1, None, op0), reads, writes)
            else:
                P.dve(lambda e: e.tensor_scalar(out, in0, s1, s2, op0, op1), reads, writes)

        def CPY(out, in_, reads, writes, eng="vector"):
            P.add(eng, lambda e: e.tensor_copy(out, in_), reads, writes)

        def RCP(out, in_, reads, writes, bias=0.0):
            ACT(out, in_, AF.Ln, reads, writes, bias=bias)
            ACT(out, out, AF.Exp, writes, writes, scale=-1.0)

        def MM(lst, reads, writes):
            def f(e):
                r = None
                for (o, l, rh, s0, s1) in lst:
                    r = e.matmul(o, l, rh, start=s0, stop=s1)
                return r
            P.pe(f, reads, writes)

        def tiles(c0, n):
            return list(range(c0 // 128, (c0 + n) // 128))

        def next_piece(eng="gpsimd"):
            i = rr["piece"]; rr["piece"] += 1
            s = rr["stg"] % 2; rr["stg"] += 1
            w = rr["wb"] % 4; rr["wb"] += 1
            P.dma(lambda e: e.dma_start(out=STG[:, s, :], in_=wd[i]), writes=[("STG", s)])
            CPY(WB[:, w, :], STG[:, s, :], [("STG", s)], [("WB", w)], eng=eng)
            return w

        def next_wo(j):
            i = rr["piece"]; rr["piece"] += 1
            s = rr["stg"] % 2; rr["stg"] += 1
            P.dma(lambda e: e.dma_start(out=STG[:, s, :], in_=wd[i]), writes=[("STG", s)])
            CPY(WO[:, j, :], STG[:, s, :], [("STG", s)], [("WO", j)], eng="gpsimd")

        def proj(w, c0, n, bank):
            wv = WB[:, w, :].rearrange("p (k m) -> p k m", k=8)
            MM([(PS[bank][:, 0:n], wv[:, kc, :], HT[:, kc, c0:c0 + n], kc == 0, kc == 7) for kc in range(8)],
               [("WB", w), ("H", c0)], [("ps", bank)])

        def sigm_recip(dst, src_ps, n, rkeys):
            ACT(dst[:, 0:n], src_ps, AF.Exp, rkeys, [("t", id(dst))], scale=-1.0)
            RCP(dst[:, 0:n], dst[:, 0:n], [("t", id(dst))], [("t", id(dst))], bias=1.0)

        P.dma(lambda e: e.dma_start(out=PAR[:], in_=pard), writes=["PAR"])
        P.dma(lambda e: e.dma_start(out=STG[:, 0, 0:768], in_=cstd), writes=[("STG", 0)])
        CPY(CB[:], STG[:, 0, 0:768], [("STG", 0)], ["CB"])
        rr["stg"] = 1
        P.pool(lambda e: e.memset(ONS[:, 0, :], 1.0 / 1024), writes=["ONS"])
        P.pool(lambda e: e.memset(ONS[:, 1, :], 1.0 / 512), writes=["ONS"])
        P.pool(lambda e: e.memset(VA[:], 1.0), writes=[("V", t) for t in range(18)])
        for q, key in enumerate(("c", "cctx")):
            src = PAR[:, PO[key]:PO[key] + 8]
            ACT(TF[0][:, 0:8], src, AF.Exp, ["PAR"], [("t", id(TF[0]))], scale=-1.0)
            RCP(TF[0][:, 0:8], TF[0][:, 0:8], [("t", id(TF[0]))], [("t", id(TF[0]))], bias=1.0)
            TT(CS[:, :, q], src, TF[0][:, 0:8], ALU.mult, [("t", id(TF[0])), "PAR"], ["CS"])
        for t in range(18):
            s = rr["stg"] % 2; rr["stg"] += 1
            P.dma(lambda e, t=t, s=s: e.dma_start(out=STG[:, s, :], in_=xd[t * 128:(t + 1) * 128, :]), writes=[("STG", s)])
            hi, lo = WB[:, 2 * (t % 2), :], WB[:, 2 * (t % 2) + 1, :]
            kh, kl = ("WB", 2 * (t % 2)), ("WB", 2 * (t % 2) + 1)
            CPY(hi, STG[:, s, :], [("STG", s)], [kh])
            TT(lo, STG[:, s, :], hi, ALU.subtract, [("STG", s), kh], [kl])
            for fg in range(2):
                bank = rr["pj"] % 2; rr["pj"] += 1
                lst = []
                for i in range(4):
                    f = fg * 4 + i
                    lst.append((PS[bank][:, i * 128:(i + 1) * 128], hi[:, f * 128:(f + 1) * 128], IDENT, True, False))
                    lst.append((PS[bank][:, i * 128:(i + 1) * 128], lo[:, f * 128:(f + 1) * 128], IDENT, False, True))
                MM(lst, [kh, kl, "CB"], [("ps", bank)])
                cc = min(t // 4, 4)
                P.act(lambda e, bank=bank, fg=fg, t=t: e.copy(XT[:, fg * 4:fg * 4 + 4, t * 128:(t + 1) * 128],
                                                                PS[bank][:, :].rearrange("p (a b) -> p a b", a=4)),
                      [("ps", bank)], [("X", kc, CHUNKS[cc][0]) for kc in range(fg * 4, fg * 4 + 4)])

        def modulation(l):
            for m in range(24):
                w = next_piece(eng="vector")
                wv = WB[:, w, :].rearrange("p (k m) -> p k m", k=8)
                MM([(PS[6][:, 0:2], wv[:, kc, :], CS[:, kc, :], kc == 0, kc == 7) for kc in range(8)],
                   [("WB", w), "CS"], [("ps", 6)])
                TS(MODW[:, m, :], PS[6][:, 0:2], PAR[:, PO["mb", l] + m:PO["mb", l] + m + 1], None, ALU.add, None,
                   [("ps", 6), "PAR"], ["MODW"])

        def mod_finish(l):
            ng = PAR[:, PO["ng", l]:PO["ng", l] + 8]
            for q in range(2):
                TS(MCO[:, 0, :, q], MODW[:, 8:16, q], 1.0, None, ALU.add, None, ["MODW"], ["MCO"])
                TT(MCO[:, 0, :, q], MCO[:, 0, :, q], ng, ALU.mult, ["MCO", "PAR"], ["MCO"])
                CPY(MCO[:, 1, :, q], MODW[:, 0:8, q], ["MODW"], ["MCO"])
                CPY(MCO[:, 2, :, q], MODW[:, 16:24, q], ["MODW"], ["MCO"])

        def norm_to_h():
            rbufs = [TF[0], TF[3]]

            def stage_a(ci):
                c0, n = CHUNKS[ci]
                rb = rbufs[ci % 2]
                bank = 6 + ci % 2
                for kc in range(8):
                    sq = TB[kc % 2]
                    ACT(sq[:, 0:n], XT[:, kc, c0:c0 + n], AF.Square, [("X", kc, c0)], [("t", id(sq))])
                    MM([(PS[bank][:, 0:n], ONS[:, 0, :], sq[:, 0:n], kc == 0, kc == 7)], [("t", id(sq)), "ONS"], [("ps", bank)])
                ACT(rb[:, 0:n], PS[bank][:, 0:n], AF.Ln, [("ps", bank)], [("t", id(rb))], bias=EPS)
                ACT(rb[:, 0:n], rb[:, 0:n], AF.Exp, [("t", id(rb))], [("t", id(rb))], scale=-0.5)

            def stage_b(ci):
                c0, n = CHUNKS[ci]
                rb = rbufs[ci % 2]
                q = 0 if c0 < NLAT else 1
                for kc in range(8):
                    tmp = TF[1 + kc % 2]
                    STT(tmp[:, 0:n], XT[:, kc, c0:c0 + n], MCO[:, 0, kc, q:q + 1], rb[:, 0:n], ALU.mult, ALU.mult,
                        [("X", kc, c0), "MCO", ("t", id(rb))], [("t", id(tmp))])
                    ACT(HT[:, kc, c0:c0 + n], tmp[:, 0:n], AF.Identity, [("t", id(tmp)), "MCO"], [("H", c0)],
                        bias=MCO[:, 1, kc, q:q + 1], scale=1.0)
            stage_a(0)
            for ci in range(len(CHUNKS)):
                if ci + 1 < len(CHUNKS):
                    stage_a(ci + 1)
                stage_b(ci)

        def qk_tile(dst_fn, dkeys_fn, use_norm, gain_ap, gain_sw_ap, chunks):
            w = next_piece()
            ws = next_piece()
            for (c0, n) in chunks:
                lat = c0 < NLAT
                b0 = 2 * (rr["pj"] % 2); rr["pj"] += 1
                b1 = b0 + 1
                proj(w, c0, n, b0)
                if lat:
                    proj(ws, c0, n, b1)
                if use_norm:
                    ACT(TB[0][:, 0:n], PS[b0][:, 0:n], AF.Square, [("ps", b0)], [("t", id(TB[0]))])
                    MM([(PS[6][:, 0:n], BONES, TB[0][:, 0:n], True, True)], [("t", id(TB[0])), "CB"], [("ps", 6)])
                    ACT(TF[0][:, 0:n], PS[6][:, 0:n], AF.Ln, [("ps", 6)], [("t", id(TF[0]))], bias=EPS)
                    ACT(TF[0][:, 0:n], TF[0][:, 0:n], AF.Exp, [("t", id(TF[0]))], [("t", id(TF[0]))], scale=-0.5)
                    STT(TF[1][:, 0:n], PS[b0][:, 0:n], gain_ap, TF[0][:, 0:n], ALU.mult, ALU.mult,
                        [("ps", b0), "PAR", ("t", id(TF[0]))], [("t", id(TF[1]))])
                    if lat:
                        STT(TF[2][:, 0:n], PS[b1][:, 0:n], gain_sw_ap, TF[0][:, 0:n], ALU.mult, ALU.mult,
                            [("ps", b1), "PAR", ("t", id(TF[0]))], [("t", id(TF[2]))])
                    a_src, b_src = TF[1][:, 0:n], TF[2][:, 0:n]
                    akeys, bkeys = [("t", id(TF[1]))], [("t", id(TF[2]))]
                else:
                    a_src, b_src = PS[b0][:, 0:n], PS[b1][:, 0:n]
                    akeys, bkeys = [("ps", b0)], [("ps", b1)]
                if lat:
                    rs = rr["rcs"] % 2; rr["rcs"] += 1
                    ci = c0 // 512
                    P.dma(lambda e, rs=rs, ci=ci: e.dma_start(out=RCS[:, rs, :], in_=roped[ci]), writes=[("RCS", rs)])
                    TT(TF[1][:, 0:n], a_src, RCS[:, rs, 0:n], ALU.mult, akeys + [("RCS", rs)], [("t", id(TF[1]))])
                    TT(TF[2][:, 0:n], b_src, RCS[:, rs, 512:512 + n], ALU.mult, bkeys + [("RCS", rs)], [("t", id(TF[2]))])
                    TT(dst_fn(c0, n), TF[1][:, 0:n], TF[2][:, 0:n], ALU.add,
                       [("t", id(TF[1])), ("t", id(TF[2]))], dkeys_fn(c0, n))
                else:
                    if use_norm:
                        CPY(dst_fn(c0, n), a_src, akeys, dkeys_fn(c0, n))
                    else:
                        P.act(lambda e, c0=c0, n=n, b0=b0: e.copy(dst_fn(c0, n), PS[b0][:, 0:n]), [("ps", b0)], dkeys_fn(c0, n))

        def plain_tile(dst_fn, dkeys_fn, chunks):
            w = next_piece()
            for (c0, n) in chunks:
                bank = rr["pj"] % 2; rr["pj"] += 1
                proj(w, c0, n, bank)
                P.act(lambda e, c0=c0, n=n, bank=bank: e.copy(dst_fn(c0, n), PS[bank][:, 0:n]), [("ps", bank)], dkeys_fn(c0, n))

        def v_tile():
            w = next_piece()
            wv = WB[:, w, :].rearrange("p (k m) -> p k m", k=8)
            for g4 in range(5):
                tl = list(range(g4 * 4, min(g4 * 4 + 4, 18)))
                bank = rr["pj"] % 2; rr["pj"] += 1
                lst = []
                for i, t in enumerate(tl):
                    for kc in range(8):
                        lst.append((PS[bank][:, i * 128:(i + 1) * 128], HT[:, kc, t * 128:(t + 1) * 128], wv[:, kc, :], kc == 0, kc == 7))
                MM(lst, [("WB", w)] + [("H", CHUNKS[min(t // 4, 4)][0]) for t in tl], [("ps", bank)])
                nt_ = len(tl)
                P.act(lambda e, bank=bank, tl=tl, nt_=nt_: e.copy(
                    VA36[:, 2 * tl[0]:2 * tl[0] + 2 * nt_, 0:64],
                    PS[bank][:, 0:nt_ * 128].rearrange("p (a b) -> p a b", b=64)),
                    [("ps", bank)], [("V", t) for t in tl])

        def gate_tile(chunks, mul_into=None):
            w = next_piece()
            for (c0, n) in chunks:
                bank = rr["pj"] % 2; rr["pj"] += 1
                proj(w, c0, n, bank)
                sigm_recip(TF[3], PS[bank][:, 0:n], n, [("ps", bank)])
                if mul_into is None:
                    TT(GTt[:, c0:c0 + n], PS[bank][:, 0:n], TF[3][:, 0:n], ALU.mult,
                       [("ps", bank), ("t", id(TF[3]))], [("G", t) for t in tiles(c0, n)])
                else:
                    j = mul_into
                    qk_ = [("Q", j, t) for t in tiles(c0, n)]
                    TT(TF[3][:, 0:n], PS[bank][:, 0:n], TF[3][:, 0:n], ALU.mult,
                       [("ps", bank), ("t", id(TF[3]))], [("t", id(TF[3]))])
                    TT(QM[:, j, c0:c0 + n], QM[:, j, c0:c0 + n], TF[3][:, 0:n], ALU.mult, qk_ + [("t", id(TF[3]))], qk_)

        def slot(nq, idx):
            if nq == 128:
                return idx * 128, [idx]
            return 0, list(range((nq + 127) // 128))

        def attend_pair(j, c0, nq, kts, g, quad=False):
            if quad:
                qk = [("Q", i, t) for i in range(4) for t in tiles(c0, 128)]
                qrhs = [QM[64 * s:64 * s + 64, 0:4, c0:c0 + 128] for s in range(2)]
                nq = 512
            else:
                qk = [("Q", j, t) for t in tiles(c0, nq)]
                qrhs = [QM[64 * s:64 * s + 64, j, c0:c0 + nq] for s in range(2)]
            G = 512 // nq
            nk = len(kts[0])
            groups = [(g0, min(g0 + G, nk)) for g0 in range(0, nk, G)]
            obank = [4 + 2 * g, 5 + 2 * g]
            ocol = [0, 0]
            okeys = [[("ps", obank[0])], [("ps", obank[1])]]
            sbank = {}

            def emitS(gi):
                a, b = groups[gi]
                for s in range(2):
                    sb = rr["sb"] % 4; rr["sb"] += 1
                    sbank[gi, s] = sb
                    grp = kts[s][a:b]
                    MM([(PS[sb][:, i * nq:(i + 1) * nq], KTt[64 * s:64 * s + 64, kt * 128:(kt + 1) * 128],
                         qrhs[s], True, True) for i, (kt, _m) in enumerate(grp)],
                       [("K", kt) for kt, _m in grp] + qk, [("ps", sb)])
            emitS(0)
            for gi, (a, b) in enumerate(groups):
                if gi + 1 < len(groups):
                    emitS(gi + 1)
                pts = []
                for s in range(2):
                    sb = sbank[gi, s]
                    pt = rr["pt"] % 6; rr["pt"] += 1
                    pts.append(pt)
                    wdt = (b - a) * nq
                    ACT(PT[:, pt, 0:wdt], PS[sb][:, 0:wdt], AF.Exp, [("ps", sb)], [("PT", pt)], scale=0.125)
                for s in range(2):
                    pt = pts[s]
                    grp = kts[s][a:b]
                    i = 0
                    while i < len(grp):
                        m = grp[i][1]
                        if m is None:
                            i += 1
                            continue
                        if len(m) == 3:
                            r = 1
                            while i + r < len(grp) and grp[i + r][1] is not None and grp[i + r][1][2] == m[2] + r:
                                r += 1
                            ptv = PT[:, pt, i * nq:(i + r) * nq]
                            TT(ptv, ptv, m[0](m[2], r), ALU.mult, [("PT", pt), m[1]], [("PT", pt)])
                            i += r
                            continue
                        ptv = PT[:, pt, i * nq:(i + 1) * nq]
                        if quad:
                            ptv = ptv.rearrange("p (a b) -> p a b", a=4)
                        TT(ptv, ptv, m[0], ALU.mult, [("PT", pt), m[1]], [("PT", pt)])
                        i += 1
                for s in range(2):
                    pt = pts[s]
                    grp = kts[s][a:b]
                    MM([(PS[obank[s]][0:65, ocol[s]:ocol[s] + nq], VA[:, kt, s, 0:65], PT[:, pt, i * nq:(i + 1) * nq],
                         a + i == 0, a + i == nk - 1) for i, (kt, _m) in enumerate(grp)],
                       [("PT", pt)] + [("V", kt) for kt, _m in grp], okeys[s])

        def attn_epilogue(j, c0, nq, sink_heads, g):
            obank = [4 + 2 * g, 5 + 2 * g]
            m0, mq = slot(nq, g)
            for s in range(2):
                o0 = 0
                okeys = [("ps", obank[s])]
                bb = rr["sb"] % 4; rr["sb"] += 1
                r0, rq = slot(nq, 2 * g + s)
                rwk = [("RW", q) for q in rq]
                rhk = [("RH", q) for q in rq]
                rlk = [("RL", q) for q in rq]
                b6k = [("ps", bb)]
                osk = [("OS", q) for q in rq]
                onk = [("ON", s, q) for q in mq]
                Osum = PS[obank[s]][64:65, o0:o0 + nq]
                bias = ESK[64:65, sink_heads[s]:sink_heads[s] + 1] if sink_heads is not None else 0.0
                RCP(RW[64:65, r0:r0 + nq], Osum, okeys + (["ESK"] if sink_heads is not None else []), rwk, bias=bias)
                CPY(RH[64:65, r0:r0 + nq], RW[64:65, r0:r0 + nq], rwk, rhk)
                TT(RL[64:65, r0:r0 + nq], RW[64:65, r0:r0 + nq], RH[64:65, r0:r0 + nq], ALU.subtract, rwk + rhk, rlk)
                MM([(PS[bb][0:64, 0:nq], ONES[64:65, 0:64], RH[64:65, r0:r0 + nq], True, False),
                    (PS[bb][0:64, 0:nq], ONES[64:65, 0:64], RL[64:65, r0:r0 + nq], False, True)], rhk + rlk + ["CB"], b6k)
                P.act(lambda e, s=s, o0=o0, r0=r0: e.copy(OS[0:64, r0:r0 + nq], PS[obank[s]][0:64, o0:o0 + nq]), okeys, osk)
                TT(ON[s][:, m0:m0 + nq], OS[0:64, r0:r0 + nq], PS[bb][0:64, 0:nq], ALU.mult, osk + b6k, onk)
            mb = rr["sb"] % 4; rr["sb"] += 1
            m7k = [("ps", mb)]
            MM([(PS[mb][:, 0:nq], IDENT[0:64, :], ON[0][:, m0:m0 + nq], True, False),
                (PS[mb][:, 0:nq], SHIFT[0:64, :], ON[1][:, m0:m0 + nq], False, True)],
               [("ON", 0, q) for q in mq] + [("ON", 1, q) for q in mq] + ["CB"], m7k)
            tl = tiles(c0, nq)
            TT(QM[:, j, c0:c0 + nq], PS[mb][:, 0:nq], GTt[:, c0:c0 + nq], ALU.mult,
               m7k + [("G", t) for t in tl], [("Q", j, t) for t in tl])

        def quad_epilogue(c0, g):
            qv = lambda ap: ap.rearrange("p (a b) -> p a b", a=4)
            tl = tiles(c0, 128)
            qkeys_ = [("Q", i, t) for i in range(4) for t in tl]
            for s in range(2):
                ob = 4 + 2 * g + s
                okeys = [("ps", ob)]
                bb = rr["sb"] % 4; rr["sb"] += 1
                for i in range(4):
                    h = 4 * s + i
                    ACT(RW[64:65, i * 128:(i + 1) * 128], PS[ob][64:65, i * 128:(i + 1) * 128], AF.Ln,
                        okeys + ["ESK"], ["RWq"], bias=ESK[64:65, h:h + 1])
                ACT(RW[64:65, 0:512], RW[64:65, 0:512], AF.Exp, ["RWq"], ["RWq"], scale=-1.0)
                CPY(RH[64:65, 0:512], RW[64:65, 0:512], ["RWq"], ["RHq"])
                TT(RL[64:65, 0:512], RW[64:65, 0:512], RH[64:65, 0:512], ALU.subtract, ["RWq", "RHq"], ["RLq"])
                MM([(PS[bb][0:64, 0:512], ONES[64:65, 0:64], RH[64:65, 0:512], True, False),
                    (PS[bb][0:64, 0:512], ONES[64:65, 0:64], RL[64:65, 0:512], False, True)], ["RHq", "RLq", "CB"], [("ps", bb)])
                P.act(lambda e, ob=ob: e.copy(OS[0:64, 0:512], PS[ob][0:64, 0:512]), okeys, ["OSq"])
                if s == 0:
                    TT(QM[0:64, 0:4, c0:c0 + 128], qv(OS[0:64, 0:512]), qv(PS[bb][0:64, 0:512]), ALU.mult,
                       ["OSq", ("ps", bb)], qkeys_)
                else:
                    TT(ON[1][:, 0:512], OS[0:64, 0:512], PS[bb][0:64, 0:512], ALU.mult, ["OSq", ("ps", bb)], ["ONq"])
                    mb = rr["sb"] % 4; rr["sb"] += 1
                    MM([(PS[mb][:, 0:512], SHIFT[0:64, :], ON[1][:, 0:512], True, True)], ["ONq", "CB"], [("ps", mb)])
                    P.act(lambda e, mb=mb: e.copy(QM[64:128, 0:4, c0:c0 + 128], qv(PS[mb][64:128, 0:512])), [("ps", mb)], qkeys_)

        def run_quad_blocks(blocks):
            pending = None
            for bi, (c0, kts) in enumerate(blocks):
                g = bi % 2
                attend_pair(0, c0, 128, kts, g, quad=True)
                if pending is not None:
                    pending()
                pending = (lambda c0=c0, g=g: quad_epilogue(c0, g))
            if pending is not None:
                pending()

        def run_blocks(j, blocks, sink_heads):
            pending = None
            for bi, (c0, nq, kts) in enumerate(blocks):
                g = bi % 2 if nq == 128 else 0
                attend_pair(j, c0, nq, kts, g)
                if pending is not None:
                    pending()
                    pending = None
                if nq == 128:
                    pending = (lambda c0=c0, nq=nq, g=g: attn_epilogue(j, c0, nq, sink_heads, g))
                else:
                    attn_epilogue(j, c0, nq, sink_heads, g)
            if pending is not None:
                pending()

        def out_proj(upd_ctx):
            for (c0, n) in CHUNKS:
                if c0 >= NLAT and not upd_ctx:
                    continue
                q = 0 if c0 < NLAT else 1
                for m in range(8):
                    bank = rr["pj"] % 2; rr["pj"] += 1
                    MM([(PS[bank][:, 0:n], WO[:, j, m * 128:(m + 1) * 128], QM[:, j, c0:c0 + n], j == 0, j == 3) for j in range(4)],
                       [("WO", j) for j in range(4)] + [("Q", j, t) for j in range(4) for t in tiles(c0, n)], [("ps", bank)])
                    STT(XT[:, m, c0:c0 + n], PS[bank][:, 0:n], MCO[:, 2, m, q:q + 1], XT[:, m, c0:c0 + n], ALU.mult, ALU.add,
                        [("ps", bank), "MCO", ("X", m, c0)], [("X", m, c0)])

        QCH = lambda upd: [ch for ch in CHUNKS if ch[0] < NLAT or upd]
        qdst = lambda j: (lambda c0, n: QM[:, j, c0:c0 + n])
        qkeys = lambda j: (lambda c0, n: [("Q", j, t) for t in tiles(c0, n)])
        kdst = lambda c0, n: KTt[:, c0:c0 + n]
        kkeys = lambda c0, n: [("K", t) for t in tiles(c0, n)]

        def even_layer(l, upd):
            jl = l // 2
            ACT(ESK[:], PAR[:, PO["sink", jl]:PO["sink", jl] + 8], AF.Exp, ["PAR"], ["ESK"])
            for mixer in range(2):
                isA = mixer == 0
                kg = PAR[:, PO["kg", jl]:PO["kg", jl] + 1]
                kgs = PAR[:, PO["kgs", jl]:PO["kgs", jl] + 1]
                qg = PAR[:, PO["qg", jl]:PO["qg", jl] + 1]
                qgs = PAR[:, PO["qgs", jl]:PO["qgs", jl] + 1]
                qk_tile(kdst, kkeys, isA, kg, kgs, CHUNKS)
                v_tile()
                if isA:
                    for j in range(4):
                        qk_tile(qdst(j), qkeys(j), True, qg, qgs, QCH(upd))
                        gate_tile(QCH(upd))
                        blocks = []
                        for (c0, n) in QCH(upd):
                            kl = [(kt, None) for kt in ([16, 17] + list(range(16)) if c0 < NLAT else [16, 17])]
                            blocks.append((c0, n, [kl, kl]))
                        run_blocks(j, blocks, None)
                else:
                    for j in range(4):
                        qk_tile(qdst(j), qkeys(j), False, qg, qgs, QCH(upd))
                    tril4 = (CB[:, 512:640].unsqueeze(1).broadcast_to((128, 4, 128)), "CB")
                    triu4 = (CB[:, 640:768].unsqueeze(1).broadcast_to((128, 4, 128)), "CB")
                    blocks = []
                    for qb in list(range(16)) + ([16, 17] if upd else []):
                        kl = [(16, None), (17, None)]
                        if qb < 16:
                            if qb > 0:
                                kl.append((qb - 1, tril4))
                            kl.append((qb, None))
                            if qb < 15:
                                kl.append((qb + 1, triu4))
                        blocks.append((qb * 128, [kl, kl]))
                    run_quad_blocks(blocks)
                    for j in range(4):
                        gate_tile(QCH(upd), mul_into=j)
                for j in range(4):
                    next_wo(j)
                out_proj(upd)
                if mixer == 0 and l + 1 < nl:
                    modulation(l + 1)

        def odd_layer(l, upd):
            jl = l // 2
            chunks = QCH(upd)
            seqs = [(0, 2048, 0)] + ([(2048, 256, 2078)] if upd else [])
            P.pool(lambda e: e.memset(UP[:, 0:2364], 0.0), writes=[("K", t) for t in range(18)] + [("G", t) for t in range(18)] + ["UP"])
            for c in range(4):
                wv_ = next_piece()
                wg_ = next_piece()
                for (c0, n) in chunks:
                    proj(wv_, c0, n, 0)
                    proj(wg_, c0, n, 1)
                    sigm_recip(TF[3], PS[1][:, 0:n], n, [("ps", 1)])
                    base = 15 + c0 if c0 < NLAT else 2078 + 15 + (c0 - NLAT)
                    TT(UP[:, base:base + n], PS[0][:, 0:n], TF[3][:, 0:n], ALU.mult, [("ps", 0), ("t", id(TF[3]))], ["UP"])
                for k in range(31):
                    col = PO["dww", jl] + c * 31 + k
                    TS(DG[:, k, :], IDENT, PAR[:, col:col + 1], None, ALU.mult, None, ["CB", "PAR"], ["DG"])
                for (c0, n) in chunks:
                    bank = rr["pj"] % 2; rr["pj"] += 1
                    base = c0 if c0 < NLAT else 2078 + (c0 - NLAT)
                    MM([(PS[bank][:, 0:n], DG[:, k, :], UP[:, base + k:base + k + n], k == 0, k == 30) for k in range(31)],
                       ["DG", "UP"], [("ps", bank)])
                    ACT(QM[:, c, c0:c0 + n], PS[bank][:, 0:n], AF.Identity, [("ps", bank), "PAR"],
                        [("Q", c, t) for t in tiles(c0, n)], bias=PAR[:, PO["dwb", jl] + c:PO["dwb", jl] + c + 1], scale=1.0)
            wgs = [next_piece() for _ in range(4)]
            for (c0, n) in chunks:
                tl = tiles(c0, n)
                for c in range(4):
                    MM([(PS[6][:, 0:n], ONS[:, 1, :], QM[:, c, c0:c0 + n], c == 0, c == 3)], [("Q", c, t) for t in tl] + ["ONS"], [("ps", 6)])
                for c in range(4):
                    sq = TB[c % 2]
                    ACT(sq[:, 0:n], QM[:, c, c0:c0 + n], AF.Square, [("Q", c, t) for t in tl], [("t", id(sq))])
                    MM([(PS[7][:, 0:n], ONS[:, 1, :], sq[:, 0:n], c == 0, c == 3)], [("t", id(sq)), "ONS"], [("ps", 7)])
                P.act(lambda e, n=n: e.copy(TF[0][:, 0:n], PS[6][:, 0:n]), [("ps", 6)], [("t", id(TF[0]))])
                TT(TF[1][:, 0:n], TF[0][:, 0:n], TF[0][:, 0:n], ALU.mult, [("t", id(TF[0]))], [("t", id(TF[1]))])
                TT(TF[1][:, 0:n], PS[7][:, 0:n], TF[1][:, 0:n], ALU.subtract, [("ps", 7), ("t", id(TF[1]))], [("t", id(TF[1]))])
                ACT(TF[1][:, 0:n], TF[1][:, 0:n], AF.Ln, [("t", id(TF[1]))], [("t", id(TF[1]))], bias=EPS)
                ACT(TF[1][:, 0:n], TF[1][:, 0:n], AF.Exp, [("t", id(TF[1]))], [("t", id(TF[1]))], scale=-0.5)
                for c in range(4):
                    lg = PAR[:, PO["lng", jl] + c:PO["lng", jl] + c + 1]
                    lb = PAR[:, PO["lnb", jl] + c:PO["lnb", jl] + c + 1]
                    qmk = [("Q", c, t) for t in tl]
                    TT(TF[2][:, 0:n], QM[:, c, c0:c0 + n], TF[0][:, 0:n], ALU.subtract, qmk + [("t", id(TF[0]))], [("t", id(TF[2]))])
                    TT(TF[2][:, 0:n], TF[2][:, 0:n], TF[1][:, 0:n], ALU.mult, [("t", id(TF[2])), ("t", id(TF[1]))], [("t", id(TF[2]))])
                    ACT(TF[2][:, 0:n], TF[2][:, 0:n], AF.Identity, [("t", id(TF[2])), "PAR"], [("t", id(TF[2]))], bias=lb, scale=lg)
                    sigm_recip(TF[3], TF[2][:, 0:n], n, [("t", id(TF[2]))])
                    TT(TF[2][:, 0:n], TF[2][:, 0:n], TF[3][:, 0:n], ALU.mult, [("t", id(TF[2])), ("t", id(TF[3]))], [("t", id(TF[2]))])
                    proj(wgs[c], c0, n, 0)
                    sigm_recip(TF[3], PS[0][:, 0:n], n, [("ps", 0)])
                    TT(TF[3][:, 0:n], PS[0][:, 0:n], TF[3][:, 0:n], ALU.mult, [("ps", 0), ("t", id(TF[3]))], [("t", id(TF[3]))])
                    TT(QM[:, c, c0:c0 + n], TF[2][:, 0:n], TF[3][:, 0:n], ALU.mult, [("t", id(TF[2])), ("t", id(TF[3]))], qmk)
            for j in range(4):
                next_wo(j)
            out_proj(upd)
            if l + 1 < nl:
                modulation(l + 1)
            for j in range(4):
                plain_tile(kdst, lambda c0, n: kkeys(c0, n) + ["UP"], CHUNKS)
                v_tile()
                plain_tile(qdst(j), qkeys(j), chunks)
                gate_tile(chunks)
                for pc in range(3):
                    s_ = rr["stg"] % 2; rr["stg"] += 1
                    P.dma(lambda e, s_=s_, pc=pc, j=j: e.dma_start(out=STG[:, s_, :], in_=tgd[jl, j, pc]), writes=[("STG", s_)])
                    ACT(ED[:, pc * 1024:(pc + 1) * 1024], STG[:, s_, :], AF.Exp, [("STG", s_)], ["DG"])
                blocks = []
                for p_ in list(range(16)) + ([16, 17] if upd else []):
                    kls = []
                    for s in range(2):
                        kl = [(16, None), (17, None)]
                        if p_ < 16:
                            if p_ == 0:
                                lt, i0 = [0, 1, 2, 3], 3
                            elif p_ == 1:
                                lt, i0 = [0, 1, 2, 3], 2
                            elif p_ == 14:
                                lt, i0 = [12, 13, 14, 15], 1
                            elif p_ == 15:
                                lt, i0 = [12, 13, 14, 15], 0
                            else:
                                lt, i0 = list(range(p_ - 2, p_ + 3)), 7
                            etf = (lambda idx, r, s=s: ED[:, (s * 12 + idx) * 128:(s * 12 + idx + r) * 128])
                            for i, kt in enumerate(lt):
                                kl.append((kt, (etf, "DG", i0 + i)))
                        kls.append(kl)
                    blocks.append((p_ * 128, 128, kls))
                run_blocks(j, blocks, None)
            for j in range(4):
                next_wo(j)
            out_proj(upd)

        modulation(0)
        for l in range(nl):
            upd = l < nl - 1
            mod_finish(l)
            norm_to_h()
            if l % 2 == 0:
                even_layer(l, upd)
            else:
                odd_layer(l, upd)
        fg = PAR[:, PO["fg"]:PO["fg"] + 8]
        for (c0, n) in CHUNKS[:4]:
            for kc in range(8):
                sq = TB[kc % 2]
                ACT(sq[:, 0:n], XT[:, kc, c0:c0 + n], AF.Square, [("X", kc, c0)], [("t", id(sq))])
                MM([(PS[6][:, 0:n], ONS[:, 0, :], sq[:, 0:n], kc == 0, kc == 7)], [("t", id(sq)), "ONS"], [("ps", 6)])
            ACT(TF[0][:, 0:n], PS[6][:, 0:n], AF.Ln, [("ps", 6)], [("t", id(TF[0]))], bias=EPS)
            ACT(TF[0][:, 0:n], TF[0][:, 0:n], AF.Exp, [("t", id(TF[0]))], [("t", id(TF[0]))], scale=-0.5)
            for kc in range(8):
                STT(TF[1][:, 0:n], XT[:, kc, c0:c0 + n], fg[:, kc:kc + 1], TF[0][:, 0:n], ALU.mult, ALU.mult,
                    [("X", kc, c0), "PAR", ("t", id(TF[0]))], [("t", id(TF[1]))])
                CPY(HT[:, kc, c0:c0 + n], TF[1][:, 0:n], [("t", id(TF[1]))], [("H", c0)])
                TT(QM[:, kc % 4, (kc // 4) * 512:(kc // 4) * 512 + n], TF[1][:, 0:n], HT[:, kc, c0:c0 + n], ALU.subtract,
                   [("t", id(TF[1])), ("H", c0)], [("LO", kc)])
            for ti in range(4):
                t = c0 // 128 + ti
                s = rr["stg"] % 2; rr["stg"] += 1
                for fgp in range(2):
                    bank = rr["pj"] % 2; rr["pj"] += 1
                    lst = []
                    for i in range(4):
                        kc = fgp * 4 + i
                        lo = QM[:, kc % 4, (kc // 4) * 512 + ti * 128:(kc // 4) * 512 + ti * 128 + 128]
                        lst.append((PS[bank][:, i * 128:(i + 1) * 128], HT[:, kc, t * 128:(t + 1) * 128], IDENT, True, False))
                        lst.append((PS[bank][:, i * 128:(i + 1) * 128], lo, IDENT, False, True))
                    MM(lst, [("H", c0), "CB"] + [("LO", kc) for kc in range(8)], [("ps", bank)])
                    P.act(lambda e, bank=bank, s=s, fgp=fgp: e.copy(STG[:, s, fgp * 512:(fgp + 1) * 512], PS[bank][:, :]),
                          [("ps", bank)], [("STG", s)])
                P.dma(lambda e, s=s, t=t: e.dma_start(out=yd[t * 128:(t + 1) * 128, :], in_=STG[:, s, :]), reads=[("STG", s)])
        assert rr["piece"] == npieces, (rr["piece"], npieces)
        P.emit(st)
    return nc, P


_CACHE = {}


def run(inputs, nl=4, cores=8):
    prep = _host_prep(inputs, nl)
    if nl not in _CACHE:
        _CACHE[nl] = build(nl)
    nc, _ = _CACHE[nl]
    in_maps = []
    for b in range(cores):
        in_maps.append({"x": prep["xs"][b], "par": prep["pars"][b], "w": prep["W"], "cst": prep["cst"][:, 0:768],
                        "rope": prep["rope"].reshape(4, 128, 1024), "tg": prep["tg"]})
    res = run_bass_kernel_spmd(nc, in_maps, core_ids=list(range(cores)))
    return np.stack([np.asarray(r["y"], dtype=np.float32) for r in res.results], axis=0)


def kernel(**inputs):
    inputs = {k: np.asarray(v) for k, v in inputs.items()}
    return run(inputs, 4, 8)
```

```python
import numpy as np
from contextlib import ExitStack
import concourse.bass as bass
import concourse.mybir as mybir
from concourse.bass_utils import run_bass_kernel_spmd

F32 = mybir.dt.float32
BF16 = mybir.dt.bfloat16
AF = mybir.ActivationFunctionType
ALU = mybir.AluOpType

NT = 2304
NLAT = 2048
CHUNKS = [(0, 512), (512, 512), (1024, 512), (1536, 512), (2048, 256)]
EPS = 1e-6
NDMA_SEMS = 24


class Prog:
    ENGS = ("tensor", "vector", "scalar", "gpsimd", "sync")

    def __init__(self, nc):
        self.nc = nc
        self.ops = []

    def add(self, eng, fn, reads=(), writes=(), dma=False):
        self.ops.append((eng, fn, tuple(reads), tuple(writes), dma))

    def pe(self, fn, reads=(), writes=()): self.add("tensor", fn, reads, writes)
    def dve(self, fn, reads=(), writes=()): self.add("vector", fn, reads, writes)
    def act(self, fn, reads=(), writes=()): self.add("scalar", fn, reads, writes)
    def pool(self, fn, reads=(), writes=()): self.add("gpsimd", fn, reads, writes)
    def dma(self, fn, reads=(), writes=()): self.add("sync", fn, reads, writes, dma=True)

    def emit(self, stack):
        nc = self.nc
        sems = {e: stack.enter_context(nc.semaphore("s_" + e)) for e in self.ENGS if e != "sync"}
        dsems = [stack.enter_context(nc.semaphore("d%d" % i)) for i in range(NDMA_SEMS)]
        dcount = [0] * NDMA_SEMS
        seq = {e: 0 for e in self.ENGS}
        waited = {e: {} for e in self.ENGS}
        last_w = {}
        readers = {}
        per_eng = {e: [] for e in self.ENGS}
        ndma = 0
        for (eng, fn, reads, writes, dma) in self.ops:
            psr = tuple(k for k in reads if isinstance(k, tuple) and k[0] == "ps")
            if psr:
                reads = tuple(k for k in reads if k not in psr)
                writes = tuple(writes) + tuple(k for k in psr if k not in writes)
            deps = {}

            def need(tok):
                s, v = tok
                if deps.get(s, 0) < v:
                    deps[s] = v
            for k in reads:
                if k in last_w:
                    if not (last_w[k][0] == eng == "tensor"):
                        need(last_w[k][1])
            for k in writes:
                if k in last_w and not (last_w[k][0] == eng == "tensor"):
                    need(last_w[k][1])
                for re, tok in readers.get(k, {}).items():
                    if re == eng == "tensor":
                        continue
                    need(tok)
            if dma:
                si = ndma % NDMA_SEMS
                ndma += 1
                if dcount[si] > 0:
                    need((("d", si), 16 * dcount[si]))
                dcount[si] += 1
                tok = (("d", si), 16 * dcount[si])
                inc = 16
            else:
                seq[eng] += 1
                tok = (("e", eng), seq[eng])
                inc = 1
            waits = []
            for s, v in deps.items():
                if waited[eng].get(s, 0) < v:
                    waited[eng][s] = v
                    waits.append((s, v))
            per_eng[eng].append((waits, fn, tok[0], inc))
            for k in reads:
                readers.setdefault(k, {})[eng if not dma else ("dma", ndma)] = tok
            for k in writes:
                last_w[k] = (eng if not dma else "dma", tok)
                readers[k] = {}
        final = [(("d", i), 16 * dcount[i]) for i in range(NDMA_SEMS) if dcount[i] > 0]

        def semof(s):
            return dsems[s[1]] if s[0] == "d" else sems[s[1]]

        block = stack.enter_context(nc.Block())

        def make(engname):
            def body(e):
                for waits, fn, s, inc in per_eng[engname]:
                    for ws, wv in waits:
                        e.wait_ge(semof(ws), wv)
                    ins = fn(e)
                    ins.then_inc(semof(s), inc)
                if engname == "sync":
                    for ws, wv in final:
                        e.wait_ge(semof(ws), wv)
            return body

        for engname in self.ENGS:
            if per_eng[engname] or engname == "sync":
                getattr(block, engname)(make(engname))
        self.stats = {e: len(per_eng[e]) for e in self.ENGS}


def _pair_cols(t):
    return np.concatenate([np.arange(t * 64, t * 64 + 64), np.arange((4 + t) * 64, (4 + t) * 64 + 64)])


def _swap(idx):
    return idx ^ 1


def _piece_list(nl):
    pieces = []

    def mod(l):
        for m in range(24):
            pieces.append(("mod", l, np.arange(m * 128, m * 128 + 128)))
    mod(0)
    for l in range(nl):
        jl = l // 2
        if l % 2 == 0:
            for base in (0, 1280):
                kb, vb, gb = base + 512, base + 640, base + 768
                kc = np.arange(128)
                pieces.append(("ein", jl, kb + kc))
                pieces.append(("ein", jl, kb + _swap(kc)))
                pieces.append(("ein", jl, vb + kc))
                if base == 0:
                    for t in range(4):
                        pc = _pair_cols(t)
                        pieces.append(("ein", jl, base + pc))
                        pieces.append(("ein", jl, base + _swap(pc)))
                        pieces.append(("ein", jl, gb + pc))
                else:
                    for t in range(4):
                        pc = _pair_cols(t)
                        pieces.append(("ein", jl, base + pc))
                        pieces.append(("ein", jl, base + _swap(pc)))
                    for t in range(4):
                        pieces.append(("ein", jl, gb + _pair_cols(t)))
                for t in range(4):
                    pieces.append(("eout", jl, (0 if base == 0 else 512) + _pair_cols(t)))
                if base == 0 and l + 1 < nl:
                    mod(l + 1)
        else:
            for c in range(4):
                pieces.append(("oin", jl, 0 + c * 128 + np.arange(128)))
                pieces.append(("oin", jl, 512 + c * 128 + np.arange(128)))
            for c in range(4):
                pieces.append(("oin", jl, 1024 + c * 128 + np.arange(128)))
            for c in range(4):
                pieces.append(("oout", jl, c * 128 + np.arange(128)))
            if l + 1 < nl:
                mod(l + 1)
            for t in range(4):
                pieces.append(("oin", jl, 2048 + t * 128 + np.arange(128)))
                pieces.append(("oin", jl, 2560 + t * 128 + np.arange(128)))
                pieces.append(("oin", jl, 1536 + t * 128 + np.arange(128)))
                pieces.append(("oin", jl, 3072 + t * 128 + np.arange(128)))
            for t in range(4):
                pieces.append(("oout", jl, 512 + t * 128 + np.arange(128)))
    return pieces


def _host_prep(inp, nl):
    f32 = np.float32
    pieces = _piece_list(nl)
    W = np.empty((len(pieces), 128, 1024), f32)
    for i, (kind, l, idx) in enumerate(pieces):
        if kind == "mod":
            w = inp["mod_w"][l][:, idx]
            W[i] = w.reshape(8, 128, 128).transpose(1, 0, 2).reshape(128, 1024)
        elif kind == "ein":
            w = inp["ev_w_in"][l][:, idx]
            W[i] = w.reshape(8, 128, 128).transpose(1, 0, 2).reshape(128, 1024)
        elif kind == "oin":
            w = inp["od_w_in"][l][:, idx]
            W[i] = w.reshape(8, 128, 128).transpose(1, 0, 2).reshape(128, 1024)
        elif kind == "eout":
            W[i] = inp["ev_w_out"][l][idx, :]
        else:
            W[i] = inp["od_w_out"][l][idx, :]
    cst = np.zeros((128, 128 * 4 + 256), f32)
    cst[:, 0:128] = np.eye(128, dtype=f32)
    sh = np.zeros((128, 128), f32)
    sh[np.arange(64), 64 + np.arange(64)] = 1.0
    cst[:, 128:256] = sh
    bo = np.zeros((128, 128), f32)
    bo[:64, :64] = 1.0 / 64
    bo[64:, 64:] = 1.0 / 64
    cst[:, 256:384] = bo
    cst[:, 384:512] = 1.0
    jj = np.arange(128)[:, None]
    ii = np.arange(128)[None, :]
    cst[:, 512:640] = (jj >= ii).astype(f32)
    cst[:, 640:768] = (jj <= ii).astype(f32)
    t = np.arange(NLAT)
    row = (t // 64).astype(f32)
    col = (t % 64).astype(f32)
    half = 32
    freqs = (f32(10000.0) ** (-np.arange(0, half, 2, dtype=f32) / f32(half))).astype(f32)
    ang = np.concatenate([row[:, None] * freqs, col[:, None] * freqs], axis=-1).astype(f32)
    cos, sin = np.cos(ang).astype(f32), np.sin(ang).astype(f32)
    d = np.arange(128) % 64
    C = cos[:, d // 2].T
    S = sin[:, d // 2].T * np.where(d % 2 == 0, -1.0, 1.0)[:, None].astype(f32)
    rope = np.empty((4, 128, 2, 512), f32)
    for c in range(4):
        rope[c, :, 0, :] = C[:, c * 512:(c + 1) * 512]
        rope[c, :, 1, :] = S[:, c * 512:(c + 1) * 512]
    n_odd = max(1, nl // 2)
    tg = np.full((n_odd, 4, 3, 128, 1024), -1e30, f32)
    variants = [(dl, False) for dl in range(-3, 4)] + [(-2, True), (-1, False), (0, False), (1, False), (2, True)]
    kk = np.arange(128)
    rkl, ck = kk // 64, kk % 64
    rl, cq = kk // 64, kk % 64
    cs = np.clip(cq - 8, 0, 48)
    colok = (ck[:, None] >= cs[None, :]) & (ck[:, None] < cs[None, :] + 16)
    dc = np.clip(ck[:, None] - cq[None, :] + 15, 0, 30)
    for jl in range(nl // 2):
        rpb = inp["d_rpb"][jl]
        for pr in range(4):
            for s in range(2):
                h = 2 * pr + s
                for vi, (dl, msk) in enumerate(variants):
                    dr = 2 * dl + rkl[:, None] - rl[None, :]
                    ok = colok & (np.abs(dr) <= 7)
                    if msk:
                        ok = ok & (dr >= -4) & (dr <= 3)
                    g = rpb[h][np.clip(dr + 7, 0, 14), dc]
                    blk = np.where(ok, g, f32(-1e30)).astype(f32)
                    q = s * 12 + vi
                    tg[jl, pr, q // 8, :, (q % 8) * 128:(q % 8) * 128 + 128] = blk
    def fm(v):
        return np.asarray(v, f32).reshape(8, 128).T
    pars = []
    for b in range(8):
        cols = [fm(inp["c"][b]), fm(inp["c_ctx"])]
        for l in range(4):
            cols.append(fm(inp["norm_g"][l]))
            cols.append(np.asarray(inp["mod_b"][l], f32).reshape(24, 128).T)
        for jl in range(2):
            for nm in ("a_q_gain", "a_k_gain"):
                gq = np.asarray(inp[nm][jl], f32)
                cols.append(gq[d][:, None])
                cols.append(gq[d ^ 1][:, None])
            cols.append(np.broadcast_to(np.asarray(inp["b_sink"][jl], f32)[None, :], (128, 8)))
        for jl in range(2):
            dw = np.asarray(inp["c_dw_w"][jl], f32)
            cols.append(dw.reshape(31, 4, 128).transpose(2, 1, 0).reshape(128, 124))
            for nm in ("c_dw_b", "c_ln_g", "c_ln_b"):
                cols.append(np.asarray(inp[nm][jl], f32).reshape(4, 128).T)
        cols.append(fm(inp["final_g"]))
        pars.append(np.ascontiguousarray(np.concatenate(cols, axis=1)))
    xs = [np.ascontiguousarray(np.concatenate([inp["x"][b], inp["ctx"][b]], axis=0)) for b in range(8)]
    return dict(W=W, cst=cst, rope=rope, tg=tg, pars=pars, xs=xs, npar=pars[0].shape[1])


def _par_off():
    o = {}
    p = 0
    o["c"] = p; p += 8
    o["cctx"] = p; p += 8
    for l in range(4):
        o["ng", l] = p; p += 8
        o["mb", l] = p; p += 24
    for jl in range(2):
        o["qg", jl] = p; p += 1
        o["qgs", jl] = p; p += 1
        o["kg", jl] = p; p += 1
        o["kgs", jl] = p; p += 1
        o["sink", jl] = p; p += 8
    for jl in range(2):
        o["dww", jl] = p; p += 124
        o["dwb", jl] = p; p += 4
        o["lng", jl] = p; p += 4
        o["lnb", jl] = p; p += 4
    o["fg"] = p; p += 8
    o["n"] = p
    return o


def build(nl=4):
    nc = bass.Bass("TRN2", target_bir_lowering=False)
    PO = _par_off()
    npieces = len(_piece_list(nl))
    n_odd = max(1, nl // 2)
    xd = nc.dram_tensor("x", [NT, 1024], F32, kind="ExternalInput").ap()
    pard = nc.dram_tensor("par", [128, PO["n"]], F32, kind="ExternalInput").ap()
    wd = nc.dram_tensor("w", [npieces, 128, 1024], F32, kind="ExternalInput").ap()
    cstd = nc.dram_tensor("cst", [128, 768], F32, kind="ExternalInput").ap()
    roped = nc.dram_tensor("rope", [4, 128, 1024], F32, kind="ExternalInput").ap()
    tgd = nc.dram_tensor("tg", [n_odd, 4, 3, 128, 1024], F32, kind="ExternalInput").ap()
    yd = nc.dram_tensor("y", [NLAT, 1024], F32, kind="ExternalOutput").ap()

    st = ExitStack()
    with st:
        def SB(name, shape, dt):
            return st.enter_context(nc.sbuf_tensor(name, shape, dt))
        XT = SB("XT", [128, 8, NT], F32)
        HT = SB("HT", [128, 8, NT], BF16)
        QM = SB("QM", [128, 4, NT], BF16)
        KG = SB("KG", [128, 2, NT], BF16)
        VA = SB("VA", [128, 18, 2, 65], BF16)
        PT = SB("PT", [128, 6, 512], BF16)
        STG = SB("STG", [128, 2, 1024], F32)
        WB = SB("WB", [128, 4, 1024], BF16)
        WO = SB("WO", [128, 4, 1024], BF16)
        RCS = SB("RCS", [128, 2, 1024], F32)
        ED = SB("ED", [128, 3968], BF16)
        CB = SB("CB", [128, 768], BF16)
        ONS = SB("ONS", [128, 2, 128], BF16)
        PAR = SB("PAR", [128, PO["n"]], F32)
        MODW = SB("MODW", [128, 24, 2], F32)
        MCO = SB("MCO", [128, 3, 8, 2], F32)
        CS = SB("CS", [128, 8, 2], BF16)
        ESK = SB("ESK", [128, 8], F32)
        TF = [SB("TF%d" % i, [128, 512], F32) for i in range(4)]
        TB = [SB("TB%d" % i, [128, 512], BF16) for i in range(2)]
        RW = SB("RW", [128, 512], F32)
        RH = SB("RH", [128, 512], BF16)
        RL = SB("RL", [128, 512], BF16)
        OS = SB("OS", [64, 512], F32)
        ON = [SB("ON%d" % i, [64, 512], BF16) for i in range(2)]
        PS = [st.enter_context(nc.psum_tensor("PS%d" % i, [128, 512], F32)) for i in range(8)]

        KTt = KG[:, 0, :]
        GTt = KG[:, 1, :]
        UP = KG[:].rearrange("p a b -> p (a b)")
        VA36 = VA[:].rearrange("p t g d -> p (t g) d")
        IDENT = CB[:, 0:128]
        SHIFT = CB[:, 128:256]
        BONES = CB[:, 256:384]
        ONES = CB[:, 384:512]
        TRIL = CB[:, 512:640]
        TRIU = CB[:, 640:768]
        DG = ED[:, 0:3968].rearrange("p (k m) -> p k m", k=31)
        ET = ED[:, 0:3072].rearrange("p (s i q) -> p s i q", s=2, i=12)

        P = Prog(nc)
        rr = {"stg": 0, "wb": 0, "pt": 0, "sb": 0, "rcs": 0, "pj": 0, "piece": 0}

        def ACT(out, in_, func, reads, writes, **kw):
            P.act(lambda e: e.activation(out, in_, func, **kw), reads, writes)

        def TT(out, a, b, op, reads, writes, eng="vector"):
            P.add(eng, lambda e: e.tensor_tensor(out, a, b, op), reads, writes)

        def STT(out, in0, scalar, in1, op0, op1, reads, writes):
            P.dve(lambda e: e.scalar_tensor_tensor(out, in0, scalar, in1, op0, op1), reads, writes)

        def TS(out, in0, s1, s2, op0, op1, reads, writes):
            if op1 is None:
                P.dve(lambda e: e.tensor_scalar(out, in0, s1, None, op0), reads, writes)
            else:
                P.dve(lambda e: e.tensor_scalar(out, in0, s1, s2, op0, op1), reads, writes)

        def CPY(out, in_, reads, writes, eng="vector"):
            P.add(eng, lambda e: e.tensor_copy(out, in_), reads, writes)

        def RCP(out, in_, reads, writes, bias=0.0):
            ACT(out, in_, AF.Ln, reads, writes, bias=bias)
            ACT(out, out, AF.Exp, writes, writes, scale=-1.0)

        def MM(lst, reads, writes):
            def f(e):
                r = None
                for (o, l, rh, s0, s1) in lst:
                    r = e.matmul(o, l, rh, start=s0, stop=s1)
                return r
            P.pe(f, reads, writes)

        def tiles(c0, n):
            return list(range(c0 // 128, (c0 + n) // 128))

        def next_piece(eng="gpsimd"):
            i = rr["piece"]; rr["piece"] += 1
            s = rr["stg"] % 2; rr["stg"] += 1
            w = rr["wb"] % 4; rr["wb"] += 1
            P.dma(lambda e: e.dma_start(out=STG[:, s, :], in_=wd[i]), writes=[("STG", s)])
            CPY(WB[:, w, :], STG[:, s, :], [("STG", s)], [("WB", w)], eng=eng)
            return w

        def next_wo(j):
            i = rr["piece"]; rr["piece"] += 1
            s = rr["stg"] % 2; rr["stg"] += 1
            P.dma(lambda e: e.dma_start(out=STG[:, s, :], in_=wd[i]), writes=[("STG", s)])
            CPY(WO[:, j, :], STG[:, s, :], [("STG", s)], [("WO", j)], eng="gpsimd")

        def proj(w, c0, n, bank):
            wv = WB[:, w, :].rearrange("p (k m) -> p k m", k=8)
            MM([(PS[bank][:, 0:n], wv[:, kc, :], HT[:, kc, c0:c0 + n], kc == 0, kc == 7) for kc in range(8)],
               [("WB", w), ("H", c0)], [("ps", bank)])

        def sigm_recip(dst, src_ps, n, rkeys):
            ACT(dst[:, 0:n], src_ps, AF.Exp, rkeys, [("t", id(dst))], scale=-1.0)
            RCP(dst[:, 0:n], dst[:, 0:n], [("t", id(dst))], [("t", id(dst))], bias=1.0)

        P.dma(lambda e: e.dma_start(out=PAR[:], in_=pard), writes=["PAR"])
        P.dma(lambda e: e.dma_start(out=STG[:, 0, 0:768], in_=cstd), writes=[("STG", 0)])
        CPY(CB[:], STG[:, 0, 0:768], [("STG", 0)], ["CB"])
        rr["stg"] = 1
        P.pool(lambda e: e.memset(ONS[:, 0, :], 1.0 / 1024), writes=["ONS"])
        P.pool(lambda e: e.memset(ONS[:, 1, :], 1.0 / 512), writes=["ONS"])
        P.pool(lambda e: e.memset(VA[:], 1.0), writes=[("V", t) for t in range(18)])
        for q, key in enumerate(("c", "cctx")):
            src = PAR[:, PO[key]:PO[key] + 8]
            ACT(TF[0][:, 0:8], src, AF.Exp, ["PAR"], [("t", id(TF[0]))], scale=-1.0)
            RCP(TF[0][:, 0:8], TF[0][:, 0:8], [("t", id(TF[0]))], [("t", id(TF[0]))], bias=1.0)
            TT(CS[:, :, q], src, TF[0][:, 0:8], ALU.mult, [("t", id(TF[0])), "PAR"], ["CS"])
        for t in range(18):
            s = rr["stg"] % 2; rr["stg"] += 1
            P.dma(lambda e, t=t, s=s: e.dma_start(out=STG[:, s, :], in_=xd[t * 128:(t + 1) * 128, :]), writes=[("STG", s)])
            hi, lo = WB[:, 2 * (t % 2), :], WB[:, 2 * (t % 2) + 1, :]
            kh, kl = ("WB", 2 * (t % 2)), ("WB", 2 * (t % 2) + 1)
            CPY(hi, STG[:, s, :], [("STG", s)], [kh])
            TT(lo, STG[:, s, :], hi, ALU.subtract, [("STG", s), kh], [kl])
            for fg in range(2):
                bank = rr["pj"] % 2; rr["pj"] += 1
                lst = []
                for i in range(4):
                    f = fg * 4 + i
                    lst.append((PS[bank][:, i * 128:(i + 1) * 128], hi[:, f * 128:(f + 1) * 128], IDENT, True, False))
                    lst.append((PS[bank][:, i * 128:(i + 1) * 128], lo[:, f * 128:(f + 1) * 128], IDENT, False, True))
                MM(lst, [kh, kl, "CB"], [("ps", bank)])
                cc = min(t // 4, 4)
                P.act(lambda e, bank=bank, fg=fg, t=t: e.copy(XT[:, fg * 4:fg * 4 + 4, t * 128:(t + 1) * 128],
                                                                PS[bank][:, :].rearrange("p (a b) -> p a b", a=4)),
                      [("ps", bank)], [("X", kc, CHUNKS[cc][0]) for kc in range(fg * 4, fg * 4 + 4)])

        def modulation(l):
            for m in range(24):
                w = next_piece(eng="vector")
                wv = WB[:, w, :].rearrange("p (k m) -> p k m", k=8)
                MM([(PS[6][:, 0:2], wv[:, kc, :], CS[:, kc, :], kc == 0, kc == 7) for kc in range(8)],
                   [("WB", w), "CS"], [("ps", 6)])
                TS(MODW[:, m, :], PS[6][:, 0:2], PAR[:, PO["mb", l] + m:PO["mb", l] + m + 1], None, ALU.add, None,
                   [("ps", 6), "PAR"], ["MODW"])

        def mod_finish(l):
            ng = PAR[:, PO["ng", l]:PO["ng", l] + 8]
            for q in range(2):
                TS(MCO[:, 0, :, q], MODW[:, 8:16, q], 1.0, None, ALU.add, None, ["MODW"], ["MCO"])
                TT(MCO[:, 0, :, q], MCO[:, 0, :, q], ng, ALU.mult, ["MCO", "PAR"], ["MCO"])
                CPY(MCO[:, 1, :, q], MODW[:, 0:8, q], ["MODW"], ["MCO"])
                CPY(MCO[:, 2, :, q], MODW[:, 16:24, q], ["MODW"], ["MCO"])

        def norm_to_h():
            rbufs = [TF[0], TF[3]]

            def stage_a(ci):
                c0, n = CHUNKS[ci]
                rb = rbufs[ci % 2]
                bank = 6 + ci % 2
                for kc in range(8):
                    sq = TB[kc % 2]
                    ACT(sq[:, 0:n], XT[:, kc, c0:c0 + n], AF.Square, [("X", kc, c0)], [("t", id(sq))])
                    MM([(PS[bank][:, 0:n], ONS[:, 0, :], sq[:, 0:n], kc == 0, kc == 7)], [("t", id(sq)), "ONS"], [("ps", bank)])
                ACT(rb[:, 0:n], PS[bank][:, 0:n], AF.Ln, [("ps", bank)], [("t", id(rb))], bias=EPS)
                ACT(rb[:, 0:n], rb[:, 0:n], AF.Exp, [("t", id(rb))], [("t", id(rb))], scale=-0.5)

            def stage_b(ci):
                c0, n = CHUNKS[ci]
                rb = rbufs[ci % 2]
                q = 0 if c0 < NLAT else 1
                for kc in range(8):
                    tmp = TF[1 + kc % 2]
                    STT(tmp[:, 0:n], XT[:, kc, c0:c0 + n], MCO[:, 0, kc, q:q + 1], rb[:, 0:n], ALU.mult, ALU.mult,
                        [("X", kc, c0), "MCO", ("t", id(rb))], [("t", id(tmp))])
                    ACT(HT[:, kc, c0:c0 + n], tmp[:, 0:n], AF.Identity, [("t", id(tmp)), "MCO"], [("H", c0)],
                        bias=MCO[:, 1, kc, q:q + 1], scale=1.0)
            stage_a(0)
            for ci in range(len(CHUNKS)):
                if ci + 1 < len(CHUNKS):
                    stage_a(ci + 1)
                stage_b(ci)

        def qk_tile(dst_fn, dkeys_fn, use_norm, gain_ap, gain_sw_ap, chunks):
            w = next_piece()
            ws = next_piece()
            for (c0, n) in chunks:
                lat = c0 < NLAT
                b0 = 2 * (rr["pj"] % 2); rr["pj"] += 1
                b1 = b0 + 1
                proj(w, c0, n, b0)
                if lat:
                    proj(ws, c0, n, b1)
                if use_norm:
                    ACT(TB[0][:, 0:n], PS[b0][:, 0:n], AF.Square, [("ps", b0)], [("t", id(TB[0]))])
                    MM([(PS[6][:, 0:n], BONES, TB[0][:, 0:n], True, True)], [("t", id(TB[0])), "CB"], [("ps", 6)])
                    ACT(TF[0][:, 0:n], PS[6][:, 0:n], AF.Ln, [("ps", 6)], [("t", id(TF[0]))], bias=EPS)
                    ACT(TF[0][:, 0:n], TF[0][:, 0:n], AF.Exp, [("t", id(TF[0]))], [("t", id(TF[0]))], scale=-0.5)
                    STT(TF[1][:, 0:n], PS[b0][:, 0:n], gain_ap, TF[0][:, 0:n], ALU.mult, ALU.mult,
                        [("ps", b0), "PAR", ("t", id(TF[0]))], [("t", id(TF[1]))])
                    if lat:
                        STT(TF[2][:, 0:n], PS[b1][:, 0:n], gain_sw_ap, TF[0][:, 0:n], ALU.mult, ALU.mult,
                            [("ps", b1), "PAR", ("t", id(TF[0]))], [("t", id(TF[2]))])
                    a_src, b_src = TF[1][:, 0:n], TF[2][:, 0:n]
                    akeys, bkeys = [("t", id(TF[1]))], [("t", id(TF[2]))]
                else:
                    a_src, b_src = PS[b0][:, 0:n], PS[b1][:, 0:n]
                    akeys, bkeys = [("ps", b0)], [("ps", b1)]
                if lat:
                    rs = rr["rcs"] % 2; rr["rcs"] += 1
                    ci = c0 // 512
                    P.dma(lambda e, rs=rs, ci=ci: e.dma_start(out=RCS[:, rs, :], in_=roped[ci]), writes=[("RCS", rs)])
                    TT(TF[1][:, 0:n], a_src, RCS[:, rs, 0:n], ALU.mult, akeys + [("RCS", rs)], [("t", id(TF[1]))])
                    TT(TF[2][:, 0:n], b_src, RCS[:, rs, 512:512 + n], ALU.mult, bkeys + [("RCS", rs)], [("t", id(TF[2]))])
                    TT(dst_fn(c0, n), TF[1][:, 0:n], TF[2][:, 0:n], ALU.add,
                       [("t", id(TF[1])), ("t", id(TF[2]))], dkeys_fn(c0, n))
                else:
                    if use_norm:
                        CPY(dst_fn(c0, n), a_src, akeys, dkeys_fn(c0, n))
                    else:
                        P.act(lambda e, c0=c0, n=n, b0=b0: e.copy(dst_fn(c0, n), PS[b0][:, 0:n]), [("ps", b0)], dkeys_fn(c0, n))

        def plain_tile(dst_fn, dkeys_fn, chunks):
            w = next_piece()
            for (c0, n) in chunks:
                bank = rr["pj"] % 2; rr["pj"] += 1
                proj(w, c0, n, bank)
                P.act(lambda e, c0=c0, n=n, bank=bank: e.copy(dst_fn(c0, n), PS[bank][:, 0:n]), [("ps", bank)], dkeys_fn(c0, n))

        def v_tile():
            w = next_piece()
            wv = WB[:, w, :].rearrange("p (k m) -> p k m", k=8)
            for g4 in range(5):
                tl = list(range(g4 * 4, min(g4 * 4 + 4, 18)))
                bank = rr["pj"] % 2; rr["pj"] += 1
                lst = []
                for i, t in enumerate(tl):
                    for kc in range(8):
                        lst.append((PS[bank][:, i * 128:(i + 1) * 128], HT[:, kc, t * 128:(t + 1) * 128], wv[:, kc, :], kc == 0, kc == 7))
                MM(lst, [("WB", w)] + [("H", CHUNKS[min(t // 4, 4)][0]) for t in tl], [("ps", bank)])
                nt_ = len(tl)
                P.act(lambda e, bank=bank, tl=tl, nt_=nt_: e.copy(
                    VA36[:, 2 * tl[0]:2 * tl[0] + 2 * nt_, 0:64],
                    PS[bank][:, 0:nt_ * 128].rearrange("p (a b) -> p a b", b=64)),
                    [("ps", bank)], [("V", t) for t in tl])

        def gate_tile(chunks, mul_into=None):
            w = next_piece()
            for (c0, n) in chunks:
                bank = rr["pj"] % 2; rr["pj"] += 1
                proj(w, c0, n, bank)
                sigm_recip(TF[3], PS[bank][:, 0:n], n, [("ps", bank)])
                if mul_into is None:
                    TT(GTt[:, c0:c0 + n], PS[bank][:, 0:n], TF[3][:, 0:n], ALU.mult,
                       [("ps", bank), ("t", id(TF[3]))], [("G", t) for t in tiles(c0, n)])
                else:
                    j = mul_into
                    qk_ = [("Q", j, t) for t in tiles(c0, n)]
                    TT(TF[3][:, 0:n], PS[bank][:, 0:n], TF[3][:, 0:n], ALU.mult,
                       [("ps", bank), ("t", id(TF[3]))], [("t", id(TF[3]))])
                    TT(QM[:, j, c0:c0 + n], QM[:, j, c0:c0 + n], TF[3][:, 0:n], ALU.mult, qk_ + [("t", id(TF[3]))], qk_)

        def slot(nq, idx):
            if nq == 128:
                return idx * 128, [idx]
            return 0, list(range((nq + 127) // 128))

        def attend_pair(j, c0, nq, kts, g, quad=False):
            if quad:
                qk = [("Q", i, t) for i in range(4) for t in tiles(c0, 128)]
                qrhs = [QM[64 * s:64 * s + 64, 0:4, c0:c0 + 128] for s in range(2)]
                nq = 512
            else:
                qk = [("Q", j, t) for t in tiles(c0, nq)]
                qrhs = [QM[64 * s:64 * s + 64, j, c0:c0 + nq] for s in range(2)]
            G = 512 // nq
            nk = len(kts[0])
            groups = [(g0, min(g0 + G, nk)) for g0 in range(0, nk, G)]
            obank = [4 + 2 * g, 5 + 2 * g]
            ocol = [0, 0]
            okeys = [[("ps", obank[0])], [("ps", obank[1])]]
            sbank = {}

            def emitS(gi):
                a, b = groups[gi]
                for s in range(2):
                    sb = rr["sb"] % 4; rr["sb"] += 1
                    sbank[gi, s] = sb
                    grp = kts[s][a:b]
                    MM([(PS[sb][:, i * nq:(i + 1) * nq], KTt[64 * s:64 * s + 64, kt * 128:(kt + 1) * 128],
                         qrhs[s], True, True) for i, (kt, _m) in enumerate(grp)],
                       [("K", kt) for kt, _m in grp] + qk, [("ps", sb)])
            emitS(0)
            for gi, (a, b) in enumerate(groups):
                if gi + 1 < len(groups):
                    emitS(gi + 1)
                pts = []
                for s in range(2):
                    sb = sbank[gi, s]
                    pt = rr["pt"] % 6; rr["pt"] += 1
                    pts.append(pt)
                    wdt = (b - a) * nq
                    ACT(PT[:, pt, 0:wdt], PS[sb][:, 0:wdt], AF.Exp, [("ps", sb)], [("PT", pt)], scale=0.125)
                for s in range(2):
                    pt = pts[s]
                    grp = kts[s][a:b]
                    i = 0
                    while i < len(grp):
                        m = grp[i][1]
                        if m is None:
                            i += 1
                            continue
                        if len(m) == 3:
                            r = 1
                            while i + r < len(grp) and grp[i + r][1] is not None and grp[i + r][1][2] == m[2] + r:
                                r += 1
                            ptv = PT[:, pt, i * nq:(i + r) * nq]
                            TT(ptv, ptv, m[0](m[2], r), ALU.mult, [("PT", pt), m[1]], [("PT", pt)])
                            i += r
                            continue
                        ptv = PT[:, pt, i * nq:(i + 1) * nq]
                        if quad:
                            ptv = ptv.rearrange("p (a b) -> p a b", a=4)
                        TT(ptv, ptv, m[0], ALU.mult, [("PT", pt), m[1]], [("PT", pt)])
                        i += 1
                for s in range(2):
                    pt = pts[s]
                    grp = kts[s][a:b]
                    MM([(PS[obank[s]][0:65, ocol[s]:ocol[s] + nq], VA[:, kt, s, 0:65], PT[:, pt, i * nq:(i + 1) * nq],
                         a + i == 0, a + i == nk - 1) for i, (kt, _m) in enumerate(grp)],
                       [("PT", pt)] + [("V", kt) for kt, _m in grp], okeys[s])

        def attn_epilogue(j, c0, nq, sink_heads, g):
            obank = [4 + 2 * g, 5 + 2 * g]
            m0, mq = slot(nq, g)
            for s in range(2):
                o0 = 0
                okeys = [("ps", obank[s])]
                bb = rr["sb"] % 4; rr["sb"] += 1
                r0, rq = slot(nq, 2 * g + s)
                rwk = [("RW", q) for q in rq]
                rhk = [("RH", q) for q in rq]
                rlk = [("RL", q) for q in rq]
                b6k = [("ps", bb)]
                osk = [("OS", q) for q in rq]
                onk = [("ON", s, q) for q in mq]
                Osum = PS[obank[s]][64:65, o0:o0 + nq]
                bias = ESK[64:65, sink_heads[s]:sink_heads[s] + 1] if sink_heads is not None else 0.0
                RCP(RW[64:65, r0:r0 + nq], Osum, okeys + (["ESK"] if sink_heads is not None else []), rwk, bias=bias)
                CPY(RH[64:65, r0:r0 + nq], RW[64:65, r0:r0 + nq], rwk, rhk)
                TT(RL[64:65, r0:r0 + nq], RW[64:65, r0:r0 + nq], RH[64:65, r0:r0 + nq], ALU.subtract, rwk + rhk, rlk)
                MM([(PS[bb][0:64, 0:nq], ONES[64:65, 0:64], RH[64:65, r0:r0 + nq], True, False),
                    (PS[bb][0:64, 0:nq], ONES[64:65, 0:64], RL[64:65, r0:r0 + nq], False, True)], rhk + rlk + ["CB"], b6k)
                P.act(lambda e, s=s, o0=o0, r0=r0: e.copy(OS[0:64, r0:r0 + nq], PS[obank[s]][0:64, o0:o0 + nq]), okeys, osk)
                TT(ON[s][:, m0:m0 + nq], OS[0:64, r0:r0 + nq], PS[bb][0:64, 0:nq], ALU.mult, osk + b6k, onk)
            mb = rr["sb"] % 4; rr["sb"] += 1
            m7k = [("ps", mb)]
            MM([(PS[mb][:, 0:nq], IDENT[0:64, :], ON[0][:, m0:m0 + nq], True, False),
                (PS[mb][:, 0:nq], SHIFT[0:64, :], ON[1][:, m0:m0 + nq], False, True)],
               [("ON", 0, q) for q in mq] + [("ON", 1, q) for q in mq] + ["CB"], m7k)
            tl = tiles(c0, nq)
            TT(QM[:, j, c0:c0 + nq], PS[mb][:, 0:nq], GTt[:, c0:c0 + nq], ALU.mult,
               m7k + [("G", t) for t in tl], [("Q", j, t) for t in tl])

        def quad_epilogue(c0, g):
            qv = lambda ap: ap.rearrange("p (a b) -> p a b", a=4)
            tl = tiles(c0, 128)
            qkeys_ = [("Q", i, t) for i in range(4) for t in tl]
            for s in range(2):
                ob = 4 + 2 * g + s
                okeys = [("ps", ob)]
                bb = rr["sb"] % 4; rr["sb"] += 1
                for i in range(4):
                    h = 4 * s + i
                    ACT(RW[64:65, i * 128:(i + 1) * 128], PS[ob][64:65, i * 128:(i + 1) * 128], AF.Ln,
                        okeys + ["ESK"], ["RWq"], bias=ESK[64:65, h:h + 1])
                ACT(RW[64:65, 0:512], RW[64:65, 0:512], AF.Exp, ["RWq"], ["RWq"], scale=-1.0)
                CPY(RH[64:65, 0:512], RW[64:65, 0:512], ["RWq"], ["RHq"])
                TT(RL[64:65, 0:512], RW[64:65, 0:512], RH[64:65, 0:512], ALU.subtract, ["RWq", "RHq"], ["RLq"])
                MM([(PS[bb][0:64, 0:512], ONES[64:65, 0:64], RH[64:65, 0:512], True, False),
                    (PS[bb][0:64, 0:512], ONES[64:65, 0:64], RL[64:65, 0:512], False, True)], ["RHq", "RLq", "CB"], [("ps", bb)])
                P.act(lambda e, ob=ob: e.copy(OS[0:64, 0:512], PS[ob][0:64, 0:512]), okeys, ["OSq"])
                if s == 0:
                    TT(QM[0:64, 0:4, c0:c0 + 128], qv(OS[0:64, 0:512]), qv(PS[bb][0:64, 0:512]), ALU.mult,
                       ["OSq", ("ps", bb)], qkeys_)
                else:
                    TT(ON[1][:, 0:512], OS[0:64, 0:512], PS[bb][0:64, 0:512], ALU.mult, ["OSq", ("ps", bb)], ["ONq"])
                    mb = rr["sb"] % 4; rr["sb"] += 1
                    MM([(PS[mb][:, 0:512], SHIFT[0:64, :], ON[1][:, 0:512], True, True)], ["ONq", "CB"], [("ps", mb)])
                    P.act(lambda e, mb=mb: e.copy(QM[64:128, 0:4, c0:c0 + 128], qv(PS[mb][64:128, 0:512])), [("ps", mb)], qkeys_)

        def run_quad_blocks(blocks):
            pending = None
            for bi, (c0, kts) in enumerate(blocks):
                g = bi % 2
                attend_pair(0, c0, 128, kts, g, quad=True)
                if pending is not None:
                    pending()
                pending = (lambda c0=c0, g=g: quad_epilogue(c0, g))
            if pending is not None:
                pending()

        def run_blocks_p(j, blocks):
            nq = 128

            def att_S(c0, kts):
                nk = len(kts[0])
                groups = [(g0, min(g0 + 4, nk)) for g0 in range(0, nk, 4)]
                assert len(groups) <= 2
                qk = [("Q", j, t) for t in tiles(c0, nq)]
                for gi, (a, b) in enumerate(groups):
                    for s in range(2):
                        sb = gi * 2 + s
                        grp = kts[s][a:b]
                        MM([(PS[sb][:, i * nq:(i + 1) * nq], KTt[64 * s:64 * s + 64, kt * 128:(kt + 1) * 128],
                             QM[64 * s:64 * s + 64, j, c0:c0 + nq], True, True) for i, (kt, _m) in enumerate(grp)],
                           [("K", kt) for kt, _m in grp] + qk, [("ps", sb)])
                return (c0, kts, groups, {})

            def att_exp(st):
                c0, kts, groups, pts = st
                for gi, (a, b) in enumerate(groups):
                    for s in range(2):
                        sb = gi * 2 + s
                        pt = rr["pt"] % 6; rr["pt"] += 1
                        pts[gi, s] = pt
                        wdt = (b - a) * nq
                        ACT(PT[:, pt, 0:wdt], PS[sb][:, 0:wdt], AF.Exp, [("ps", sb)], [("PT", pt)], scale=0.125)
                for gi, (a, b) in enumerate(groups):
                    for s in range(2):
                        pt = pts[gi, s]
                        grp = kts[s][a:b]
                        i = 0
                        while i < len(grp):
                            m = grp[i][1]
                            if m is None:
                                i += 1
                                continue
                            r = 1
                            while i + r < len(grp) and grp[i + r][1] is not None and grp[i + r][1][2] == m[2] + r:
                                r += 1
                            ptv = PT[:, pt, i * nq:(i + r) * nq]
                            TT(ptv, ptv, m[0](m[2], r), ALU.mult, [("PT", pt), m[1]], [("PT", pt)])
                            i += r

            def att_PV(st):
                c0, kts, groups, pts = st
                nk = len(kts[0])
                for gi, (a, b) in enumerate(groups):
                    for s in range(2):
                        pt = pts[gi, s]
                        grp = kts[s][a:b]
                        MM([(PS[4 + s][0:65, 0:nq], VA[:, kt, s, 0:65], PT[:, pt, i * nq:(i + 1) * nq],
                             a + i == 0, a + i == nk - 1) for i, (kt, _m) in enumerate(grp)],
                           [("PT", pt)] + [("V", kt) for kt, _m in grp], [("ps", 4 + s)])

            def epi_rows():
                for s in range(2):
                    ACT(RW[64:65, s * 128:(s + 1) * 128], PS[4 + s][64:65, 0:nq], AF.Ln, [("ps", 4 + s)], ["RWp"], bias=0.0)
                ACT(RW[64:65, 0:256], RW[64:65, 0:256], AF.Exp, ["RWp"], ["RWp"], scale=-1.0)
                for s in range(2):
                    P.act(lambda e, s=s: e.copy(OS[0:64, s * 128:(s + 1) * 128], PS[4 + s][0:64, 0:nq]), [("ps", 4 + s)], [("OSp", s)])
                CPY(RH[64:65, 0:256], RW[64:65, 0:256], ["RWp"], ["RHp"])
                TT(RL[64:65, 0:256], RW[64:65, 0:256], RH[64:65, 0:256], ALU.subtract, ["RWp", "RHp"], ["RLp"])

            def epi_norm_stack(c0):
                for s in range(2):
                    MM([(PS[6 + s][0:64, 0:nq], ONES[64:65, 0:64], RH[64:65, s * 128:(s + 1) * 128], True, False),
                        (PS[6 + s][0:64, 0:nq], ONES[64:65, 0:64], RL[64:65, s * 128:(s + 1) * 128], False, True)],
                       ["RHp", "RLp", "CB"], [("ps", 6 + s)])
                for s in range(2):
                    TT(ON[s][:, 0:nq], OS[0:64, s * 128:(s + 1) * 128], PS[6 + s][0:64, 0:nq], ALU.mult,
                       [("OSp", s), ("ps", 6 + s)], [("ONp", s)])
                MM([(PS[6][:, 0:nq], IDENT[0:64, :], ON[0][:, 0:nq], True, False),
                    (PS[6][:, 0:nq], SHIFT[0:64, :], ON[1][:, 0:nq], False, True)], [("ONp", 0), ("ONp", 1), "CB"], [("ps", 6)])
                tl = tiles(c0, nq)
                TT(QM[:, j, c0:c0 + nq], PS[6][:, 0:nq], GTt[:, c0:c0 + nq], ALU.mult,
                   [("ps", 6)] + [("G", t) for t in tl], [("Q", j, t) for t in tl])

            prev = None
            for (c0, _nq, kts) in blocks:
                st = att_S(c0, kts)
                if prev is not None:
                    epi_rows()
                att_exp(st)
                if prev is not None:
                    epi_norm_stack(prev)
                att_PV(st)
                prev = c0
            epi_rows()
            epi_norm_stack(prev)

        def run_blocks(j, blocks, sink_heads):
            pending = None
            for bi, (c0, nq, kts) in enumerate(blocks):
                g = bi % 2 if nq == 128 else 0
                attend_pair(j, c0, nq, kts, g)
                if pending is not None:
                    pending()
                    pending = None
                if nq == 128:
                    pending = (lambda c0=c0, nq=nq, g=g: attn_epilogue(j, c0, nq, sink_heads, g))
                else:
                    attn_epilogue(j, c0, nq, sink_heads, g)
            if pending is not None:
                pending()

        def out_proj(upd_ctx):
            for (c0, n) in CHUNKS:
                if c0 >= NLAT and not upd_ctx:
                    continue
                q = 0 if c0 < NLAT else 1
                for m in range(8):
                    bank = rr["pj"] % 2; rr["pj"] += 1
                    MM([(PS[bank][:, 0:n], WO[:, j, m * 128:(m + 1) * 128], QM[:, j, c0:c0 + n], j == 0, j == 3) for j in range(4)],
                       [("WO", j) for j in range(4)] + [("Q", j, t) for j in range(4) for t in tiles(c0, n)], [("ps", bank)])
                    STT(XT[:, m, c0:c0 + n], PS[bank][:, 0:n], MCO[:, 2, m, q:q + 1], XT[:, m, c0:c0 + n], ALU.mult, ALU.add,
                        [("ps", bank), "MCO", ("X", m, c0)], [("X", m, c0)])

        QCH = lambda upd: [ch for ch in CHUNKS if ch[0] < NLAT or upd]
        qdst = lambda j: (lambda c0, n: QM[:, j, c0:c0 + n])
        qkeys = lambda j: (lambda c0, n: [("Q", j, t) for t in tiles(c0, n)])
        kdst = lambda c0, n: KTt[:, c0:c0 + n]
        kkeys = lambda c0, n: [("K", t) for t in tiles(c0, n)]

        def even_layer(l, upd):
            jl = l // 2
            ACT(ESK[:], PAR[:, PO["sink", jl]:PO["sink", jl] + 8], AF.Exp, ["PAR"], ["ESK"])
            for mixer in range(2):
                isA = mixer == 0
                kg = PAR[:, PO["kg", jl]:PO["kg", jl] + 1]
                kgs = PAR[:, PO["kgs", jl]:PO["kgs", jl] + 1]
                qg = PAR[:, PO["qg", jl]:PO["qg", jl] + 1]
                qgs = PAR[:, PO["qgs", jl]:PO["qgs", jl] + 1]
                qk_tile(kdst, kkeys, isA, kg, kgs, CHUNKS)
                v_tile()
                if isA:
                    for j in range(4):
                        qk_tile(qdst(j), qkeys(j), True, qg, qgs, QCH(upd))
                        gate_tile(QCH(upd))
                        blocks = []
                        for (c0, n) in QCH(upd):
                            kl = [(kt, None) for kt in ([16, 17] + list(range(16)) if c0 < NLAT else [16, 17])]
                            blocks.append((c0, n, [kl, kl]))
                        run_blocks(j, blocks, None)
                else:
                    for j in range(4):
                        qk_tile(qdst(j), qkeys(j), False, qg, qgs, QCH(upd))
                    tril4 = (CB[:, 512:640].unsqueeze(1).broadcast_to((128, 4, 128)), "CB")
                    triu4 = (CB[:, 640:768].unsqueeze(1).broadcast_to((128, 4, 128)), "CB")
                    blocks = []
                    for qb in list(range(16)) + ([16, 17] if upd else []):
                        kl = [(16, None), (17, None)]
                        if qb < 16:
                            if qb > 0:
                                kl.append((qb - 1, tril4))
                            kl.append((qb, None))
                            if qb < 15:
                                kl.append((qb + 1, triu4))
                        blocks.append((qb * 128, [kl, kl]))
                    run_quad_blocks(blocks)
                    for j in range(4):
                        gate_tile(QCH(upd), mul_into=j)
                for j in range(4):
                    next_wo(j)
                out_proj(upd)
                if mixer == 0 and l + 1 < nl:
                    modulation(l + 1)

        def odd_layer(l, upd):
            jl = l // 2
            chunks = QCH(upd)
            seqs = [(0, 2048, 0)] + ([(2048, 256, 2078)] if upd else [])
            P.pool(lambda e: e.memset(UP[:, 0:2364], 0.0), writes=[("K", t) for t in range(18)] + [("G", t) for t in range(18)] + ["UP"])
            for c in range(4):
                wv_ = next_piece()
                wg_ = next_piece()
                for (c0, n) in chunks:
                    proj(wv_, c0, n, 0)
                    proj(wg_, c0, n, 1)
                    sigm_recip(TF[3], PS[1][:, 0:n], n, [("ps", 1)])
                    base = 15 + c0 if c0 < NLAT else 2078 + 15 + (c0 - NLAT)
                    TT(UP[:, base:base + n], PS[0][:, 0:n], TF[3][:, 0:n], ALU.mult, [("ps", 0), ("t", id(TF[3]))], ["UP"])
                for k in range(31):
                    col = PO["dww", jl] + c * 31 + k
                    TS(DG[:, k, :], IDENT, PAR[:, col:col + 1], None, ALU.mult, None, ["CB", "PAR"], ["DG"])
                for (c0, n) in chunks:
                    bank = rr["pj"] % 2; rr["pj"] += 1
                    base = c0 if c0 < NLAT else 2078 + (c0 - NLAT)
                    MM([(PS[bank][:, 0:n], DG[:, k, :], UP[:, base + k:base + k + n], k == 0, k == 30) for k in range(31)],
                       ["DG", "UP"], [("ps", bank)])
                    ACT(QM[:, c, c0:c0 + n], PS[bank][:, 0:n], AF.Identity, [("ps", bank), "PAR"],
                        [("Q", c, t) for t in tiles(c0, n)], bias=PAR[:, PO["dwb", jl] + c:PO["dwb", jl] + c + 1], scale=1.0)
            wgs = [next_piece() for _ in range(4)]
            for (c0, n) in chunks:
                tl = tiles(c0, n)
                for c in range(4):
                    MM([(PS[6][:, 0:n], ONS[:, 1, :], QM[:, c, c0:c0 + n], c == 0, c == 3)], [("Q", c, t) for t in tl] + ["ONS"], [("ps", 6)])
                for c in range(4):
                    sq = TB[c % 2]
                    ACT(sq[:, 0:n], QM[:, c, c0:c0 + n], AF.Square, [("Q", c, t) for t in tl], [("t", id(sq))])
                    MM([(PS[7][:, 0:n], ONS[:, 1, :], sq[:, 0:n], c == 0, c == 3)], [("t", id(sq)), "ONS"], [("ps", 7)])
                P.act(lambda e, n=n: e.copy(TF[0][:, 0:n], PS[6][:, 0:n]), [("ps", 6)], [("t", id(TF[0]))])
                TT(TF[1][:, 0:n], TF[0][:, 0:n], TF[0][:, 0:n], ALU.mult, [("t", id(TF[0]))], [("t", id(TF[1]))])
                TT(TF[1][:, 0:n], PS[7][:, 0:n], TF[1][:, 0:n], ALU.subtract, [("ps", 7), ("t", id(TF[1]))], [("t", id(TF[1]))])
                ACT(TF[1][:, 0:n], TF[1][:, 0:n], AF.Ln, [("t", id(TF[1]))], [("t", id(TF[1]))], bias=EPS)
                ACT(TF[1][:, 0:n], TF[1][:, 0:n], AF.Exp, [("t", id(TF[1]))], [("t", id(TF[1]))], scale=-0.5)
                for c in range(4):
                    lg = PAR[:, PO["lng", jl] + c:PO["lng", jl] + c + 1]
                    lb = PAR[:, PO["lnb", jl] + c:PO["lnb", jl] + c + 1]
                    qmk = [("Q", c, t) for t in tl]
                    TT(TF[2][:, 0:n], QM[:, c, c0:c0 + n], TF[0][:, 0:n], ALU.subtract, qmk + [("t", id(TF[0]))], [("t", id(TF[2]))])
                    TT(TF[2][:, 0:n], TF[2][:, 0:n], TF[1][:, 0:n], ALU.mult, [("t", id(TF[2])), ("t", id(TF[1]))], [("t", id(TF[2]))])
                    ACT(TF[2][:, 0:n], TF[2][:, 0:n], AF.Identity, [("t", id(TF[2])), "PAR"], [("t", id(TF[2]))], bias=lb, scale=lg)
                    sigm_recip(TF[3], TF[2][:, 0:n], n, [("t", id(TF[2]))])
                    TT(TF[2][:, 0:n], TF[2][:, 0:n], TF[3][:, 0:n], ALU.mult, [("t", id(TF[2])), ("t", id(TF[3]))], [("t", id(TF[2]))])
                    proj(wgs[c], c0, n, 0)
                    sigm_recip(TF[3], PS[0][:, 0:n], n, [("ps", 0)])
                    TT(TF[3][:, 0:n], PS[0][:, 0:n], TF[3][:, 0:n], ALU.mult, [("ps", 0), ("t", id(TF[3]))], [("t", id(TF[3]))])
                    TT(QM[:, c, c0:c0 + n], TF[2][:, 0:n], TF[3][:, 0:n], ALU.mult, [("t", id(TF[2])), ("t", id(TF[3]))], qmk)
            for j in range(4):
                next_wo(j)
            out_proj(upd)
            if l + 1 < nl:
                modulation(l + 1)
            for j in range(4):
                plain_tile(kdst, lambda c0, n: kkeys(c0, n) + ["UP"], CHUNKS)
                v_tile()
                plain_tile(qdst(j), qkeys(j), chunks)
                gate_tile(chunks)
                for pc in range(3):
                    s_ = rr["stg"] % 2; rr["stg"] += 1
                    P.dma(lambda e, s_=s_, pc=pc, j=j: e.dma_start(out=STG[:, s_, :], in_=tgd[jl, j, pc]), writes=[("STG", s_)])
                    ACT(ED[:, pc * 1024:(pc + 1) * 1024], STG[:, s_, :], AF.Exp, [("STG", s_)], ["DG"])
                blocks = []
                for p_ in list(range(16)) + ([16, 17] if upd else []):
                    kls = []
                    for s in range(2):
                        kl = [(16, None), (17, None)]
                        if p_ < 16:
                            if p_ == 0:
                                lt, i0 = [0, 1, 2, 3], 3
                            elif p_ == 1:
                                lt, i0 = [0, 1, 2, 3], 2
                            elif p_ == 14:
                                lt, i0 = [12, 13, 14, 15], 1
                            elif p_ == 15:
                                lt, i0 = [12, 13, 14, 15], 0
                            else:
                                lt, i0 = list(range(p_ - 2, p_ + 3)), 7
                            etf = (lambda idx, r, s=s: ED[:, (s * 12 + idx) * 128:(s * 12 + idx + r) * 128])
                            for i, kt in enumerate(lt):
                                kl.append((kt, (etf, "DG", i0 + i)))
                        kls.append(kl)
                    blocks.append((p_ * 128, 128, kls))
                run_blocks_p(j, blocks)
            for j in range(4):
                next_wo(j)
            out_proj(upd)

        modulation(0)
        for l in range(nl):
            upd = l < nl - 1
            mod_finish(l)
            norm_to_h()
            if l % 2 == 0:
                even_layer(l, upd)
            else:
                odd_layer(l, upd)
        fg = PAR[:, PO["fg"]:PO["fg"] + 8]
        for (c0, n) in CHUNKS[:4]:
            for kc in range(8):
                sq = TB[kc % 2]
                ACT(sq[:, 0:n], XT[:, kc, c0:c0 + n], AF.Square, [("X", kc, c0)], [("t", id(sq))])
                MM([(PS[6][:, 0:n], ONS[:, 0, :], sq[:, 0:n], kc == 0, kc == 7)], [("t", id(sq)), "ONS"], [("ps", 6)])
            ACT(TF[0][:, 0:n], PS[6][:, 0:n], AF.Ln, [("ps", 6)], [("t", id(TF[0]))], bias=EPS)
            ACT(TF[0][:, 0:n], TF[0][:, 0:n], AF.Exp, [("t", id(TF[0]))], [("t", id(TF[0]))], scale=-0.5)
            for kc in range(8):
                STT(TF[1][:, 0:n], XT[:, kc, c0:c0 + n], fg[:, kc:kc + 1], TF[0][:, 0:n], ALU.mult, ALU.mult,
                    [("X", kc, c0), "PAR", ("t", id(TF[0]))], [("t", id(TF[1]))])
                CPY(HT[:, kc, c0:c0 + n], TF[1][:, 0:n], [("t", id(TF[1]))], [("H", c0)])
                TT(QM[:, kc % 4, (kc // 4) * 512:(kc // 4) * 512 + n], TF[1][:, 0:n], HT[:, kc, c0:c0 + n], ALU.subtract,
                   [("t", id(TF[1])), ("H", c0)], [("LO", kc)])
            for ti in range(4):
                t = c0 // 128 + ti
                s = rr["stg"] % 2; rr["stg"] += 1
                for fgp in range(2):
                    bank = rr["pj"] % 2; rr["pj"] += 1
                    lst = []
                    for i in range(4):
                        kc = fgp * 4 + i
                        lo = QM[:, kc % 4, (kc // 4) * 512 + ti * 128:(kc // 4) * 512 + ti * 128 + 128]
                        lst.append((PS[bank][:, i * 128:(i + 1) * 128], HT[:, kc, t * 128:(t + 1) * 128], IDENT, True, False))
                        lst.append((PS[bank][:, i * 128:(i + 1) * 128], lo, IDENT, False, True))
                    MM(lst, [("H", c0), "CB"] + [("LO", kc) for kc in range(8)], [("ps", bank)])
                    P.act(lambda e, bank=bank, s=s, fgp=fgp: e.copy(STG[:, s, fgp * 512:(fgp + 1) * 512], PS[bank][:, :]),
                          [("ps", bank)], [("STG", s)])
                P.dma(lambda e, s=s, t=t: e.dma_start(out=yd[t * 128:(t + 1) * 128, :], in_=STG[:, s, :]), reads=[("STG", s)])
        assert rr["piece"] == npieces, (rr["piece"], npieces)
        P.emit(st)
    return nc, P


_CACHE = {}


def run(inputs, nl=4, cores=8):
    prep = _host_prep(inputs, nl)
    if nl not in _CACHE:
        _CACHE[nl] = build(nl)
    nc, _ = _CACHE[nl]
    in_maps = []
    for b in range(cores):
        in_maps.append({"x": prep["xs"][b], "par": prep["pars"][b], "w": prep["W"], "cst": prep["cst"][:, 0:768],
                        "rope": prep["rope"].reshape(4, 128, 1024), "tg": prep["tg"]})
    res = run_bass_kernel_spmd(nc, in_maps, core_ids=list(range(cores)))
    return np.stack([np.asarray(r["y"], dtype=np.float32) for r in res.results], axis=0)


def kernel(**inputs):
    inputs = {k: np.asarray(v) for k, v in inputs.items()}
    return run(inputs, 4, 8)
```
